# Optimizing a Trainium2 kernel written in Bass

```python
import math
import jax, jax.numpy as jnp
from jax import lax
import numpy as np

D_MODEL = 1024
BATCH = 4
SEQ = 8192
DEPTH = 2
DEC_BATCH = 16
DEC_SEQ = 4096
PAST_LEN = 128

N_FOURIER_GROUPS = 4
FOURIER_GROUP = 64
FOURIER_WIDTH = N_FOURIER_GROUPS * FOURIER_GROUP
SGU_HEADS = 4
SGU_HEAD_DIM = 64
SGU_WIDTH = SGU_HEADS * SGU_HEAD_DIM
CHUNK = 128
DIFF_HEADS = 4
DIFF_HEAD_DIM = 64
DIFF_V_DIM = 2 * DIFF_HEAD_DIM
DIFF_QK_WIDTH = DIFF_HEADS * 2 * DIFF_HEAD_DIM
DIFF_V_WIDTH = DIFF_HEADS * DIFF_V_DIM
Q_BLOCK = 128
ROPE_THETA = 10000.0
N_BRANCHES = 3
SPLITS = (FOURIER_WIDTH,
          FOURIER_WIDTH + SGU_WIDTH,
          FOURIER_WIDTH + 2 * SGU_WIDTH,
          FOURIER_WIDTH + 2 * SGU_WIDTH + DIFF_QK_WIDTH,
          FOURIER_WIDTH + 2 * SGU_WIDTH + 2 * DIFF_QK_WIDTH,
          FOURIER_WIDTH + 2 * SGU_WIDTH + 2 * DIFF_QK_WIDTH + DIFF_V_WIDTH)
IN_WIDTH = SPLITS[-1] + N_BRANCHES * D_MODEL
D_FF = 2752
N_EXPERTS = 8
TOP_K = 2
N_DENSE = (DEPTH + 1) // 2
N_MOE = DEPTH // 2
ALPHA = (2 * DEPTH) ** 0.25
BETA = (8 * DEPTH) ** -0.25
LN_EPS = 1e-5
RMS_EPS = 1e-5

kernel_name = "fourier_sgu_diffattn_gated_deepnorm_encoder"


def layer_norm(x, g, b):
    xf = x.astype(jnp.float32)
    mu = jnp.mean(xf, -1, keepdims=True)
    var = jnp.mean(jnp.square(xf - mu), -1, keepdims=True)
    return ((xf - mu) * lax.rsqrt(var + LN_EPS) * g.astype(jnp.float32) + b.astype(jnp.float32)).astype(x.dtype)


def rope_tables(seq):
    inv = ROPE_THETA ** (-jnp.arange(0, DIFF_HEAD_DIM, 2, dtype=jnp.float32) / DIFF_HEAD_DIM)
    ang = jnp.arange(seq, dtype=jnp.float32)[:, None] * inv[None, :]
    return jnp.cos(ang), jnp.sin(ang)


def apply_rope(t, cos, sin):
    t1, t2 = jnp.split(t.astype(jnp.float32), 2, axis=-1)
    c = cos[None, :, None, None, :]
    s = sin[None, :, None, None, :]
    return jnp.concatenate([t1 * c - t2 * s, t2 * c + t1 * s], axis=-1).astype(t.dtype)


def fourier_mix(f):
    B, S, _ = f.shape
    fr = f.astype(jnp.float32).reshape(B, S, N_FOURIER_GROUPS, FOURIER_GROUP)
    out = jnp.fft.fft2(fr, axes=(1, 3), norm="ortho").real
    return out.reshape(B, S, FOURIER_WIDTH).astype(f.dtype)


def spatial_gating(u, v, vn_g, vn_b, sgu_w, sgu_b):
    B, S, _ = v.shape
    vn = layer_norm(v, vn_g, vn_b)
    vr = vn.reshape(B, S // CHUNK, CHUNK, SGU_HEADS, SGU_HEAD_DIM)
    mixed = jnp.einsum('hij,bcjhd->bcihd', sgu_w, vr) + jnp.transpose(sgu_b)[None, None, :, :, None]
    return u * mixed.reshape(B, S, SGU_WIDTH)


def diff_attention(q, k, v, lam, subln_g, lambda_init):
    B, S = q.shape[:2]
    nb = S // Q_BLOCK
    qb = jnp.moveaxis(q.reshape(B, nb, Q_BLOCK, DIFF_HEADS, 2, DIFF_HEAD_DIM), 1, 0)
    scale = DIFF_HEAD_DIM ** -0.5

    def block(qblk):
        s = jnp.einsum('bqhmd,bkhmd->bhmqk', qblk, k).astype(jnp.float32) * scale
        p = jax.nn.softmax(s, axis=-1)
        a = p[:, :, 0] - lam * p[:, :, 1]
        return jnp.einsum('bhqk,bkhe->bqhe', a.astype(v.dtype), v)

    o = lax.map(block, qb)
    o = jnp.moveaxis(o, 0, 1).reshape(B, S, DIFF_HEADS, DIFF_V_DIM).astype(jnp.float32)
    o = o * lax.rsqrt(jnp.mean(o * o, -1, keepdims=True) + RMS_EPS) * subln_g.astype(jnp.float32)
    o = o * (1.0 - lambda_init)
    return o.reshape(B, S, DIFF_V_WIDTH).astype(v.dtype)


def swiglu(x, w_gate, w_up, w_down):
    return (jax.nn.silu(x @ w_gate) * (x @ w_up)) @ w_down


def moe_swiglu(x, w_router, w_gate, w_up, w_down):
    B, S, D = x.shape
    xt = x.reshape(-1, D)
    logits = (xt @ w_router).astype(jnp.float32)
    top_v, top_i = lax.top_k(logits, TOP_K)
    top_w = jax.nn.softmax(top_v, axis=-1)
    combine = jnp.sum(jax.nn.one_hot(top_i, N_EXPERTS, dtype=jnp.float32) * top_w[..., None], axis=1)
    y = jnp.zeros_like(xt)
    for e in range(N_EXPERTS):
        y = y + combine[:, e:e + 1].astype(xt.dtype) * swiglu(xt, w_gate[e], w_up[e], w_down[e])
    return y.reshape(B, S, D)


def trunk(x, w_in, w_fourier, w_sgu, w_diff, w_out, vn_g, vn_b, sgu_w, sgu_b,
          lam_q1, lam_k1, lam_q2, lam_k2, subln_g, ln1_g, ln1_b, ln2_g, ln2_b,
          ffn_w_gate, ffn_w_up, ffn_w_down, w_router, moe_w_gate, moe_w_up, moe_w_down):
    B, S, D = x.shape
    cos, sin = rope_tables(S)
    for l in range(DEPTH):
        h = x @ w_in[l]
        f_in, u, v, q, k, va, g = jnp.split(h, SPLITS, axis=-1)
        fo = fourier_mix(f_in)
        so = spatial_gating(u, v, vn_g[l], vn_b[l], sgu_w[l], sgu_b[l])
        q = apply_rope(q.reshape(B, S, DIFF_HEADS, 2, DIFF_HEAD_DIM), cos, sin)
        k = apply_rope(k.reshape(B, S, DIFF_HEADS, 2, DIFF_HEAD_DIM), cos, sin)
        va = va.reshape(B, S, DIFF_HEADS, DIFF_V_DIM)
        lambda_init = 0.8 - 0.6 * math.exp(-0.3 * l)
        lam = (jnp.exp(jnp.sum(lam_q1[l].astype(jnp.float32) * lam_k1[l].astype(jnp.float32)))
               - jnp.exp(jnp.sum(lam_q2[l].astype(jnp.float32) * lam_k2[l].astype(jnp.float32)))
               + lambda_init)
        do = diff_attention(q, k, va, lam, subln_g[l], lambda_init)
        gates = jax.nn.sigmoid(g.reshape(B, S, N_BRANCHES, D))
        merged = (gates[:, :, 0] * (fo @ w_fourier[l])
                  + gates[:, :, 1] * (so @ w_sgu[l])
                  + gates[:, :, 2] * (do @ w_diff[l]))
        x = layer_norm(ALPHA * x + merged @ w_out[l], ln1_g[l], ln1_b[l])
        if l % 2 == 0:
            j = l // 2
            ff = swiglu(x, ffn_w_gate[j], ffn_w_up[j], ffn_w_down[j])
        else:
            j = l // 2
            ff = moe_swiglu(x, w_router[j], moe_w_gate[j], moe_w_up[j], moe_w_down[j])
        x = layer_norm(ALPHA * x + ff, ln2_g[l], ln2_b[l])
    return x


def setup_inputs(seed: int = 0) -> dict:
    key = jax.random.key(seed)
    ks = jax.random.split(key, 32)
    f32 = jnp.float32
    nrm = lambda k, shape, scale: jax.random.normal(k, shape, f32) * scale
    return {
        "x_prompt": jax.random.normal(ks[0], (BATCH, SEQ, D_MODEL), f32),
        "x_sample": jax.random.normal(ks[1], (DEC_BATCH, DEC_SEQ, D_MODEL), f32),
        "w_in": nrm(ks[2], (DEPTH, D_MODEL, IN_WIDTH), D_MODEL ** -0.5),
        "w_fourier": nrm(ks[3], (DEPTH, FOURIER_WIDTH, D_MODEL), BETA * FOURIER_WIDTH ** -0.5),
        "w_sgu": nrm(ks[4], (DEPTH, SGU_WIDTH, D_MODEL), BETA * SGU_WIDTH ** -0.5),
        "w_diff": nrm(ks[5], (DEPTH, DIFF_V_WIDTH, D_MODEL), BETA * DIFF_V_WIDTH ** -0.5),
        "w_out": nrm(ks[6], (DEPTH, D_MODEL, D_MODEL), BETA * D_MODEL ** -0.5),
        "vn_g": 1.0 + nrm(ks[7], (DEPTH, SGU_WIDTH), 0.02),
        "vn_b": nrm(ks[8], (DEPTH, SGU_WIDTH), 0.02),
        "sgu_w": nrm(ks[9], (DEPTH, SGU_HEADS, CHUNK, CHUNK), 0.5 * CHUNK ** -0.5),
        "sgu_b": 1.0 + nrm(ks[10], (DEPTH, SGU_HEADS, CHUNK), 0.01),
        "lam_q1": nrm(ks[11], (DEPTH, DIFF_HEAD_DIM), 0.1),
        "lam_k1": nrm(ks[12], (DEPTH, DIFF_HEAD_DIM), 0.1),
        "lam_q2": nrm(ks[13], (DEPTH, DIFF_HEAD_DIM), 0.1),
        "lam_k2": nrm(ks[14], (DEPTH, DIFF_HEAD_DIM), 0.1),
        "subln_g": 1.0 + nrm(ks[15], (DEPTH, DIFF_V_DIM), 0.02),
        "ln1_g": 1.0 + nrm(ks[16], (DEPTH, D_MODEL), 0.02),
        "ln1_b": nrm(ks[17], (DEPTH, D_MODEL), 0.02),
        "ln2_g": 1.0 + nrm(ks[18], (DEPTH, D_MODEL), 0.02),
        "ln2_b": nrm(ks[19], (DEPTH, D_MODEL), 0.02),
        "ffn_w_gate": nrm(ks[20], (N_DENSE, D_MODEL, D_FF), BETA * D_MODEL ** -0.5),
        "ffn_w_up": nrm(ks[21], (N_DENSE, D_MODEL, D_FF), BETA * D_MODEL ** -0.5),
        "ffn_w_down": nrm(ks[22], (N_DENSE, D_FF, D_MODEL), BETA * D_FF ** -0.5),
        "w_router": nrm(ks[23], (N_MOE, D_MODEL, N_EXPERTS), D_MODEL ** -0.5),
        "moe_w_gate": nrm(ks[24], (N_MOE, N_EXPERTS, D_MODEL, D_FF), BETA * D_MODEL ** -0.5),
        "moe_w_up": nrm(ks[25], (N_MOE, N_EXPERTS, D_MODEL, D_FF), BETA * D_MODEL ** -0.5),
        "moe_w_down": nrm(ks[26], (N_MOE, N_EXPERTS, D_FF, D_MODEL), BETA * D_FF ** -0.5),
    }


def reference(x_prompt, x_sample, w_in, w_fourier, w_sgu, w_diff, w_out, vn_g, vn_b, sgu_w, sgu_b,
              lam_q1, lam_k1, lam_q2, lam_k2, subln_g, ln1_g, ln1_b, ln2_g, ln2_b,
              ffn_w_gate, ffn_w_up, ffn_w_down, w_router, moe_w_gate, moe_w_up, moe_w_down):
    y_prompt = trunk(x_prompt, w_in, w_fourier, w_sgu, w_diff, w_out, vn_g, vn_b, sgu_w, sgu_b,
                     lam_q1, lam_k1, lam_q2, lam_k2, subln_g, ln1_g, ln1_b, ln2_g, ln2_b,
                     ffn_w_gate, ffn_w_up, ffn_w_down, w_router, moe_w_gate, moe_w_up, moe_w_down)
    y_sample = trunk(x_sample, w_in, w_fourier, w_sgu, w_diff, w_out, vn_g, vn_b, sgu_w, sgu_b,
                     lam_q1, lam_k1, lam_q2, lam_k2, subln_g, ln1_g, ln1_b, ln2_g, ln2_b,
                     ffn_w_gate, ffn_w_up, ffn_w_down, w_router, moe_w_gate, moe_w_up, moe_w_down)
    return (y_prompt, y_sample)
```

```python
import math
from contextlib import ExitStack

import numpy as np
import ml_dtypes

import concourse.bass as bass
import concourse.mybir as mybir
from concourse.bass_utils import run_bass_kernel_spmd

F32 = mybir.dt.float32
BF16 = mybir.dt.bfloat16
AF = mybir.ActivationFunctionType
ALU = mybir.AluOpType
AX = mybir.AxisListType

D = 1024
DFF = 2752
NE = 8
NFC = 22
DEPTH = 2
ALPHA = (2 * DEPTH) ** 0.25
LN_EPS = 1e-5
RMS_EPS = 1e-5
ROPE_THETA = 10000.0
C_F, C_U, C_V, C_Q, C_K, C_VA, C_G, C_QS, C_KS = 0, 256, 512, 768, 1280, 1792, 2304, 5376, 5888
NW = 6400
BLK = 512


class Buf:
    def __init__(self, ap, name=""):
        self.ap = ap
        self.name = name
        self.w = {}
        self.r = {}

    def __getitem__(self, idx):
        return self.ap[idx]


class Tracker:
    def __init__(self, nc, es):
        self.nc = nc
        self.es = es
        self.eng = {"pe": nc.tensor, "act": nc.scalar, "dve": nc.vector, "pool": nc.gpsimd, "sp": nc.sync}
        self.sems = {}
        self.cnt = {}
        self.waited = {k: {} for k in self.eng}
        self.nconst = 0
        for k in self.eng:
            self._mksem(k)

    def _mksem(self, key):
        self.sems[key] = self.es.enter_context(self.nc.semaphore("s_" + key))
        self.cnt[key] = 0

    def _wait(self, e, key, val):
        if key == "pe" and e == "pe":
            return
        if key not in self.eng:
            val = self.cnt[key]
        if self.waited[e].get(key, 0) >= val:
            return
        self.eng[e].wait_ge(self.sems[key], val)
        self.waited[e][key] = val

    def _deps(self, e, reads, writes):
        for b in reads:
            for k, v in b.w.items():
                self._wait(e, k, v)
        for b in writes:
            for k, v in b.w.items():
                self._wait(e, k, v)
            for k, v in b.r.items():
                self._wait(e, k, v)

    def _mark(self, key, val, reads, writes):
        for b in reads:
            if b.r.get(key, 0) < val:
                b.r[key] = val
        for b in writes:
            if b.w.get(key, 0) < val:
                b.w[key] = val

    def op(self, e, fn, reads=(), writes=()):
        self._deps(e, reads, writes)
        ins = fn(self.eng[e])
        ins.then_inc(self.sems[e], 1)
        self.cnt[e] += 1
        self._mark(e, self.cnt[e], reads, writes)

    def dma(self, q, semkey, out, in_, reads=(), writes=(), **kw):
        if semkey == "ld_const":
            semkey = f"ldc{self.nconst}"
            self.nconst += 1
        if semkey not in self.sems:
            self._mksem(semkey)
        self._deps(q, reads, writes)
        ins = self.eng[q].dma_start(out=out, in_=in_, **kw)
        ins.then_inc(self.sems[semkey], 16)
        self.cnt[semkey] += 16
        self._mark(semkey, self.cnt[semkey], reads, writes)

    def barrier(self):
        self.nconst = 0
        for e in self.eng:
            for k in self.sems:
                if self.cnt[k] > 0:
                    self._wait(e, k, self.cnt[k])


class _StopBuild(Exception):
    pass


def build_program(SS, SP, NLAYER=2, debug=None, stop_phase=None):
    T0 = 2 * SS + SP
    HP = SP // 2
    T1 = 2 * SS + HP
    units = [(0, SS), (SS, SS), (2 * SS, SP)]

    nc = bass.Bass("TRN2", target_bir_lowering=False)

    def din(name, shape, dt=F32):
        return nc.dram_tensor(name, list(shape), dt, kind="ExternalInput").ap()

    def dscr(name, shape, dt):
        kind = {}
        if debug and name in debug:
            kind = dict(kind="ExternalOutput")
        return nc.dram_tensor(name, list(shape), dt, **kind).ap()

    xin = din("xin", [T0, D])
    w_in = din("w_in_e", [2, D, NW])
    w_fourier = din("w_fourier", [2, 256, D])
    w_sgu = din("w_sgu", [2, 256, D])
    w_diff = din("w_diff", [2, 512, D])
    w_out = din("w_out", [2, D, D])
    vn_g = din("vn_g", [2, 256])
    vn_b = din("vn_b", [2, 256])
    sgu_wT = din("sgu_wT", [2, 4, 128, 128])
    sgu_bt = din("sgu_bt", [2, 128, 256])
    lam_q1 = din("lam_q1", [2, 64])
    lam_k1 = din("lam_k1", [2, 64])
    lam_q2 = din("lam_q2", [2, 64])
    lam_k2 = din("lam_k2", [2, 64])
    subln_g = din("subln_g", [2, 128])
    ln1_g = din("ln1_g", [2, D])
    ln1_b = din("ln1_b", [2, D])
    ln2_g = din("ln2_g", [2, D])
    ln2_b = din("ln2_b", [2, D])
    ffn_w_gate = din("ffn_w_gate", [1, D, DFF])
    ffn_w_up = din("ffn_w_up", [1, D, DFF])
    ffn_w_down = din("ffn_w_down", [1, DFF, D])
    w_router = din("w_router", [1, D, NE])
    moe_w_gate = din("moe_w_gate", [1, NE, D, DFF])
    moe_w_up = din("moe_w_up", [1, NE, D, DFF])
    moe_w_down = din("moe_w_down", [1, NE, DFF, D])
    ident_d = din("ident", [128, 128])
    ropec = din("ropec", [128, T0])
    ropes = din("ropes", [128, T0])
    dft_cs = din("dft_cs", [SS, SS], BF16)
    dft_ss = din("dft_ss", [SS, SS], BF16)
    dft_cp = din("dft_cp", [SP, SP], BF16)
    dft_sp = din("dft_sp", [SP, SP], BF16)
    bdc_d = din("bdc", [128, 128])
    bdsn_d = din("bdsn", [128, 128])
    out = nc.dram_tensor("out", [T1, D], F32, kind="ExternalOutput").ap()

    XTd = dscr("XTd", [D, T0], BF16)
    KTd = dscr("KTd", [512, T0], BF16)
    Vd = dscr("Vd", [4, 128, T0 // 128, 128], BF16)
    Fd = dscr("Fd", [T0, 256], BF16)
    DOd = dscr("DOd", [512, T0], BF16)
    SOd = dscr("SOd", [256, T0], BF16)
    FOd = dscr("FOd", [256, T0], BF16)
    MTd = dscr("MTd", [D, T0], BF16)
    X1d = dscr("X1d", [T0, D], F32)
    X1Td = dscr("X1Td", [D, T0], BF16)
    CMBd = dscr("CMBd", [T0, NE], F32)
    Yd = dscr("Yd", [T0, D], F32)
    Xmid = dscr("Xmid", [T0, D], F32)

    es = ExitStack()
    with es:
      try:
        tk = Tracker(nc, es)
        uid = [0]

        def sb(stack, shape, dt, name):
            uid[0] += 1
            t = stack.enter_context(nc.sbuf_tensor(f"{name}_{uid[0]}", list(shape), dt))
            return Buf(t, name)

        PS = [Buf(es.enter_context(nc.psum_tensor(f"psum{i}", [128, 512], F32)), f"ps{i}") for i in range(8)]

        ident = sb(es, [128, 128], F32, "ident")
        tk.dma("sp", "ld_const", ident[:], ident_d[:, :], writes=[ident])
        ones_bf = sb(es, [128, 128], BF16, "ones_bf")
        onesm_bf = sb(es, [128, 128], BF16, "onesm_bf")
        tk.op("dve", lambda e: e.memset(ones_bf[:], 1.0), writes=[ones_bf])
        tk.op("dve", lambda e: e.memset(onesm_bf[:], 1.0 / 128.0), writes=[onesm_bf])
        bdc = sb(es, [128, 128], BF16, "bdc")
        bdsn = sb(es, [128, 128], BF16, "bdsn")
        tk.dma("pool", "ld_constp", bdc[:], bdc_d[:, :], writes=[bdc])
        tk.dma("pool", "ld_constp", bdsn[:], bdsn_d[:, :], writes=[bdsn])

        def load_w(dst, dst_c0, src2d, K, c0, ncols, semkey):
            nk = (K + 127) // 128
            for kc in range(nk):
                rows = min(128, K - kc * 128)
                cc = 0
                while cc < ncols:
                    n = min(2048, ncols - cc)
                    tk.dma("pool", semkey, dst[0:rows, kc, dst_c0 + cc:dst_c0 + cc + n],
                           src2d[kc * 128:kc * 128 + rows, c0 + cc:c0 + cc + n], writes=[dst])
                    cc += n

        def layer_norm_tile(stack_bufs, y, gtab, btab, outb, l_eng="pool"):
            st, mv = stack_bufs
            tk.op("dve", lambda e: e.bn_stats(out=st[:, 0:6], in_=y[:, 0:512]), reads=[y], writes=[st])
            tk.op("dve", lambda e: e.bn_stats(out=st[:, 6:12], in_=y[:, 512:1024]), reads=[y], writes=[st])
            tk.op("dve", lambda e: e.bn_aggr(out=mv[:, 0:2], in_=st[:, 0:12]), reads=[st], writes=[mv])
            tk.op("dve", lambda e: e.tensor_scalar(out=mv[:, 2:3], in0=mv[:, 1:2], scalar1=LN_EPS, scalar2=None,
                                                   op0=ALU.add), reads=[mv], writes=[mv])
            tk.op("act", lambda e: e.activation(out=mv[:, 3:4], in_=mv[:, 2:3], func=AF.Sqrt), reads=[mv], writes=[mv])
            tk.op("dve", lambda e: e.reciprocal(out=mv[:, 2:3], in_=mv[:, 3:4]), reads=[mv], writes=[mv])
            tk.op("dve", lambda e: e.tensor_scalar(out=y[:, :], in0=y[:, :], scalar1=mv[:, 0:1], scalar2=mv[:, 2:3],
                                                   op0=ALU.subtract, op1=ALU.mult), reads=[y, mv], writes=[y])
            tk.op(l_eng, lambda e: e.tensor_tensor(out=y[:, :], in0=y[:, :], in1=gtab[:, :], op=ALU.mult),
                  reads=[y, gtab], writes=[y])
            tk.op(l_eng, lambda e: e.tensor_tensor(out=outb[:, :], in0=y[:, :], in1=btab[:, :], op=ALU.add),
                  reads=[y, btab], writes=[outb])

        def bcast_load(dst, vec_ap, semkey="ld_const"):
            tk.dma("sp", semkey, dst[:, :], vec_ap.partition_broadcast(128), writes=[dst])

        phc = [0]

        def phase_end():
            phc[0] += 1
            if stop_phase is not None and phc[0] >= stop_phase:
                raise _StopBuild()

        for l in range(NLAYER):
            Xsrc = xin if l == 0 else Xmid
            TQ = T0 if l == 0 else T1
            qunits = [(t0, Sk, (Sk if l == 0 else min(Sk, SS if t0 < 2 * SS else HP))) for (t0, Sk) in units]
            lambda_init = 0.8 - 0.6 * math.exp(-0.3 * l)
            W2 = w_in[l]

            with ExitStack() as st1:
                Wk = sb(st1, [128, 8, 1792], BF16, "Wk")
                load_w(Wk, 0, W2, D, C_K, 512, "ldw")
                load_w(Wk, 512, W2, D, C_KS, 512, "ldw")
                load_w(Wk, 1024, W2, D, C_VA, 512, "ldw")
                load_w(Wk, 1536, W2, D, C_F, 256, "ldw")
                xs = [sb(st1, [128, 4, D], F32, "xs") for _ in range(2)]
                cs = [sb(st1, [128, 2, BLK], F32, "cs") for _ in range(2)]
                XT = [sb(st1, [128, 8, BLK], BF16, "XT") for _ in range(2)]
                kst = [sb(st1, [128, 4, BLK], BF16, "kst") for _ in range(2)]
                vst = [sb(st1, [128, 4, 4, 128], BF16, "vst") for _ in range(2)]
                fst = [sb(st1, [128, 4, 256], BF16, "fst") for _ in range(2)]
                tmp = [sb(st1, [128, BLK], F32, "tmp") for _ in range(4)]
                nb = T0 // BLK

                def p1_load(bi):
                    s = bi % 2
                    t0 = bi * BLK
                    tk.dma("sp", f"p1x{s}", xs[s][:, :, :], Xsrc[t0:t0 + BLK, :].rearrange("(j p) d -> p j d", p=128),
                           writes=[xs[s]])
                    tk.dma("sp", f"p1x{s}", cs[s][:, 0, :], ropec[:, t0:t0 + BLK], writes=[cs[s]])
                    tk.dma("sp", f"p1x{s}", cs[s][:, 1, :], ropes[:, t0:t0 + BLK], writes=[cs[s]])

                p1_load(0)
                for bi in range(nb):
                    s = bi % 2
                    t0 = bi * BLK
                    if bi + 1 < nb:
                        p1_load(bi + 1)
                    for kc in range(8):
                        bank = PS[kc % 2]
                        for j in range(4):
                            tk.op("pe", lambda e, j=j, kc=kc, bank=bank: e.transpose(
                                bank[:, j * 128:(j + 1) * 128], xs[s][:, j, kc * 128:(kc + 1) * 128], ident[:, :]),
                                reads=[xs[s], ident], writes=[bank])
                        if kc % 2 == 0:
                            tk.op("act", lambda e, kc=kc, bank=bank: e.activation(out=XT[s][:, kc, :], in_=bank[:, :], func=AF.Copy),
                                  reads=[bank], writes=[XT[s]])
                        else:
                            tk.op("dve", lambda e, kc=kc, bank=bank: e.tensor_copy(out=XT[s][:, kc, :], in_=bank[:, :]),
                                  reads=[bank], writes=[XT[s]])
                    tk.dma("pool", f"p1s{s}", XTd.rearrange("(kc p) t -> p kc t", p=128)[:, :, t0:t0 + BLK], XT[s][:, :, :],
                           reads=[XT[s]])
                    for h in range(4):
                        A = PS[2 + 2 * (h % 2)]
                        B = PS[3 + 2 * (h % 2)]
                        for kc in range(8):
                            tk.op("pe", lambda e, kc=kc, A=A: e.matmul(A[:, :], Wk[:, kc, h * 128:(h + 1) * 128], XT[s][:, kc, :],
                                                                    start=(kc == 0), stop=(kc == 7)),
                                  reads=[Wk, XT[s]], writes=[A])
                        for kc in range(8):
                            tk.op("pe", lambda e, kc=kc, B=B: e.matmul(B[:, :], Wk[:, kc, 512 + h * 128:512 + (h + 1) * 128], XT[s][:, kc, :],
                                                                    start=(kc == 0), stop=(kc == 7)),
                                  reads=[Wk, XT[s]], writes=[B])
                        ta, tb = tmp[2 * (h % 2)], tmp[2 * (h % 2) + 1]
                        tk.op("dve", lambda e, A=A, ta=ta: e.tensor_tensor(out=ta[:, :], in0=A[:, :], in1=cs[s][:, 0, :], op=ALU.mult),
                              reads=[A, cs[s]], writes=[ta])
                        tk.op("dve", lambda e, B=B, tb=tb: e.tensor_tensor(out=tb[:, :], in0=B[:, :], in1=cs[s][:, 1, :], op=ALU.mult),
                              reads=[B, cs[s]], writes=[tb])
                        tk.op("pool", lambda e, ta=ta, tb=tb: e.tensor_tensor(out=kst[s][:, h, :], in0=ta[:, :], in1=tb[:, :], op=ALU.add),
                              reads=[ta, tb], writes=[kst[s]])
                    tk.dma("pool", f"p1s{s}", KTd.rearrange("(h p) t -> p h t", p=128)[:, :, t0:t0 + BLK], kst[s][:, :, :],
                           reads=[kst[s]])
                    for st_ in range(8):
                        j = st_ % 4
                        bank = PS[6 + st_ % 2]
                        if st_ < 4:
                            for kc in range(8):
                                tk.op("pe", lambda e, kc=kc, bank=bank, j=j: e.matmul(bank[:, :], XT[s][:, kc, j * 128:(j + 1) * 128],
                                                                                  Wk[:, kc, 1024:1536], start=(kc == 0), stop=(kc == 7)),
                                      reads=[Wk, XT[s]], writes=[bank])
                            tk.op("act", lambda e, bank=bank, j=j: e.activation(
                                out=vst[s][:, :, j, :], in_=bank[:, :].rearrange("p (h e) -> p h e", h=4), func=AF.Copy),
                                reads=[bank], writes=[vst[s]])
                        else:
                            for kc in range(8):
                                tk.op("pe", lambda e, kc=kc, bank=bank, j=j: e.matmul(bank[:, 0:256], XT[s][:, kc, j * 128:(j + 1) * 128],
                                                                                  Wk[:, kc, 1536:1792], start=(kc == 0), stop=(kc == 7)),
                                      reads=[Wk, XT[s]], writes=[bank])
                            tk.op("dve", lambda e, bank=bank, j=j: e.tensor_copy(out=fst[s][:, j, :], in_=bank[:, 0:256]),
                                  reads=[bank], writes=[fst[s]])
                    c0 = t0 // 128
                    tk.dma("pool", f"p1s{s}", Vd[:, :, c0:c0 + 4, :].rearrange("h p c e -> p h c e"), vst[s][:, :, :, :],
                           reads=[vst[s]])
                    tk.dma("pool", f"p1s{s}", Fd[t0:t0 + BLK, :].rearrange("(j p) f -> p j f", p=128), fst[s][:, :, :],
                           reads=[fst[s]])
            tk.barrier()

            phase_end()
            with ExitStack() as st2:
                Wq = sb(st2, [128, 8, 1024], BF16, "Wq")
                load_w(Wq, 0, W2, D, C_Q, 512, "ldw")
                load_w(Wq, 512, W2, D, C_QS, 512, "ldw")
                lv = [sb(st2, [128, 64], F32, "lv") for _ in range(4)]
                bcast_load(lv[0], lam_q1[l, :])
                bcast_load(lv[1], lam_k1[l, :])
                bcast_load(lv[2], lam_q2[l, :])
                bcast_load(lv[3], lam_k2[l, :])
                sm = sb(st2, [128, 8], F32, "sm")
                lt = sb(st2, [128, 64], F32, "lt")
                tk.op("dve", lambda e: e.tensor_tensor(out=lt[:, :], in0=lv[0][:, :], in1=lv[1][:, :], op=ALU.mult),
                      reads=[lv[0], lv[1]], writes=[lt])
                tk.op("dve", lambda e: e.reduce_sum(out=sm[:, 0:1], in_=lt[:, :], axis=AX.X), reads=[lt], writes=[sm])
                tk.op("dve", lambda e: e.tensor_tensor(out=lt[:, :], in0=lv[2][:, :], in1=lv[3][:, :], op=ALU.mult),
                      reads=[lv[2], lv[3], sm], writes=[lt])
                tk.op("dve", lambda e: e.reduce_sum(out=sm[:, 1:2], in_=lt[:, :], axis=AX.X), reads=[lt], writes=[sm])
                tk.op("act", lambda e: e.activation(out=sm[:, 2:4], in_=sm[:, 0:2], func=AF.Exp), reads=[sm], writes=[sm])
                tk.op("dve", lambda e: e.tensor_tensor(out=sm[:, 4:5], in0=sm[:, 3:4], in1=sm[:, 2:3], op=ALU.subtract),
                      reads=[sm], writes=[sm])
                tk.op("dve", lambda e: e.tensor_scalar(out=sm[:, 5:6], in0=sm[:, 4:5], scalar1=-lambda_init, scalar2=None,
                                                       op0=ALU.add), reads=[sm], writes=[sm])
                negl = sm
                gcol = sb(st2, [128, 2], F32, "gcol")
                tk.dma("sp", "ld_const", gcol[:, 0:1], subln_g[l, :].rearrange("(p o) -> p o", o=1), writes=[gcol])
                tk.op("dve", lambda e: e.tensor_scalar(out=gcol[:, 1:2], in0=gcol[:, 0:1], scalar1=(1.0 - lambda_init),
                                                       scalar2=None, op0=ALU.mult), reads=[gcol], writes=[gcol])

                SKM = max(Sk for _, Sk in units)
                XTb = [sb(st2, [128, 8, BLK], BF16, "XTb") for _ in range(2)]
                csb = [sb(st2, [128, 2, BLK], F32, "csb") for _ in range(2)]
                QT = [sb(st2, [128, 4, BLK], BF16, "QT") for _ in range(2)]
                Kh = [sb(st2, [128, SKM], BF16, "Kh") for _ in range(2)]
                Vh = [sb(st2, [128, SKM // 128, 128], BF16, "Vh") for _ in range(2)]
                NPT = 6
                pT = [sb(st2, [128, BLK], BF16, "pT") for _ in range(NPT)]
                tmpq = [sb(st2, [128, BLK], F32, "tmpq") for _ in range(2)]
                ep_r = [sb(st2, [128, BLK], F32, "ep_r") for _ in range(2)]
                ep_t = [sb(st2, [128, BLK], F32, "ep_t") for _ in range(2)]
                ep_a = [sb(st2, [128, BLK], F32, "ep_a") for _ in range(2)]
                ep_sq = [sb(st2, [128, BLK], BF16, "ep_sq") for _ in range(2)]
                ep_sd = [sb(st2, [128, BLK], F32, "ep_sd") for _ in range(2)]
                doT = [sb(st2, [128, 4, BLK], BF16, "doT") for _ in range(2)]

                jobs = []
                for (t0u, Sk, Sq) in qunits:
                    for qb in range(Sq // BLK):
                        jobs.append((t0u, Sk, t0u + qb * BLK))
                hjobs = [(ji, h) for ji in range(len(jobs)) for h in range(4)]

                def a_load_q(ji):
                    s = ji % 2
                    _, _, tq = jobs[ji]
                    tk.dma("sp", f"ax{s}", XTb[s][:, :, :], XTd.rearrange("(kc p) t -> p kc t", p=128)[:, :, tq:tq + BLK],
                           writes=[XTb[s]])
                    tk.dma("sp", f"ax{s}", csb[s][:, 0, :], ropec[:, tq:tq + BLK], writes=[csb[s]])
                    tk.dma("sp", f"ax{s}", csb[s][:, 1, :], ropes[:, tq:tq + BLK], writes=[csb[s]])

                def a_load_kv(hi):
                    ji, h = hjobs[hi]
                    t0u, Sk, _ = jobs[ji]
                    s = hi % 2
                    tk.dma("sp", f"akv{s}", Kh[s][:, 0:Sk], KTd[h * 128:(h + 1) * 128, t0u:t0u + Sk], writes=[Kh[s]])
                    c0 = t0u // 128
                    tk.dma("sp", f"akv{s}", Vh[s][:, 0:Sk // 128, :], Vd[h, :, c0:c0 + Sk // 128, :], writes=[Vh[s]])

                pending_b = []

                def epi_a(hi, O, Ssum):
                    p = hi % 2
                    for m in range(2):
                        tk.op("dve", lambda e, m=m: e.reciprocal(out=ep_r[m][:, :], in_=Ssum[m][:, :]),
                              reads=[Ssum[m]], writes=[ep_r[m]])
                        tk.op("dve", lambda e, m=m: e.tensor_tensor(out=ep_t[m][:, :], in0=O[m][:, :], in1=ep_r[m][:, :], op=ALU.mult),
                              reads=[O[m], ep_r[m]], writes=[ep_t[m]])
                    tk.op("dve", lambda e: e.scalar_tensor_tensor(out=ep_a[p][:, :], in0=ep_t[1][:, :], scalar=negl[:, 5:6],
                                                                  in1=ep_t[0][:, :], op0=ALU.mult, op1=ALU.add),
                          reads=[ep_t[0], ep_t[1], negl], writes=[ep_a[p]])
                    tk.op("act", lambda e: e.activation(out=ep_sq[p][:, :], in_=ep_a[p][:, :], func=AF.Square),
                          reads=[ep_a[p]], writes=[ep_sq[p]])

                def epi_b(hi, bank):
                    p = hi % 2
                    ji, h = hjobs[hi]
                    s = ji % 2
                    tk.op("pe", lambda e: e.matmul(bank[:, :], onesm_bf[:, :], ep_sq[p][:, :], start=True, stop=True),
                          reads=[onesm_bf, ep_sq[p]], writes=[bank])
                    tk.op("dve", lambda e: e.tensor_scalar(out=ep_sd[p][:, :], in0=bank[:, :], scalar1=RMS_EPS, scalar2=None,
                                                           op0=ALU.add), reads=[bank], writes=[ep_sd[p]])
                    tk.op("act", lambda e: e.activation(out=ep_sd[p][:, :], in_=ep_sd[p][:, :], func=AF.Sqrt),
                          reads=[ep_sd[p]], writes=[ep_sd[p]])
                    tk.op("dve", lambda e: e.reciprocal(out=ep_sd[p][:, :], in_=ep_sd[p][:, :]), reads=[ep_sd[p]], writes=[ep_sd[p]])
                    tk.op("dve", lambda e: e.tensor_tensor(out=ep_a[p][:, :], in0=ep_a[p][:, :], in1=ep_sd[p][:, :], op=ALU.mult),
                          reads=[ep_a[p], ep_sd[p]], writes=[ep_a[p]])
                    tk.op("dve", lambda e: e.tensor_scalar(out=doT[s][:, h, :], in0=ep_a[p][:, :], scalar1=gcol[:, 1:2], scalar2=None,
                                                           op0=ALU.mult), reads=[ep_a[p], gcol], writes=[doT[s]])
                    if h == 3:
                        _, _, tq = jobs[ji]
                        tk.dma("pool", f"ast{s}", DOd.rearrange("(h p) t -> p h t", p=128)[:, :, tq:tq + BLK], doT[s][:, :, :],
                               reads=[doT[s]])

                a_load_q(0)
                a_load_kv(0)
                for hi, (ji, h) in enumerate(hjobs):
                    t0u, Sk, tq = jobs[ji]
                    s = ji % 2
                    ks = hi % 2
                    if h == 0:
                        if ji + 1 < len(jobs):
                            a_load_q(ji + 1)
                        for hh in range(4):
                            A = PS[4 + 2 * (hh % 2)]
                            B = PS[5 + 2 * (hh % 2)]
                            for kc in range(8):
                                tk.op("pe", lambda e, kc=kc, A=A, hh=hh: e.matmul(A[:, :], Wq[:, kc, hh * 128:(hh + 1) * 128], XTb[s][:, kc, :],
                                                                                start=(kc == 0), stop=(kc == 7)),
                                      reads=[Wq, XTb[s]], writes=[A])
                            for kc in range(8):
                                tk.op("pe", lambda e, kc=kc, B=B, hh=hh: e.matmul(B[:, :], Wq[:, kc, 512 + hh * 128:512 + (hh + 1) * 128], XTb[s][:, kc, :],
                                                                                start=(kc == 0), stop=(kc == 7)),
                                      reads=[Wq, XTb[s]], writes=[B])
                            tk.op("dve", lambda e, A=A: e.tensor_tensor(out=tmpq[0][:, :], in0=A[:, :], in1=csb[s][:, 0, :], op=ALU.mult),
                                  reads=[A, csb[s]], writes=[tmpq[0]])
                            tk.op("dve", lambda e, B=B: e.tensor_tensor(out=tmpq[1][:, :], in0=B[:, :], in1=csb[s][:, 1, :], op=ALU.mult),
                                  reads=[B, csb[s]], writes=[tmpq[1]])
                            tk.op("pool", lambda e, hh=hh: e.tensor_tensor(out=QT[s][:, hh, :], in0=tmpq[0][:, :], in1=tmpq[1][:, :], op=ALU.add),
                                  reads=[tmpq[0], tmpq[1]], writes=[QT[s]])
                    if hi + 1 < len(hjobs):
                        a_load_kv(hi + 1)
                    O = [PS[0], PS[1]]
                    Ssum = [PS[2], PS[3]]
                    nkc = Sk // 128
                    steps = [(kc, m) for kc in range(nkc) for m in range(2)]
                    LA = 2

                    def qk(i):
                        kc, m = steps[i]
                        sc = PS[4 + i % 4]
                        tk.op("pe", lambda e: e.matmul(sc[:, :], Kh[ks][m * 64:(m + 1) * 64, kc * 128:(kc + 1) * 128],
                                                       QT[s][m * 64:(m + 1) * 64, h, :], start=True, stop=True),
                              reads=[Kh[ks], QT[s]], writes=[sc])
                        pt = pT[i % NPT]
                        tk.op("act", lambda e: e.activation(out=pt[:, :], in_=sc[:, :], func=AF.Exp, scale=0.125),
                              reads=[sc], writes=[pt])

                    for i in range(min(LA, len(steps))):
                        qk(i)
                    for i, (kc, m) in enumerate(steps):
                        if i + LA < len(steps):
                            qk(i + LA)
                        pt = pT[i % NPT]
                        tk.op("pe", lambda e: e.matmul(O[m][:, :], Vh[ks][:, kc, :], pt[:, :], start=(kc == 0), stop=(kc == nkc - 1)),
                              reads=[Vh[ks], pt], writes=[O[m]])
                        tk.op("pe", lambda e: e.matmul(Ssum[m][:, :], ones_bf[:, :], pt[:, :], start=(kc == 0), stop=(kc == nkc - 1)),
                              reads=[ones_bf, pt], writes=[Ssum[m]])
                        if i == 8 and pending_b:
                            pending_b.pop(0)()
                    while pending_b:
                        pending_b.pop(0)()
                    epi_a(hi, O, Ssum)
                    pending_b.append(lambda hi=hi: epi_b(hi, PS[4 + (hi % 2)]))
                while pending_b:
                    pending_b.pop(0)()
            tk.barrier()

            phase_end()
            with ExitStack() as st3:
                GRP = 4
                Fg = [sb(st3, [128, GRP, 256], BF16, "Fg") for _ in range(2)]
                Cg = [sb(st3, [128, GRP, BLK], BF16, "Cg") for _ in range(2)]
                Sg = [sb(st3, [128, GRP, BLK], BF16, "Sg") for _ in range(2)]
                cfs = [sb(st3, [128, 4, BLK], BF16, "cfs") for _ in range(2)]
                foT = [sb(st3, [128, 2, BLK], BF16, "foT") for _ in range(2)]
                gj = []
                bjobs = []
                for (t0u, Sk, Sq) in qunits:
                    for qb in range(Sq // BLK):
                        bjobs.append((t0u, Sk, qb))
                for bi_, (t0u, Sk, qb) in enumerate(bjobs):
                    for g in range(Sk // (128 * GRP)):
                        gj.append((bi_, t0u, Sk, qb, g))

                def f_load(gi):
                    bi_, t0u, Sk, qb, g = gj[gi]
                    s = gi % 2
                    r0 = g * 128 * GRP
                    Cm, Sm = (dft_cs, dft_ss) if Sk == SS and t0u < 2 * SS else (dft_cp, dft_sp)
                    tk.dma("sp", f"fl{s}", Fg[s][:, :, :], Fd[t0u + r0:t0u + r0 + 128 * GRP, :].rearrange("(c p) f -> p c f", p=128),
                           writes=[Fg[s]])
                    tk.dma("sp", f"fl{s}", Cg[s][:, :, :], Cm[r0:r0 + 128 * GRP, qb * BLK:(qb + 1) * BLK].rearrange("(c p) n -> p c n", p=128),
                           writes=[Cg[s]])
                    tk.dma("sp", f"fl{s}", Sg[s][:, :, :], Sm[r0:r0 + 128 * GRP, qb * BLK:(qb + 1) * BLK].rearrange("(c p) n -> p c n", p=128),
                           writes=[Sg[s]])

                f_load(0)
                for gi, (bi_, t0u, Sk, qb, g) in enumerate(gj):
                    s = gi % 2
                    if gi + 1 < len(gj):
                        f_load(gi + 1)
                    ng = Sk // (128 * GRP)
                    for c in range(GRP):
                        first = (g == 0 and c == 0)
                        last = (g == ng - 1 and c == GRP - 1)
                        for a in range(4):
                            mat = Cg[s] if a < 2 else Sg[s]
                            tk.op("pe", lambda e, a=a, c=c, mat=mat: e.matmul(PS[a][:, :], Fg[s][:, c, (a % 2) * 128:(a % 2 + 1) * 128], mat[:, c, :],
                                                                          start=first, stop=last),
                                  reads=[Fg[s], mat], writes=[PS[a]])
                    if g == ng - 1:
                        bs = bi_ % 2
                        tq = t0u + qb * BLK
                        for a in range(4):
                            if a % 2 == 0:
                                tk.op("act", lambda e, a=a: e.activation(out=cfs[bs][:, a, :], in_=PS[a][:, :], func=AF.Copy),
                                      reads=[PS[a]], writes=[cfs[bs]])
                            else:
                                tk.op("dve", lambda e, a=a: e.tensor_copy(out=cfs[bs][:, a, :], in_=PS[a][:, :]),
                                      reads=[PS[a]], writes=[cfs[bs]])
                        for cc in range(2):
                            bank = PS[4 + cc + 2 * (bi_ % 2)]
                            tk.op("pe", lambda e, cc=cc, bank=bank: e.matmul(bank[:, :], bdc[:, :], cfs[bs][:, cc, :], start=True, stop=False),
                                  reads=[bdc, cfs[bs]], writes=[bank])
                            tk.op("pe", lambda e, cc=cc, bank=bank: e.matmul(bank[:, :], bdsn[:, :], cfs[bs][:, 2 + cc, :], start=False, stop=True),
                                  reads=[bdsn, cfs[bs]], writes=[bank])
                            if cc == 0:
                                tk.op("act", lambda e, cc=cc, bank=bank: e.activation(out=foT[bs][:, cc, :], in_=bank[:, :], func=AF.Copy),
                                      reads=[bank], writes=[foT[bs]])
                            else:
                                tk.op("dve", lambda e, cc=cc, bank=bank: e.tensor_copy(out=foT[bs][:, cc, :], in_=bank[:, :]),
                                      reads=[bank], writes=[foT[bs]])
                        tk.dma("pool", f"fst{bs}", FOd.rearrange("(c p) t -> p c t", p=128)[:, :, tq:tq + BLK], foT[bs][:, :, :],
                               reads=[foT[bs]])
            tk.barrier()

            phase_end()
            with ExitStack() as st4:
                Wuv = sb(st4, [128, 8, 512], BF16, "Wuv")
                load_w(Wuv, 0, W2, D, C_U, 512, "ldw")
                swT = sb(st4, [128, 4, 128], BF16, "swT")
                for h in range(4):
                    tk.dma("pool", "ldw", swT[:, h, :], sgu_wT[l, h, :, :], writes=[swT])
                btab = sb(st4, [128, 256], F32, "btab")
                tk.dma("sp", "ld_const", btab[:, :], sgu_bt[l, :, :], writes=[btab])
                gvn = sb(st4, [128, 256], F32, "gvn")
                bvn = sb(st4, [128, 256], F32, "bvn")
                bcast_load(gvn, vn_g[l, :])
                bcast_load(bvn, vn_b[l, :])
                XTc = [sb(st4, [128, 8, BLK], BF16, "XTc") for _ in range(2)]
                ub = [sb(st4, [128, 256], F32, "ub") for _ in range(2)]
                vn0 = [sb(st4, [128, 256], F32, "vn0") for _ in range(2)]
                vnb = [sb(st4, [128, 256], BF16, "vnb") for _ in range(2)]
                junk = [sb(st4, [128, 256], F32, "junk") for _ in range(2)]
                so = [sb(st4, [128, 256], F32, "so") for _ in range(2)]
                sst = [sb(st4, [128, 8], F32, "sst") for _ in range(2)]
                soT = [sb(st4, [128, 2, BLK], BF16, "soT") for _ in range(2)]
                cjobs = []
                for (t0u, Sk, Sq) in qunits:
                    for qb in range(Sq // BLK):
                        cjobs.append(t0u + qb * BLK)

                def c_load(bi_):
                    s = bi_ % 2
                    tq = cjobs[bi_]
                    tk.dma("sp", f"cx{s}", XTc[s][:, :, :], XTd.rearrange("(kc p) t -> p kc t", p=128)[:, :, tq:tq + BLK],
                           writes=[XTc[s]])

                c_load(0)
                for bi_, tq in enumerate(cjobs):
                    s = bi_ % 2
                    if bi_ + 1 < len(cjobs):
                        c_load(bi_ + 1)
                    for j in range(4):
                        p = j % 2
                        bank = PS[p]
                        for kc in range(8):
                            tk.op("pe", lambda e, kc=kc, bank=bank, j=j: e.matmul(bank[:, :], XTc[s][:, kc, j * 128:(j + 1) * 128], Wuv[:, kc, :],
                                                                              start=(kc == 0), stop=(kc == 7)),
                                  reads=[XTc[s], Wuv], writes=[bank])
                        tk.op("act", lambda e, bank=bank, p=p: e.activation(out=junk[p][:, :], in_=bank[:, 256:512], func=AF.Copy,
                                                                            accum_out=sst[p][:, 0:1]),
                              reads=[bank], writes=[junk[p], sst[p]])
                        tk.op("act", lambda e, bank=bank, p=p: e.activation(out=junk[p][:, :], in_=bank[:, 256:512], func=AF.Square,
                                                                            accum_out=sst[p][:, 1:2]),
                              reads=[bank], writes=[junk[p], sst[p]])
                        tk.op("act", lambda e, bank=bank, p=p: e.activation(out=ub[p][:, :], in_=bank[:, 0:256], func=AF.Copy),
                              reads=[bank], writes=[ub[p]])
                        tk.op("dve", lambda e, p=p: e.tensor_scalar(out=sst[p][:, 2:4], in0=sst[p][:, 0:2], scalar1=1.0 / 256.0, scalar2=None,
                                                                    op0=ALU.mult), reads=[sst[p]], writes=[sst[p]])
                        tk.op("dve", lambda e, p=p: e.tensor_tensor(out=sst[p][:, 4:5], in0=sst[p][:, 2:3], in1=sst[p][:, 2:3], op=ALU.mult),
                              reads=[sst[p]], writes=[sst[p]])
                        tk.op("dve", lambda e, p=p: e.tensor_tensor(out=sst[p][:, 5:6], in0=sst[p][:, 3:4], in1=sst[p][:, 4:5], op=ALU.subtract),
                              reads=[sst[p]], writes=[sst[p]])
                        tk.op("dve", lambda e, p=p: e.tensor_scalar(out=sst[p][:, 6:7], in0=sst[p][:, 5:6], scalar1=LN_EPS, scalar2=None,
                                                                    op0=ALU.add), reads=[sst[p]], writes=[sst[p]])
                        tk.op("act", lambda e, p=p: e.activation(out=sst[p][:, 7:8], in_=sst[p][:, 6:7], func=AF.Sqrt),
                              reads=[sst[p]], writes=[sst[p]])
                        tk.op("dve", lambda e, p=p: e.reciprocal(out=sst[p][:, 6:7], in_=sst[p][:, 7:8]), reads=[sst[p]], writes=[sst[p]])
                        tk.op("dve", lambda e, bank=bank, p=p: e.tensor_scalar(out=vn0[p][:, :], in0=bank[:, 256:512], scalar1=sst[p][:, 2:3],
                                                                               scalar2=sst[p][:, 6:7], op0=ALU.subtract, op1=ALU.mult),
                              reads=[bank, sst[p]], writes=[vn0[p]])
                        tk.op("pool", lambda e, p=p: e.tensor_tensor(out=vn0[p][:, :], in0=vn0[p][:, :], in1=gvn[:, :], op=ALU.mult),
                              reads=[vn0[p], gvn], writes=[vn0[p]])
                        tk.op("pool", lambda e, p=p: e.tensor_tensor(out=vnb[p][:, :], in0=vn0[p][:, :], in1=bvn[:, :], op=ALU.add),
                              reads=[vn0[p], bvn], writes=[vnb[p]])
                        bank2 = PS[2 + p]
                        for h in range(4):
                            tk.op("pe", lambda e, h=h, bank2=bank2, p=p: e.matmul(bank2[:, h * 64:(h + 1) * 64], swT[:, h, :], vnb[p][:, h * 64:(h + 1) * 64],
                                                                              start=True, stop=True),
                                  reads=[swT, vnb[p]], writes=[bank2])
                        tk.op("dve", lambda e, bank2=bank2, p=p: e.tensor_tensor(out=so[p][:, :], in0=bank2[:, 0:256], in1=btab[:, :], op=ALU.add),
                              reads=[bank2, btab], writes=[so[p]])
                        tk.op("dve", lambda e, p=p: e.tensor_tensor(out=so[p][:, :], in0=so[p][:, :], in1=ub[p][:, :], op=ALU.mult),
                              reads=[so[p], ub[p]], writes=[so[p]])
                        for cc in range(2):
                            bank3 = PS[4 + cc + 2 * (bi_ % 2)]
                            tk.op("pe", lambda e, cc=cc, bank3=bank3, p=p, j=j: e.transpose(bank3[:, j * 128:(j + 1) * 128], so[p][:, cc * 128:(cc + 1) * 128],
                                                                                        ident[:, :]),
                                  reads=[so[p], ident], writes=[bank3])
                    for cc in range(2):
                        bank3 = PS[4 + cc + 2 * (bi_ % 2)]
                        if cc == 0:
                            tk.op("act", lambda e, cc=cc, bank3=bank3: e.activation(out=soT[s][:, cc, :], in_=bank3[:, :], func=AF.Copy),
                                  reads=[bank3], writes=[soT[s]])
                        else:
                            tk.op("dve", lambda e, cc=cc, bank3=bank3: e.tensor_copy(out=soT[s][:, cc, :], in_=bank3[:, :]),
                                  reads=[bank3], writes=[soT[s]])
                    tk.dma("pool", f"cst{s}", SOd.rearrange("(c p) t -> p c t", p=128)[:, :, tq:tq + BLK], soT[s][:, :, :],
                           reads=[soT[s]])
            tk.barrier()

            phase_end()
            with ExitStack() as st5:
                Wg = sb(st5, [128, 8, 3072], BF16, "Wg")
                load_w(Wg, 0, W2, D, C_G, 3072, "ldw")
                Wfo = sb(st5, [128, 2, D], BF16, "Wfo")
                load_w(Wfo, 0, w_fourier[l], 256, 0, D, "ldw")
                Wsg = sb(st5, [128, 2, D], BF16, "Wsg")
                load_w(Wsg, 0, w_sgu[l], 256, 0, D, "ldw")
                Wdf = sb(st5, [128, 4, D], BF16, "Wdf")
                load_w(Wdf, 0, w_diff[l], 512, 0, D, "ldw")
                XTe = [sb(st5, [128, 8, BLK], BF16, "XTe") for _ in range(2)]
                dob = [sb(st5, [128, 4, BLK], BF16, "dob") for _ in range(2)]
                sob = [sb(st5, [128, 2, BLK], BF16, "sob") for _ in range(2)]
                fob = [sb(st5, [128, 2, BLK], BF16, "fob") for _ in range(2)]
                sgt = [[sb(st5, [128, BLK], F32, "sgt") for _ in range(3)] for _ in range(2)]
                mt = [sb(st5, [128, BLK], F32, "mt") for _ in range(2)]
                mT = [sb(st5, [128, 8, BLK], BF16, "mT") for _ in range(2)]
                djobs = list(cjobs)

                def d_load(bi_):
                    s = bi_ % 2
                    tq = djobs[bi_]
                    tk.dma("sp", f"dx{s}", XTe[s][:, :, :], XTd.rearrange("(kc p) t -> p kc t", p=128)[:, :, tq:tq + BLK], writes=[XTe[s]])
                    tk.dma("sp", f"dx{s}", dob[s][:, :, :], DOd.rearrange("(h p) t -> p h t", p=128)[:, :, tq:tq + BLK], writes=[dob[s]])
                    tk.dma("sp", f"dx{s}", sob[s][:, :, :], SOd.rearrange("(c p) t -> p c t", p=128)[:, :, tq:tq + BLK], writes=[sob[s]])
                    tk.dma("sp", f"dx{s}", fob[s][:, :, :], FOd.rearrange("(c p) t -> p c t", p=128)[:, :, tq:tq + BLK], writes=[fob[s]])

                d_load(0)
                for bi_, tq in enumerate(djobs):
                    s = bi_ % 2
                    if bi_ + 1 < len(djobs):
                        d_load(bi_ + 1)
                    for n in range(8):
                        p = n % 2
                        for b in range(3):
                            bank = PS[b]
                            for kc in range(8):
                                tk.op("pe", lambda e, kc=kc, b=b, bank=bank, n=n: e.matmul(
                                    bank[:, :], Wg[:, kc, b * D + n * 128:b * D + (n + 1) * 128], XTe[s][:, kc, :],
                                    start=(kc == 0), stop=(kc == 7)), reads=[Wg, XTe[s]], writes=[bank])
                            tk.op("act", lambda e, b=b, bank=bank, p=p: e.activation(out=sgt[p][b][:, :], in_=bank[:, :], func=AF.Sigmoid),
                                  reads=[bank], writes=[sgt[p][b]])
                        brs = [(Wfo, fob[s], 2), (Wsg, sob[s], 2), (Wdf, dob[s], 4)]
                        for b, (Wb, xb_, nkc) in enumerate(brs):
                            bank = PS[3 + b]
                            for kc in range(nkc):
                                tk.op("pe", lambda e, kc=kc, bank=bank, Wb=Wb, xb_=xb_, nkc=nkc, n=n: e.matmul(
                                    bank[:, :], Wb[:, kc, n * 128:(n + 1) * 128], xb_[:, kc, :], start=(kc == 0), stop=(kc == nkc - 1)),
                                    reads=[Wb, xb_], writes=[bank])
                        tk.op("dve", lambda e, p=p: e.tensor_tensor(out=mt[0][:, :], in0=PS[3][:, :], in1=sgt[p][0][:, :], op=ALU.mult),
                              reads=[PS[3], sgt[p][0]], writes=[mt[0]])
                        tk.op("dve", lambda e, p=p: e.tensor_tensor(out=mt[1][:, :], in0=PS[4][:, :], in1=sgt[p][1][:, :], op=ALU.mult),
                              reads=[PS[4], sgt[p][1]], writes=[mt[1]])
                        tk.op("pool", lambda e: e.tensor_tensor(out=mt[0][:, :], in0=mt[0][:, :], in1=mt[1][:, :], op=ALU.add),
                              reads=[mt[0], mt[1]], writes=[mt[0]])
                        tk.op("dve", lambda e, p=p: e.tensor_tensor(out=mt[1][:, :], in0=PS[5][:, :], in1=sgt[p][2][:, :], op=ALU.mult),
                              reads=[PS[5], sgt[p][2]], writes=[mt[1]])
                        tk.op("pool", lambda e, n=n: e.tensor_tensor(out=mT[s][:, n, :], in0=mt[0][:, :], in1=mt[1][:, :], op=ALU.add),
                              reads=[mt[0], mt[1]], writes=[mT[s]])
                    tk.dma("pool", f"dst{s}", MTd.rearrange("(kc p) t -> p kc t", p=128)[:, :, tq:tq + BLK], mT[s][:, :, :], reads=[mT[s]])
            tk.barrier()

            phase_end()
            with ExitStack() as st6:
                Wo = sb(st6, [128, 8, D], BF16, "Wo")
                load_w(Wo, 0, w_out[l], D, 0, D, "ldw")
                g1 = sb(st6, [128, D], F32, "g1")
                b1 = sb(st6, [128, D], F32, "b1")
                bcast_load(g1, ln1_g[l, :])
                bcast_load(b1, ln1_b[l, :])
                moe = (l % 2 == 1)
                if moe:
                    Wr = sb(st6, [128, 8, NE], F32, "Wr")
                    tk.dma("sp", "ld_const", Wr[:, :, :], w_router[l // 2].rearrange("(kc p) e -> p kc e", p=128), writes=[Wr])
                MTb = [sb(st6, [128, 8, BLK], BF16, "MTb") for _ in range(2)]
                xsb = [sb(st6, [128, 4, D], F32, "xsb") for _ in range(2)]
                yb = [sb(st6, [128, D], F32, "yb") for _ in range(2)]
                x1s = [sb(st6, [128, D], F32, "x1s") for _ in range(2)]
                stt = [sb(st6, [128, 12], F32, "stt") for _ in range(2)]
                mvt = [sb(st6, [128, 4], F32, "mvt") for _ in range(2)]
                x1T32 = [sb(st6, [128, 8, 128], F32, "x1T32") for _ in range(2)]
                x1Tb = [sb(st6, [128, 8, BLK], BF16, "x1Tb") for _ in range(2)]
                rt = [sb(st6, [128, 64], F32, "rt") for _ in range(2)]
                ejobs = list(cjobs)

                def e_load(bi_):
                    s = bi_ % 2
                    tq = ejobs[bi_]
                    tk.dma("sp", f"ex{s}", MTb[s][:, :, :], MTd.rearrange("(kc p) t -> p kc t", p=128)[:, :, tq:tq + BLK], writes=[MTb[s]])
                    tk.dma("sp", f"ex{s}", xsb[s][:, :, :], Xsrc[tq:tq + BLK, :].rearrange("(j p) d -> p j d", p=128), writes=[xsb[s]])

                e_load(0)
                for bi_, tq in enumerate(ejobs):
                    s = bi_ % 2
                    if bi_ + 1 < len(ejobs):
                        e_load(bi_ + 1)
                    for j in range(4):
                        p = j % 2
                        for nh in range(2):
                            bank = PS[(2 * j + nh) % 4]
                            for kc in range(8):
                                tk.op("pe", lambda e, kc=kc, bank=bank, nh=nh, j=j: e.matmul(
                                    bank[:, :], MTb[s][:, kc, j * 128:(j + 1) * 128], Wo[:, kc, nh * 512:(nh + 1) * 512],
                                    start=(kc == 0), stop=(kc == 7)), reads=[MTb[s], Wo], writes=[bank])
                            tk.op("dve", lambda e, bank=bank, nh=nh, j=j, p=p: e.scalar_tensor_tensor(
                                out=yb[p][:, nh * 512:(nh + 1) * 512], in0=xsb[s][:, j, nh * 512:(nh + 1) * 512], scalar=ALPHA,
                                in1=bank[:, :], op0=ALU.mult, op1=ALU.add), reads=[xsb[s], bank], writes=[yb[p]])
                        layer_norm_tile((stt[p], mvt[p]), yb[p], g1, b1, x1s[p])
                        tk.dma("pool", f"est{p}", X1d[tq + j * 128:tq + (j + 1) * 128, :], x1s[p][:, :], reads=[x1s[p]])
                        for g in range(2):
                            bank = PS[4 + g]
                            for k4 in range(4):
                                kc = g * 4 + k4
                                tk.op("pe", lambda e, bank=bank, k4=k4, kc=kc, p=p: e.transpose(
                                    bank[:, k4 * 128:(k4 + 1) * 128], x1s[p][:, kc * 128:(kc + 1) * 128], ident[:, :]),
                                    reads=[x1s[p], ident], writes=[bank])
                            if not moe:
                                tk.op("act", lambda e, bank=bank, g=g, j=j: e.activation(
                                    out=x1Tb[s][:, g * 4:(g + 1) * 4, j * 128:(j + 1) * 128],
                                    in_=bank[:, :].rearrange("p (k t) -> p k t", k=4), func=AF.Copy),
                                    reads=[bank], writes=[x1Tb[s]])
                            else:
                                tk.op("act", lambda e, bank=bank, g=g, p=p: e.activation(
                                    out=x1T32[p][:, g * 4:(g + 1) * 4, :],
                                    in_=bank[:, :].rearrange("p (k t) -> p k t", k=4), func=AF.Copy),
                                    reads=[bank], writes=[x1T32[p]])
                                tk.op("pool", lambda e, g=g, j=j, p=p: e.tensor_copy(
                                    out=x1Tb[s][:, g * 4:(g + 1) * 4, j * 128:(j + 1) * 128],
                                    in_=x1T32[p][:, g * 4:(g + 1) * 4, :]),
                                    reads=[x1T32[p]], writes=[x1Tb[s]])
                        if moe:
                            bank = PS[6 + p]
                            for kc in range(8):
                                tk.op("pe", lambda e, kc=kc, bank=bank, p=p: e.matmul(bank[:, 0:NE], x1T32[p][:, kc, :], Wr[:, kc, :],
                                                                                  start=(kc == 0), stop=(kc == 7)),
                                      reads=[x1T32[p], Wr], writes=[bank])
                            r = rt[p]
                            tk.op("dve", lambda e, bank=bank, r=r: e.tensor_copy(out=r[:, 0:8], in_=bank[:, 0:NE]), reads=[bank], writes=[r])
                            tk.op("dve", lambda e, r=r: e.reduce_max(out=r[:, 32:33], in_=r[:, 0:8], axis=AX.X), reads=[r], writes=[r])
                            tk.op("dve", lambda e, r=r: e.tensor_scalar(out=r[:, 8:16], in0=r[:, 0:8], scalar1=r[:, 32:33], scalar2=None,
                                                                        op0=ALU.is_equal), reads=[r], writes=[r])
                            tk.op("dve", lambda e, r=r: e.scalar_tensor_tensor(out=r[:, 16:24], in0=r[:, 8:16], scalar=-1e30, in1=r[:, 0:8],
                                                                               op0=ALU.mult, op1=ALU.add), reads=[r], writes=[r])
                            tk.op("dve", lambda e, r=r: e.reduce_max(out=r[:, 33:34], in_=r[:, 16:24], axis=AX.X), reads=[r], writes=[r])
                            tk.op("dve", lambda e, r=r: e.tensor_scalar(out=r[:, 24:32], in0=r[:, 16:24], scalar1=r[:, 33:34], scalar2=None,
                                                                        op0=ALU.is_equal), reads=[r], writes=[r])
                            tk.op("dve", lambda e, r=r: e.tensor_tensor(out=r[:, 34:35], in0=r[:, 33:34], in1=r[:, 32:33], op=ALU.subtract),
                                  reads=[r], writes=[r])
                            tk.op("act", lambda e, r=r: e.activation(out=r[:, 35:36], in_=r[:, 34:35], func=AF.Sigmoid), reads=[r], writes=[r])
                            tk.op("dve", lambda e, r=r: e.tensor_scalar(out=r[:, 36:37], in0=r[:, 35:36], scalar1=-1.0, scalar2=1.0,
                                                                        op0=ALU.mult, op1=ALU.add), reads=[r], writes=[r])
                            tk.op("dve", lambda e, r=r: e.tensor_scalar(out=r[:, 48:56], in0=r[:, 8:16], scalar1=r[:, 36:37], scalar2=None,
                                                                        op0=ALU.mult), reads=[r], writes=[r])
                            tk.op("dve", lambda e, r=r: e.scalar_tensor_tensor(out=r[:, 40:48], in0=r[:, 24:32], scalar=r[:, 35:36], in1=r[:, 48:56],
                                                                               op0=ALU.mult, op1=ALU.add), reads=[r], writes=[r])
                            tk.dma("pool", f"est{p}", CMBd[tq + j * 128:tq + (j + 1) * 128, :], r[:, 40:48], reads=[r])
                    tk.dma("pool", f"est2{s}", X1Td.rearrange("(kc p) t -> p kc t", p=128)[:, :, tq:tq + BLK], x1Tb[s][:, :, :], reads=[x1Tb[s]])
            tk.barrier()

            phase_end()
            moe = (l % 2 == 1)
            if moe:
                passes = [(moe_w_gate[l // 2, e_], moe_w_up[l // 2, e_], moe_w_down[l // 2, e_], e_) for e_ in range(NE)]
            else:
                passes = [(ffn_w_gate[l // 2], ffn_w_up[l // 2], ffn_w_down[l // 2], None)]
            with ExitStack() as st7:
                Wgs = sb(st7, [128, 8, DFF], BF16, "Wgs")
                Wus = sb(st7, [128, 8, DFF], BF16, "Wus")
                Wds = sb(st7, [128, NFC, D], BF16, "Wds")
                XTf = [sb(st7, [128, 8, BLK], BF16, "XTf") for _ in range(2)]
                cmbb = [sb(st7, [128, 4, NE], F32, "cmbb") for _ in range(2)]
                hT = sb(st7, [128, NFC, BLK], BF16, "hT")
                sgf = [sb(st7, [128, BLK], F32, "sgf") for _ in range(2)]
                ysb = [sb(st7, [128, D], F32, "ysb") for _ in range(2)]
                fjobs = list(cjobs)
                for (wg_ap, wu_ap, wd_ap, ex) in passes:
                    load_w(Wgs, 0, wg_ap, D, 0, DFF, "ldw")
                    load_w(Wus, 0, wu_ap, D, 0, DFF, "ldw")
                    load_w(Wds, 0, wd_ap, DFF, 0, D, "ldw")

                    def g_load(bi_):
                        s = bi_ % 2
                        tq = fjobs[bi_]
                        tk.dma("sp", f"gx{s}", XTf[s][:, :, :], X1Td.rearrange("(kc p) t -> p kc t", p=128)[:, :, tq:tq + BLK], writes=[XTf[s]])
                        if ex is not None:
                            tk.dma("sp", f"gx{s}", cmbb[s][:, :, :], CMBd[tq:tq + BLK, :].rearrange("(j p) e -> p j e", p=128), writes=[cmbb[s]])

                    g_load(0)
                    for bi_, tq in enumerate(fjobs):
                        s = bi_ % 2
                        if bi_ + 1 < len(fjobs):
                            g_load(bi_ + 1)
                        for c in range(NFC):
                            rows = min(128, DFF - c * 128)
                            bg = PS[(2 * c) % 4]
                            bu = PS[(2 * c + 1) % 4]
                            for kc in range(8):
                                tk.op("pe", lambda e, kc=kc, bg=bg, c=c, rows=rows: e.matmul(bg[0:rows, :], Wgs[:, kc, c * 128:c * 128 + rows], XTf[s][:, kc, :],
                                                                                         start=(kc == 0), stop=(kc == 7)),
                                      reads=[Wgs, XTf[s]], writes=[bg])
                            for kc in range(8):
                                tk.op("pe", lambda e, kc=kc, bu=bu, c=c, rows=rows: e.matmul(bu[0:rows, :], Wus[:, kc, c * 128:c * 128 + rows], XTf[s][:, kc, :],
                                                                                         start=(kc == 0), stop=(kc == 7)),
                                      reads=[Wus, XTf[s]], writes=[bu])
                            sg_ = sgf[c % 2]
                            tk.op("act", lambda e, bg=bg, sg_=sg_, rows=rows: e.activation(out=sg_[0:rows, :], in_=bg[0:rows, :], func=AF.Silu),
                                  reads=[bg], writes=[sg_])
                            tk.op("dve", lambda e, bu=bu, sg_=sg_, rows=rows, c=c: e.tensor_tensor(out=hT[0:rows, c, :], in0=bu[0:rows, :], in1=sg_[0:rows, :],
                                                                                               op=ALU.mult),
                                  reads=[bu, sg_], writes=[hT])
                        for j in range(4):
                            p = j % 2
                            for nh in range(2):
                                bank = PS[4 + (2 * j + nh) % 4]
                                for c in range(NFC):
                                    rows = min(128, DFF - c * 128)
                                    tk.op("pe", lambda e, c=c, bank=bank, rows=rows, nh=nh, j=j: e.matmul(
                                        bank[:, :], hT[0:rows, c, j * 128:(j + 1) * 128], Wds[0:rows, c, nh * 512:(nh + 1) * 512],
                                        start=(c == 0), stop=(c == NFC - 1)), reads=[hT, Wds], writes=[bank])
                                if ex is None:
                                    tk.op("act", lambda e, bank=bank, nh=nh, p=p: e.activation(out=ysb[p][:, nh * 512:(nh + 1) * 512], in_=bank[:, :],
                                                                                           func=AF.Copy), reads=[bank], writes=[ysb[p]])
                                else:
                                    tk.op("act", lambda e, bank=bank, nh=nh, p=p, j=j: e.activation(
                                        out=ysb[p][:, nh * 512:(nh + 1) * 512], in_=bank[:, :], func=AF.Copy, scale=cmbb[s][:, j, ex:ex + 1]),
                                        reads=[bank, cmbb[s]], writes=[ysb[p]])
                            ydst = Yd[tq + j * 128:tq + (j + 1) * 128, :]
                            if ex is None or ex == 0:
                                tk.dma("pool", f"gst{p}", ydst, ysb[p][:, :], reads=[ysb[p]])
                            else:
                                tk.dma("pool", f"gst{p}", ydst, ysb[p][:, :], reads=[ysb[p]], accum_op=ALU.add)
                    tk.barrier()

            phase_end()
            with ExitStack() as st8:
                g2 = sb(st8, [128, D], F32, "g2")
                b2 = sb(st8, [128, D], F32, "b2")
                bcast_load(g2, ln2_g[l, :])
                bcast_load(b2, ln2_b[l, :])
                xa = [sb(st8, [128, 4, D], F32, "xa") for _ in range(2)]
                ya = [sb(st8, [128, 4, D], F32, "ya") for _ in range(2)]
                tb_ = [sb(st8, [128, D], F32, "tb") for _ in range(2)]
                ob = [sb(st8, [128, D], F32, "ob") for _ in range(2)]
                stt2 = [sb(st8, [128, 12], F32, "stt2") for _ in range(2)]
                mvt2 = [sb(st8, [128, 4], F32, "mvt2") for _ in range(2)]
                hjobs4 = list(cjobs)
                dest = Xmid if l + 1 < NLAYER else out

                def h_load(bi_):
                    s = bi_ % 2
                    tq = hjobs4[bi_]
                    tk.dma("sp", f"hx{s}", xa[s][:, :, :], X1d[tq:tq + BLK, :].rearrange("(j p) d -> p j d", p=128), writes=[xa[s]])
                    tk.dma("sp", f"hx{s}", ya[s][:, :, :], Yd[tq:tq + BLK, :].rearrange("(j p) d -> p j d", p=128), writes=[ya[s]])

                h_load(0)
                for bi_, tq in enumerate(hjobs4):
                    s = bi_ % 2
                    if bi_ + 1 < len(hjobs4):
                        h_load(bi_ + 1)
                    for j in range(4):
                        p = j % 2
                        tk.op("dve", lambda e, j=j, p=p: e.scalar_tensor_tensor(out=tb_[p][:, :], in0=xa[s][:, j, :], scalar=ALPHA, in1=ya[s][:, j, :],
                                                                            op0=ALU.mult, op1=ALU.add), reads=[xa[s], ya[s]], writes=[tb_[p]])
                        layer_norm_tile((stt2[p], mvt2[p]), tb_[p], g2, b2, ob[p])
                        tk.dma("pool", f"hst{p}", dest[tq + j * 128:tq + (j + 1) * 128, :], ob[p][:, :], reads=[ob[p]])
            tk.barrier()
            phase_end()
      except _StopBuild:
        pass
    return nc


def _rope_tables(pos):
    inv = ROPE_THETA ** (-np.arange(0, 64, 2, dtype=np.float64) / 64.0)
    ang = pos.astype(np.float64)[None, :] * inv[:, None]
    c = np.cos(ang)
    s = np.sin(ang)
    d = np.arange(128) % 64
    j = d % 32
    sign = np.where(d < 32, -1.0, 1.0)
    return c[j].astype(np.float32), (s[j] * sign[:, None]).astype(np.float32)


def _dft_mats(pos_rows, pos_cols, S):
    k = (pos_rows.astype(np.int64)[:, None] * pos_cols.astype(np.int64)[None, :]) % S
    ang = (2.0 * np.pi / S) * k.astype(np.float64)
    sc = 1.0 / math.sqrt(S * 64.0)
    return (np.cos(ang) * sc).astype(ml_dtypes.bfloat16), (np.sin(ang) * sc).astype(ml_dtypes.bfloat16)


_PROG_CACHE = {}


def run_module(inputs, SS, SP, NLAYER=2, debug=None, stop_phase=None, run_cores=8):
    f32 = np.float32
    xp = np.asarray(inputs["x_prompt"], f32)
    xs = np.asarray(inputs["x_sample"], f32)
    ncores = 8
    HP = SP // 2
    assert xp.shape[0] * 2 == ncores and xs.shape[0] == 2 * ncores
    w_in = np.asarray(inputs["w_in"], f32)
    perm = np.arange(512).reshape(8, 2, 32)[:, ::-1, :].reshape(512)
    q = w_in[:, :, 768:1280]
    k = w_in[:, :, 1280:1792]
    w_in_e = np.ascontiguousarray(np.concatenate([w_in, q[:, :, perm], k[:, :, perm]], axis=2))
    sgu_w = np.asarray(inputs["sgu_w"], f32)
    sgu_b = np.asarray(inputs["sgu_b"], f32)
    sgu_wT = np.ascontiguousarray(np.transpose(sgu_w, (0, 1, 3, 2)))
    sgu_bt = np.ascontiguousarray(np.repeat(np.transpose(sgu_b, (0, 2, 1))[:, :, :, None], 64, axis=3).reshape(sgu_b.shape[0], 128, 256))
    ident = np.eye(128, dtype=f32)
    kk = np.arange(64)
    ang = 2.0 * np.pi * ((kk[:, None] * kk[None, :]) % 64) / 64.0
    bdc = np.zeros((128, 128), f32)
    bdsn = np.zeros((128, 128), f32)
    for g in range(2):
        bdc[g * 64:(g + 1) * 64, g * 64:(g + 1) * 64] = np.cos(ang)
        bdsn[g * 64:(g + 1) * 64, g * 64:(g + 1) * 64] = -np.sin(ang)
    pos_s = np.arange(SS)
    dcs, dss = _dft_mats(pos_s, pos_s, SS)
    shared = {
        "w_in_e": w_in_e, "sgu_wT": sgu_wT, "sgu_bt": sgu_bt, "ident": ident, "bdc": bdc, "bdsn": bdsn,
        "dft_cs": dcs, "dft_ss": dss,
    }
    for nm in ["w_fourier", "w_sgu", "w_diff", "w_out", "vn_g", "vn_b", "lam_q1", "lam_k1", "lam_q2", "lam_k2", "subln_g",
               "ln1_g", "ln1_b", "ln2_g", "ln2_b", "ffn_w_gate", "ffn_w_up", "ffn_w_down", "w_router",
               "moe_w_gate", "moe_w_up", "moe_w_down"]:
        shared[nm] = np.ascontiguousarray(np.asarray(inputs[nm], f32))
    par = {}
    for parity in range(2):
        pos_p = np.concatenate([np.arange(parity * HP, (parity + 1) * HP), np.arange((1 - parity) * HP, (2 - parity) * HP)])
        dcp, dsp = _dft_mats(pos_p, pos_p, SP)
        pos_all = np.concatenate([pos_s, pos_s, pos_p])
        rc, rs = _rope_tables(pos_all)
        par[parity] = dict(pos_p=pos_p, dft_cp=dcp, dft_sp=dsp, ropec=rc, ropes=rs)
    in_maps = []
    for c in range(ncores):
        parity = c % 2
        P = par[parity]
        xin = np.concatenate([xs[2 * c], xs[2 * c + 1], xp[c // 2][P["pos_p"]]], axis=0)
        m = dict(shared)
        m.update(xin=np.ascontiguousarray(xin), dft_cp=P["dft_cp"], dft_sp=P["dft_sp"], ropec=P["ropec"], ropes=P["ropes"])
        in_maps.append(m)
    key = (SS, SP, NLAYER, tuple(sorted(debug)) if debug else None, stop_phase)
    if key not in _PROG_CACHE:
        _PROG_CACHE[key] = build_program(SS, SP, NLAYER, debug, stop_phase)
    nc = _PROG_CACHE[key]
    res = run_bass_kernel_spmd(nc, in_maps[:run_cores], core_ids=list(range(run_cores)))
    outs = [r["out"] for r in res.results]
    outs = outs + [outs[0]] * (ncores - run_cores)
    y_sample = np.stack([outs[c][i * SS:(i + 1) * SS] for c in range(ncores) for i in range(2)], axis=0)
    y_prompt = np.stack([np.concatenate([outs[2 * b][2 * SS:2 * SS + HP], outs[2 * b + 1][2 * SS:2 * SS + HP]], axis=0)
                         for b in range(ncores // 2)], axis=0)
    if debug:
        return (y_prompt.astype(f32), y_sample.astype(f32)), res.results
    return (y_prompt.astype(f32), y_sample.astype(f32))


def kernel(**inputs):
    return run_module(inputs, SS=4096, SP=8192)
```

```python
import math
from contextlib import ExitStack

import numpy as np
import ml_dtypes

import concourse.bass as bass
import concourse.mybir as mybir
from concourse.bass_utils import run_bass_kernel_spmd

F32 = mybir.dt.float32
BF16 = mybir.dt.bfloat16
AF = mybir.ActivationFunctionType
ALU = mybir.AluOpType
AX = mybir.AxisListType

D = 1024
DFF = 2752
NE = 8
NFC = 22
DEPTH = 2
ALPHA = (2 * DEPTH) ** 0.25
LN_EPS = 1e-5
RMS_EPS = 1e-5
ROPE_THETA = 10000.0
C_F, C_U, C_V, C_Q, C_K, C_VA, C_G, C_QS, C_KS = 0, 256, 512, 768, 1280, 1792, 2304, 5376, 5888
NW = 6400
BLK = 512


class Buf:
    def __init__(self, ap, name=""):
        self.ap = ap
        self.name = name
        self.w = {}
        self.r = {}

    def __getitem__(self, idx):
        return self.ap[idx]


class Tracker:
    def __init__(self, nc, es):
        self.nc = nc
        self.es = es
        self.eng = {"pe": nc.tensor, "act": nc.scalar, "dve": nc.vector, "pool": nc.gpsimd, "sp": nc.sync}
        self.sems = {}
        self.cnt = {}
        self.waited = {k: {} for k in self.eng}
        self.nconst = 0
        for k in self.eng:
            self._mksem(k)

    def _mksem(self, key):
        self.sems[key] = self.es.enter_context(self.nc.semaphore("s_" + key))
        self.cnt[key] = 0

    def _wait(self, e, key, val):
        if key == "pe" and e == "pe":
            return
        if key not in self.eng:
            val = self.cnt[key]
        if self.waited[e].get(key, 0) >= val:
            return
        self.eng[e].wait_ge(self.sems[key], val)
        self.waited[e][key] = val

    def _deps(self, e, reads, writes):
        for b in reads:
            for k, v in b.w.items():
                self._wait(e, k, v)
        for b in writes:
            for k, v in b.w.items():
                self._wait(e, k, v)
            for k, v in b.r.items():
                self._wait(e, k, v)

    def _mark(self, key, val, reads, writes):
        for b in reads:
            if b.r.get(key, 0) < val:
                b.r[key] = val
        for b in writes:
            if b.w.get(key, 0) < val:
                b.w[key] = val

    def op(self, e, fn, reads=(), writes=()):
        self._deps(e, reads, writes)
        ins = fn(self.eng[e])
        ins.then_inc(self.sems[e], 1)
        self.cnt[e] += 1
        self._mark(e, self.cnt[e], reads, writes)

    def dma(self, q, semkey, out, in_, reads=(), writes=(), **kw):
        if semkey == "ld_const":
            semkey = f"ldc{self.nconst}"
            self.nconst += 1
        if semkey not in self.sems:
            self._mksem(semkey)
        self._deps(q, reads, writes)
        ins = self.eng[q].dma_start(out=out, in_=in_, **kw)
        ins.then_inc(self.sems[semkey], 16)
        self.cnt[semkey] += 16
        self._mark(semkey, self.cnt[semkey], reads, writes)

    def barrier(self):
        self.nconst = 0
        for e in self.eng:
            for k in self.sems:
                if self.cnt[k] > 0:
                    self._wait(e, k, self.cnt[k])


class _StopBuild(Exception):
    pass


def build_program(SS, SP, NLAYER=2, debug=None, stop_phase=None):
    T0 = 2 * SS + SP
    HP = SP // 2
    T1 = 2 * SS + HP
    units = [(0, SS), (SS, SS), (2 * SS, SP)]

    nc = bass.Bass("TRN2", target_bir_lowering=False)

    def din(name, shape, dt=F32):
        return nc.dram_tensor(name, list(shape), dt, kind="ExternalInput").ap()

    def dscr(name, shape, dt):
        kind = {}
        if debug and name in debug:
            kind = dict(kind="ExternalOutput")
        return nc.dram_tensor(name, list(shape), dt, **kind).ap()

    xin = din("xin", [T0, D])
    w_in = din("w_in_e", [2, D, NW])
    w_fourier = din("w_fourier", [2, 256, D])
    w_sgu = din("w_sgu", [2, 256, D])
    w_diff = din("w_diff", [2, 512, D])
    w_out = din("w_out", [2, D, D])
    vn_g = din("vn_g", [2, 256])
    vn_b = din("vn_b", [2, 256])
    sgu_wT = din("sgu_wT", [2, 4, 128, 128])
    sgu_bt = din("sgu_bt", [2, 128, 256])
    lam_q1 = din("lam_q1", [2, 64])
    lam_k1 = din("lam_k1", [2, 64])
    lam_q2 = din("lam_q2", [2, 64])
    lam_k2 = din("lam_k2", [2, 64])
    subln_g = din("subln_g", [2, 128])
    ln1_g = din("ln1_g", [2, D])
    ln1_b = din("ln1_b", [2, D])
    ln2_g = din("ln2_g", [2, D])
    ln2_b = din("ln2_b", [2, D])
    ffn_w_gate = din("ffn_w_gate", [1, D, DFF])
    ffn_w_up = din("ffn_w_up", [1, D, DFF])
    ffn_w_down = din("ffn_w_down", [1, DFF, D])
    w_router = din("w_router", [1, D, NE])
    moe_w_gate = din("moe_w_gate", [1, NE, D, DFF])
    moe_w_up = din("moe_w_up", [1, NE, D, DFF])
    moe_w_down = din("moe_w_down", [1, NE, DFF, D])
    ident_d = din("ident", [128, 128])
    ropec = din("ropec", [128, T0])
    ropes = din("ropes", [128, T0])
    dft_cs = din("dft_cs", [SS, SS], BF16)
    dft_ss = din("dft_ss", [SS, SS], BF16)
    dft_cp = din("dft_cp", [SP, SP], BF16)
    dft_sp = din("dft_sp", [SP, SP], BF16)
    bdc_d = din("bdc", [128, 128])
    bdsn_d = din("bdsn", [128, 128])
    out = nc.dram_tensor("out", [T1, D], F32, kind="ExternalOutput").ap()

    XTd = dscr("XTd", [D, T0], BF16)
    KTd = dscr("KTd", [512, T0], BF16)
    Vd = dscr("Vd", [4, 128, T0 // 128, 128], BF16)
    Fd = dscr("Fd", [T0, 256], BF16)
    DOd = dscr("DOd", [512, T0], BF16)
    SOd = dscr("SOd", [256, T0], BF16)
    FOd = dscr("FOd", [256, T0], BF16)
    MTd = dscr("MTd", [D, T0], BF16)
    X1d = dscr("X1d", [T0, D], F32)
    X1Td = dscr("X1Td", [D, T0], BF16)
    CMBd = dscr("CMBd", [T0, NE], F32)
    Yd = dscr("Yd", [T0, D], F32)
    Xmid = dscr("Xmid", [T0, D], F32)

    es = ExitStack()
    with es:
      try:
        tk = Tracker(nc, es)
        uid = [0]

        def sb(stack, shape, dt, name):
            uid[0] += 1
            t = stack.enter_context(nc.sbuf_tensor(f"{name}_{uid[0]}", list(shape), dt))
            return Buf(t, name)

        PS = [Buf(es.enter_context(nc.psum_tensor(f"psum{i}", [128, 512], F32)), f"ps{i}") for i in range(8)]

        ident = sb(es, [128, 128], F32, "ident")
        tk.dma("sp", "ld_const", ident[:], ident_d[:, :], writes=[ident])
        ones_bf = sb(es, [128, 128], BF16, "ones_bf")
        onesm_bf = sb(es, [128, 128], BF16, "onesm_bf")
        tk.op("dve", lambda e: e.memset(ones_bf[:], 1.0), writes=[ones_bf])
        tk.op("dve", lambda e: e.memset(onesm_bf[:], 1.0 / 128.0), writes=[onesm_bf])
        bdc = sb(es, [128, 128], BF16, "bdc")
        bdsn = sb(es, [128, 128], BF16, "bdsn")
        tk.dma("pool", "ld_constp", bdc[:], bdc_d[:, :], writes=[bdc])
        tk.dma("pool", "ld_constp", bdsn[:], bdsn_d[:, :], writes=[bdsn])

        def load_w(dst, dst_c0, src2d, K, c0, ncols, semkey):
            nk = (K + 127) // 128
            for kc in range(nk):
                rows = min(128, K - kc * 128)
                cc = 0
                while cc < ncols:
                    n = min(2048, ncols - cc)
                    tk.dma("pool", semkey, dst[0:rows, kc, dst_c0 + cc:dst_c0 + cc + n],
                           src2d[kc * 128:kc * 128 + rows, c0 + cc:c0 + cc + n], writes=[dst])
                    cc += n

        def layer_norm_tile(stack_bufs, y, gtab, btab, outb, l_eng="pool"):
            st, mv = stack_bufs
            tk.op("dve", lambda e: e.bn_stats(out=st[:, 0:6], in_=y[:, 0:512]), reads=[y], writes=[st])
            tk.op("dve", lambda e: e.bn_stats(out=st[:, 6:12], in_=y[:, 512:1024]), reads=[y], writes=[st])
            tk.op("dve", lambda e: e.bn_aggr(out=mv[:, 0:2], in_=st[:, 0:12]), reads=[st], writes=[mv])
            tk.op("dve", lambda e: e.tensor_scalar(out=mv[:, 2:3], in0=mv[:, 1:2], scalar1=LN_EPS, scalar2=None,
                                                   op0=ALU.add), reads=[mv], writes=[mv])
            tk.op("act", lambda e: e.activation(out=mv[:, 3:4], in_=mv[:, 2:3], func=AF.Sqrt), reads=[mv], writes=[mv])
            tk.op("dve", lambda e: e.reciprocal(out=mv[:, 2:3], in_=mv[:, 3:4]), reads=[mv], writes=[mv])
            tk.op("dve", lambda e: e.tensor_scalar(out=y[:, :], in0=y[:, :], scalar1=mv[:, 0:1], scalar2=mv[:, 2:3],
                                                   op0=ALU.subtract, op1=ALU.mult), reads=[y, mv], writes=[y])
            tk.op(l_eng, lambda e: e.tensor_tensor(out=y[:, :], in0=y[:, :], in1=gtab[:, :], op=ALU.mult),
                  reads=[y, gtab], writes=[y])
            tk.op(l_eng, lambda e: e.tensor_tensor(out=outb[:, :], in0=y[:, :], in1=btab[:, :], op=ALU.add),
                  reads=[y, btab], writes=[outb])

        def bcast_load(dst, vec_ap, semkey="ld_const"):
            tk.dma("sp", semkey, dst[:, :], vec_ap.partition_broadcast(128), writes=[dst])

        phc = [0]

        def phase_end():
            phc[0] += 1
            if stop_phase is not None and phc[0] >= stop_phase:
                raise _StopBuild()

        for l in range(NLAYER):
            Xsrc = xin if l == 0 else Xmid
            TQ = T0 if l == 0 else T1
            qunits = [(t0, Sk, (Sk if l == 0 else min(Sk, SS if t0 < 2 * SS else HP))) for (t0, Sk) in units]
            lambda_init = 0.8 - 0.6 * math.exp(-0.3 * l)
            W2 = w_in[l]

            with ExitStack() as st1:
                Wk = sb(st1, [128, 8, 1792], BF16, "Wk")
                load_w(Wk, 0, W2, D, C_K, 512, "ldw")
                load_w(Wk, 512, W2, D, C_KS, 512, "ldw")
                load_w(Wk, 1024, W2, D, C_VA, 512, "ldw")
                load_w(Wk, 1536, W2, D, C_F, 256, "ldw")
                xs = [sb(st1, [128, 4, D], F32, "xs") for _ in range(2)]
                cs = [sb(st1, [128, 2, BLK], F32, "cs") for _ in range(2)]
                XT = [sb(st1, [128, 8, BLK], BF16, "XT") for _ in range(2)]
                kst = [sb(st1, [128, 4, BLK], BF16, "kst") for _ in range(2)]
                vst = [sb(st1, [128, 4, 4, 128], BF16, "vst") for _ in range(2)]
                fst = [sb(st1, [128, 4, 256], BF16, "fst") for _ in range(2)]
                tmp = [sb(st1, [128, BLK], F32, "tmp") for _ in range(4)]
                nb = T0 // BLK

                def p1_load(bi):
                    s = bi % 2
                    t0 = bi * BLK
                    tk.dma("sp", f"p1x{s}", xs[s][:, :, :], Xsrc[t0:t0 + BLK, :].rearrange("(j p) d -> p j d", p=128),
                           writes=[xs[s]])
                    tk.dma("sp", f"p1x{s}", cs[s][:, 0, :], ropec[:, t0:t0 + BLK], writes=[cs[s]])
                    tk.dma("sp", f"p1x{s}", cs[s][:, 1, :], ropes[:, t0:t0 + BLK], writes=[cs[s]])

                p1_load(0)
                for bi in range(nb):
                    s = bi % 2
                    t0 = bi * BLK
                    if bi + 1 < nb:
                        p1_load(bi + 1)
                    for kc in range(8):
                        bank = PS[kc % 2]
                        for j in range(4):
                            tk.op("pe", lambda e, j=j, kc=kc, bank=bank: e.transpose(
                                bank[:, j * 128:(j + 1) * 128], xs[s][:, j, kc * 128:(kc + 1) * 128], ident[:, :]),
                                reads=[xs[s], ident], writes=[bank])
                        if kc % 2 == 0:
                            tk.op("act", lambda e, kc=kc, bank=bank: e.activation(out=XT[s][:, kc, :], in_=bank[:, :], func=AF.Copy),
                                  reads=[bank], writes=[XT[s]])
                        else:
                            tk.op("dve", lambda e, kc=kc, bank=bank: e.tensor_copy(out=XT[s][:, kc, :], in_=bank[:, :]),
                                  reads=[bank], writes=[XT[s]])
                    tk.dma("pool", f"p1s{s}", XTd.rearrange("(kc p) t -> p kc t", p=128)[:, :, t0:t0 + BLK], XT[s][:, :, :],
                           reads=[XT[s]])
                    for h in range(4):
                        A = PS[2 + 2 * (h % 2)]
                        B = PS[3 + 2 * (h % 2)]
                        for kc in range(8):
                            tk.op("pe", lambda e, kc=kc, A=A: e.matmul(A[:, :], Wk[:, kc, h * 128:(h + 1) * 128], XT[s][:, kc, :],
                                                                    start=(kc == 0), stop=(kc == 7)),
                                  reads=[Wk, XT[s]], writes=[A])
                        for kc in range(8):
                            tk.op("pe", lambda e, kc=kc, B=B: e.matmul(B[:, :], Wk[:, kc, 512 + h * 128:512 + (h + 1) * 128], XT[s][:, kc, :],
                                                                    start=(kc == 0), stop=(kc == 7)),
                                  reads=[Wk, XT[s]], writes=[B])
                        ta, tb = tmp[2 * (h % 2)], tmp[2 * (h % 2) + 1]
                        tk.op("dve", lambda e, A=A, ta=ta: e.tensor_tensor(out=ta[:, :], in0=A[:, :], in1=cs[s][:, 0, :], op=ALU.mult),
                              reads=[A, cs[s]], writes=[ta])
                        tk.op("dve", lambda e, B=B, tb=tb: e.tensor_tensor(out=tb[:, :], in0=B[:, :], in1=cs[s][:, 1, :], op=ALU.mult),
                              reads=[B, cs[s]], writes=[tb])
                        tk.op("pool", lambda e, ta=ta, tb=tb: e.tensor_tensor(out=kst[s][:, h, :], in0=ta[:, :], in1=tb[:, :], op=ALU.add),
                              reads=[ta, tb], writes=[kst[s]])
                    tk.dma("pool", f"p1s{s}", KTd.rearrange("(h p) t -> p h t", p=128)[:, :, t0:t0 + BLK], kst[s][:, :, :],
                           reads=[kst[s]])
                    for st_ in range(8):
                        j = st_ % 4
                        bank = PS[6 + st_ % 2]
                        if st_ < 4:
                            for kc in range(8):
                                tk.op("pe", lambda e, kc=kc, bank=bank, j=j: e.matmul(bank[:, :], XT[s][:, kc, j * 128:(j + 1) * 128],
                                                                                  Wk[:, kc, 1024:1536], start=(kc == 0), stop=(kc == 7)),
                                      reads=[Wk, XT[s]], writes=[bank])
                            tk.op("act", lambda e, bank=bank, j=j: e.activation(
                                out=vst[s][:, :, j, :], in_=bank[:, :].rearrange("p (h e) -> p h e", h=4), func=AF.Copy),
                                reads=[bank], writes=[vst[s]])
                        else:
                            for kc in range(8):
                                tk.op("pe", lambda e, kc=kc, bank=bank, j=j: e.matmul(bank[:, 0:256], XT[s][:, kc, j * 128:(j + 1) * 128],
                                                                                  Wk[:, kc, 1536:1792], start=(kc == 0), stop=(kc == 7)),
                                      reads=[Wk, XT[s]], writes=[bank])
                            tk.op("dve", lambda e, bank=bank, j=j: e.tensor_copy(out=fst[s][:, j, :], in_=bank[:, 0:256]),
                                  reads=[bank], writes=[fst[s]])
                    c0 = t0 // 128
                    tk.dma("pool", f"p1s{s}", Vd[:, :, c0:c0 + 4, :].rearrange("h p c e -> p h c e"), vst[s][:, :, :, :],
                           reads=[vst[s]])
                    tk.dma("pool", f"p1s{s}", Fd[t0:t0 + BLK, :].rearrange("(j p) f -> p j f", p=128), fst[s][:, :, :],
                           reads=[fst[s]])
            tk.barrier()

            phase_end()
            with ExitStack() as st2:
                Wq = sb(st2, [128, 8, 1024], BF16, "Wq")
                load_w(Wq, 0, W2, D, C_Q, 512, "ldw")
                load_w(Wq, 512, W2, D, C_QS, 512, "ldw")
                lv = [sb(st2, [128, 64], F32, "lv") for _ in range(4)]
                bcast_load(lv[0], lam_q1[l, :])
                bcast_load(lv[1], lam_k1[l, :])
                bcast_load(lv[2], lam_q2[l, :])
                bcast_load(lv[3], lam_k2[l, :])
                sm = sb(st2, [128, 8], F32, "sm")
                lt = sb(st2, [128, 64], F32, "lt")
                tk.op("dve", lambda e: e.tensor_tensor(out=lt[:, :], in0=lv[0][:, :], in1=lv[1][:, :], op=ALU.mult),
                      reads=[lv[0], lv[1]], writes=[lt])
                tk.op("dve", lambda e: e.reduce_sum(out=sm[:, 0:1], in_=lt[:, :], axis=AX.X), reads=[lt], writes=[sm])
                tk.op("dve", lambda e: e.tensor_tensor(out=lt[:, :], in0=lv[2][:, :], in1=lv[3][:, :], op=ALU.mult),
                      reads=[lv[2], lv[3], sm], writes=[lt])
                tk.op("dve", lambda e: e.reduce_sum(out=sm[:, 1:2], in_=lt[:, :], axis=AX.X), reads=[lt], writes=[sm])
                tk.op("act", lambda e: e.activation(out=sm[:, 2:4], in_=sm[:, 0:2], func=AF.Exp), reads=[sm], writes=[sm])
                tk.op("dve", lambda e: e.tensor_tensor(out=sm[:, 4:5], in0=sm[:, 3:4], in1=sm[:, 2:3], op=ALU.subtract),
                      reads=[sm], writes=[sm])
                tk.op("dve", lambda e: e.tensor_scalar(out=sm[:, 5:6], in0=sm[:, 4:5], scalar1=-lambda_init, scalar2=None,
                                                       op0=ALU.add), reads=[sm], writes=[sm])
                negl = sm
                gcol = sb(st2, [128, 2], F32, "gcol")
                tk.dma("sp", "ld_const", gcol[:, 0:1], subln_g[l, :].rearrange("(p o) -> p o", o=1), writes=[gcol])
                tk.op("dve", lambda e: e.tensor_scalar(out=gcol[:, 1:2], in0=gcol[:, 0:1], scalar1=(1.0 - lambda_init),
                                                       scalar2=None, op0=ALU.mult), reads=[gcol], writes=[gcol])

                SKM = max(Sk for _, Sk in units)
                XTb = [sb(st2, [128, 8, BLK], BF16, "XTb") for _ in range(2)]
                csb = [sb(st2, [128, 2, BLK], F32, "csb") for _ in range(2)]
                QT = [sb(st2, [128, 4, 2, BLK], BF16, "QT") for _ in range(2)]
                for s_ in range(2):
                    tk.op("pool", lambda e, s_=s_: e.memset(QT[s_][:, :, :, :], 0.0), writes=[QT[s_]])
                Kh = [sb(st2, [128, SKM], BF16, "Kh") for _ in range(2)]
                Vh = [sb(st2, [128, SKM // 128, 128], BF16, "Vh") for _ in range(2)]
                NPT = 6
                pT = [sb(st2, [128, BLK], BF16, "pT") for _ in range(NPT)]
                tmpq = [sb(st2, [128, BLK], F32, "tmpq") for _ in range(2)]
                ep_r = [sb(st2, [128, BLK], F32, "ep_r") for _ in range(2)]
                ep_t = [sb(st2, [128, BLK], F32, "ep_t") for _ in range(2)]
                ep_a = [sb(st2, [128, BLK], F32, "ep_a") for _ in range(2)]
                ep_sq = [sb(st2, [128, BLK], BF16, "ep_sq") for _ in range(2)]
                ep_sd = [sb(st2, [128, BLK], F32, "ep_sd") for _ in range(2)]
                doT = [sb(st2, [128, 4, BLK], BF16, "doT") for _ in range(2)]

                jobs = []
                for (t0u, Sk, Sq) in qunits:
                    for qb in range(Sq // BLK):
                        jobs.append((t0u, Sk, t0u + qb * BLK))
                hjobs = [(ji, h) for ji in range(len(jobs)) for h in range(4)]

                def a_load_q(ji):
                    s = ji % 2
                    _, _, tq = jobs[ji]
                    tk.dma("sp", f"ax{s}", XTb[s][:, :, :], XTd.rearrange("(kc p) t -> p kc t", p=128)[:, :, tq:tq + BLK],
                           writes=[XTb[s]])
                    tk.dma("sp", f"ax{s}", csb[s][:, 0, :], ropec[:, tq:tq + BLK], writes=[csb[s]])
                    tk.dma("sp", f"ax{s}", csb[s][:, 1, :], ropes[:, tq:tq + BLK], writes=[csb[s]])

                def a_load_kv(hi):
                    ji, h = hjobs[hi]
                    t0u, Sk, _ = jobs[ji]
                    s = hi % 2
                    tk.dma("sp", f"akv{s}", Kh[s][:, 0:Sk], KTd[h * 128:(h + 1) * 128, t0u:t0u + Sk], writes=[Kh[s]])
                    c0 = t0u // 128
                    tk.dma("sp", f"akv{s}", Vh[s][:, 0:Sk // 128, :], Vd[h, :, c0:c0 + Sk // 128, :], writes=[Vh[s]])

                pending_b = []

                def epi_a(hi, O, Ssum):
                    p = hi % 2
                    for m in range(2):
                        tk.op("dve", lambda e, m=m: e.reciprocal(out=ep_r[m][:, :], in_=Ssum[m][:, :]),
                              reads=[Ssum[m]], writes=[ep_r[m]])
                        tk.op("dve", lambda e, m=m: e.tensor_tensor(out=ep_t[m][:, :], in0=O[m][:, :], in1=ep_r[m][:, :], op=ALU.mult),
                              reads=[O[m], ep_r[m]], writes=[ep_t[m]])
                    tk.op("dve", lambda e: e.scalar_tensor_tensor(out=ep_a[p][:, :], in0=ep_t[1][:, :], scalar=negl[:, 5:6],
                                                                  in1=ep_t[0][:, :], op0=ALU.mult, op1=ALU.add),
                          reads=[ep_t[0], ep_t[1], negl], writes=[ep_a[p]])
                    tk.op("act", lambda e: e.activation(out=ep_sq[p][:, :], in_=ep_a[p][:, :], func=AF.Square),
                          reads=[ep_a[p]], writes=[ep_sq[p]])

                def epi_b(hi, bank):
                    p = hi % 2
                    ji, h = hjobs[hi]
                    s = ji % 2
                    tk.op("pe", lambda e: e.matmul(bank[:, :], onesm_bf[:, :], ep_sq[p][:, :], start=True, stop=True),
                          reads=[onesm_bf, ep_sq[p]], writes=[bank])
                    tk.op("dve", lambda e: e.tensor_scalar(out=ep_sd[p][:, :], in0=bank[:, :], scalar1=RMS_EPS, scalar2=None,
                                                           op0=ALU.add), reads=[bank], writes=[ep_sd[p]])
                    tk.op("act", lambda e: e.activation(out=ep_sd[p][:, :], in_=ep_sd[p][:, :], func=AF.Sqrt),
                          reads=[ep_sd[p]], writes=[ep_sd[p]])
                    tk.op("dve", lambda e: e.reciprocal(out=ep_sd[p][:, :], in_=ep_sd[p][:, :]), reads=[ep_sd[p]], writes=[ep_sd[p]])
                    tk.op("dve", lambda e: e.tensor_tensor(out=ep_a[p][:, :], in0=ep_a[p][:, :], in1=ep_sd[p][:, :], op=ALU.mult),
                          reads=[ep_a[p], ep_sd[p]], writes=[ep_a[p]])
                    tk.op("dve", lambda e: e.tensor_scalar(out=doT[s][:, h, :], in0=ep_a[p][:, :], scalar1=gcol[:, 1:2], scalar2=None,
                                                           op0=ALU.mult), reads=[ep_a[p], gcol], writes=[doT[s]])
                    if h == 3:
                        _, _, tq = jobs[ji]
                        tk.dma("pool", f"ast{s}", DOd.rearrange("(h p) t -> p h t", p=128)[:, :, tq:tq + BLK], doT[s][:, :, :],
                               reads=[doT[s]])

                a_load_q(0)
                a_load_kv(0)
                for hi, (ji, h) in enumerate(hjobs):
                    t0u, Sk, tq = jobs[ji]
                    s = ji % 2
                    ks = hi % 2
                    if h == 0:
                        if ji + 1 < len(jobs):
                            a_load_q(ji + 1)
                        for hh in range(4):
                            A = PS[4 + 2 * (hh % 2)]
                            B = PS[5 + 2 * (hh % 2)]
                            for kc in range(8):
                                tk.op("pe", lambda e, kc=kc, A=A, hh=hh: e.matmul(A[:, :], Wq[:, kc, hh * 128:(hh + 1) * 128], XTb[s][:, kc, :],
                                                                                start=(kc == 0), stop=(kc == 7)),
                                      reads=[Wq, XTb[s]], writes=[A])
                            for kc in range(8):
                                tk.op("pe", lambda e, kc=kc, B=B, hh=hh: e.matmul(B[:, :], Wq[:, kc, 512 + hh * 128:512 + (hh + 1) * 128], XTb[s][:, kc, :],
                                                                                start=(kc == 0), stop=(kc == 7)),
                                      reads=[Wq, XTb[s]], writes=[B])
                            tk.op("dve", lambda e, A=A: e.tensor_tensor(out=tmpq[0][:, :], in0=A[:, :], in1=csb[s][:, 0, :], op=ALU.mult),
                                  reads=[A, csb[s]], writes=[tmpq[0]])
                            tk.op("dve", lambda e, B=B: e.tensor_tensor(out=tmpq[1][:, :], in0=B[:, :], in1=csb[s][:, 1, :], op=ALU.mult),
                                  reads=[B, csb[s]], writes=[tmpq[1]])
                            for m_ in range(2):
                                tk.op("pool", lambda e, hh=hh, m_=m_: e.tensor_tensor(
                                    out=QT[s][m_ * 64:(m_ + 1) * 64, hh, m_, :], in0=tmpq[0][m_ * 64:(m_ + 1) * 64, :],
                                    in1=tmpq[1][m_ * 64:(m_ + 1) * 64, :], op=ALU.add),
                                    reads=[tmpq[0], tmpq[1]], writes=[QT[s]])
                    if hi + 1 < len(hjobs):
                        a_load_kv(hi + 1)
                    O = [PS[0], PS[1]]
                    Ssum = [PS[2], PS[3]]
                    nkc = Sk // 128
                    steps = [(kc, m) for kc in range(nkc) for m in range(2)]
                    LA = 2

                    def qk(i):
                        kc, m = steps[i]
                        sc = PS[4 + i % 4]
                        tk.op("pe", lambda e: e.matmul(sc[:, :], Kh[ks][:, kc * 128:(kc + 1) * 128],
                                                       QT[s][:, h, m, :], start=True, stop=True),
                              reads=[Kh[ks], QT[s]], writes=[sc])
                        pt = pT[i % NPT]
                        tk.op("act", lambda e: e.activation(out=pt[:, :], in_=sc[:, :], func=AF.Exp, scale=0.125),
                              reads=[sc], writes=[pt])

                    for i in range(min(LA, len(steps))):
                        qk(i)
                    for i, (kc, m) in enumerate(steps):
                        if i + LA < len(steps):
                            qk(i + LA)
                        pt = pT[i % NPT]
                        tk.op("pe", lambda e: e.matmul(O[m][:, :], Vh[ks][:, kc, :], pt[:, :], start=(kc == 0), stop=(kc == nkc - 1)),
                              reads=[Vh[ks], pt], writes=[O[m]])
                        tk.op("pe", lambda e: e.matmul(Ssum[m][:, :], ones_bf[:, :], pt[:, :], start=(kc == 0), stop=(kc == nkc - 1)),
                              reads=[ones_bf, pt], writes=[Ssum[m]])
                        if i == 8 and pending_b:
                            pending_b.pop(0)()
                    while pending_b:
                        pending_b.pop(0)()
                    epi_a(hi, O, Ssum)
                    pending_b.append(lambda hi=hi: epi_b(hi, PS[4 + (hi % 2)]))
                while pending_b:
                    pending_b.pop(0)()
            tk.barrier()

            phase_end()
            with ExitStack() as st3:
                GRP = 4
                Fg = [sb(st3, [128, GRP, 256], BF16, "Fg") for _ in range(2)]
                Cg = [sb(st3, [128, GRP, BLK], BF16, "Cg") for _ in range(2)]
                Sg = [sb(st3, [128, GRP, BLK], BF16, "Sg") for _ in range(2)]
                cfs = [sb(st3, [128, 4, BLK], BF16, "cfs") for _ in range(2)]
                foT = [sb(st3, [128, 2, BLK], BF16, "foT") for _ in range(2)]
                gj = []
                bjobs = []
                for (t0u, Sk, Sq) in qunits:
                    for qb in range(Sq // BLK):
                        bjobs.append((t0u, Sk, qb))
                for bi_, (t0u, Sk, qb) in enumerate(bjobs):
                    for g in range(Sk // (128 * GRP)):
                        gj.append((bi_, t0u, Sk, qb, g))

                def f_load(gi):
                    bi_, t0u, Sk, qb, g = gj[gi]
                    s = gi % 2
                    r0 = g * 128 * GRP
                    Cm, Sm = (dft_cs, dft_ss) if Sk == SS and t0u < 2 * SS else (dft_cp, dft_sp)
                    tk.dma("sp", f"fl{s}", Fg[s][:, :, :], Fd[t0u + r0:t0u + r0 + 128 * GRP, :].rearrange("(c p) f -> p c f", p=128),
                           writes=[Fg[s]])
                    tk.dma("sp", f"fl{s}", Cg[s][:, :, :], Cm[r0:r0 + 128 * GRP, qb * BLK:(qb + 1) * BLK].rearrange("(c p) n -> p c n", p=128),
                           writes=[Cg[s]])
                    tk.dma("sp", f"fl{s}", Sg[s][:, :, :], Sm[r0:r0 + 128 * GRP, qb * BLK:(qb + 1) * BLK].rearrange("(c p) n -> p c n", p=128),
                           writes=[Sg[s]])

                f_load(0)
                for gi, (bi_, t0u, Sk, qb, g) in enumerate(gj):
                    s = gi % 2
                    if gi + 1 < len(gj):
                        f_load(gi + 1)
                    ng = Sk // (128 * GRP)
                    for c in range(GRP):
                        first = (g == 0 and c == 0)
                        last = (g == ng - 1 and c == GRP - 1)
                        for a in range(4):
                            mat = Cg[s] if a < 2 else Sg[s]
                            tk.op("pe", lambda e, a=a, c=c, mat=mat: e.matmul(PS[a][:, :], Fg[s][:, c, (a % 2) * 128:(a % 2 + 1) * 128], mat[:, c, :],
                                                                          start=first, stop=last),
                                  reads=[Fg[s], mat], writes=[PS[a]])
                    if g == ng - 1:
                        bs = bi_ % 2
                        tq = t0u + qb * BLK
                        for a in range(4):
                            if a % 2 == 0:
                                tk.op("act", lambda e, a=a: e.activation(out=cfs[bs][:, a, :], in_=PS[a][:, :], func=AF.Copy),
                                      reads=[PS[a]], writes=[cfs[bs]])
                            else:
                                tk.op("dve", lambda e, a=a: e.tensor_copy(out=cfs[bs][:, a, :], in_=PS[a][:, :]),
                                      reads=[PS[a]], writes=[cfs[bs]])
                        for cc in range(2):
                            bank = PS[4 + cc + 2 * (bi_ % 2)]
                            tk.op("pe", lambda e, cc=cc, bank=bank: e.matmul(bank[:, :], bdc[:, :], cfs[bs][:, cc, :], start=True, stop=False),
                                  reads=[bdc, cfs[bs]], writes=[bank])
                            tk.op("pe", lambda e, cc=cc, bank=bank: e.matmul(bank[:, :], bdsn[:, :], cfs[bs][:, 2 + cc, :], start=False, stop=True),
                                  reads=[bdsn, cfs[bs]], writes=[bank])
                            if cc == 0:
                                tk.op("act", lambda e, cc=cc, bank=bank: e.activation(out=foT[bs][:, cc, :], in_=bank[:, :], func=AF.Copy),
                                      reads=[bank], writes=[foT[bs]])
                            else:
                                tk.op("dve", lambda e, cc=cc, bank=bank: e.tensor_copy(out=foT[bs][:, cc, :], in_=bank[:, :]),
                                      reads=[bank], writes=[foT[bs]])
                        tk.dma("pool", f"fst{bs}", FOd.rearrange("(c p) t -> p c t", p=128)[:, :, tq:tq + BLK], foT[bs][:, :, :],
                               reads=[foT[bs]])
            tk.barrier()

            phase_end()
            with ExitStack() as st4:
                Wuv = sb(st4, [128, 8, 512], BF16, "Wuv")
                load_w(Wuv, 0, W2, D, C_U, 512, "ldw")
                swT = sb(st4, [128, 4, 128], BF16, "swT")
                for h in range(4):
                    tk.dma("pool", "ldw", swT[:, h, :], sgu_wT[l, h, :, :], writes=[swT])
                btab = sb(st4, [128, 256], F32, "btab")
                tk.dma("sp", "ld_const", btab[:, :], sgu_bt[l, :, :], writes=[btab])
                gvn = sb(st4, [128, 256], F32, "gvn")
                bvn = sb(st4, [128, 256], F32, "bvn")
                bcast_load(gvn, vn_g[l, :])
                bcast_load(bvn, vn_b[l, :])
                XTc = [sb(st4, [128, 8, BLK], BF16, "XTc") for _ in range(2)]
                ub = [sb(st4, [128, 256], F32, "ub") for _ in range(2)]
                vn0 = [sb(st4, [128, 256], F32, "vn0") for _ in range(2)]
                vnb = [sb(st4, [128, 256], BF16, "vnb") for _ in range(2)]
                junk = [sb(st4, [128, 256], F32, "junk") for _ in range(2)]
                so = [sb(st4, [128, 256], F32, "so") for _ in range(2)]
                sst = [sb(st4, [128, 8], F32, "sst") for _ in range(2)]
                soT = [sb(st4, [128, 2, BLK], BF16, "soT") for _ in range(2)]
                cjobs = []
                for (t0u, Sk, Sq) in qunits:
                    for qb in range(Sq // BLK):
                        cjobs.append(t0u + qb * BLK)

                def c_load(bi_):
                    s = bi_ % 2
                    tq = cjobs[bi_]
                    tk.dma("sp", f"cx{s}", XTc[s][:, :, :], XTd.rearrange("(kc p) t -> p kc t", p=128)[:, :, tq:tq + BLK],
                           writes=[XTc[s]])

                c_load(0)
                for bi_, tq in enumerate(cjobs):
                    s = bi_ % 2
                    if bi_ + 1 < len(cjobs):
                        c_load(bi_ + 1)
                    for j in range(4):
                        p = j % 2
                        bank = PS[p]
                        for kc in range(8):
                            tk.op("pe", lambda e, kc=kc, bank=bank, j=j: e.matmul(bank[:, :], XTc[s][:, kc, j * 128:(j + 1) * 128], Wuv[:, kc, :],
                                                                              start=(kc == 0), stop=(kc == 7)),
                                  reads=[XTc[s], Wuv], writes=[bank])
                        tk.op("act", lambda e, bank=bank, p=p: e.activation(out=junk[p][:, :], in_=bank[:, 256:512], func=AF.Copy,
                                                                            accum_out=sst[p][:, 0:1]),
                              reads=[bank], writes=[junk[p], sst[p]])
                        tk.op("act", lambda e, bank=bank, p=p: e.activation(out=junk[p][:, :], in_=bank[:, 256:512], func=AF.Square,
                                                                            accum_out=sst[p][:, 1:2]),
                              reads=[bank], writes=[junk[p], sst[p]])
                        tk.op("act", lambda e, bank=bank, p=p: e.activation(out=ub[p][:, :], in_=bank[:, 0:256], func=AF.Copy),
                              reads=[bank], writes=[ub[p]])
                        tk.op("dve", lambda e, p=p: e.tensor_scalar(out=sst[p][:, 2:4], in0=sst[p][:, 0:2], scalar1=1.0 / 256.0, scalar2=None,
                                                                    op0=ALU.mult), reads=[sst[p]], writes=[sst[p]])
                        tk.op("dve", lambda e, p=p: e.tensor_tensor(out=sst[p][:, 4:5], in0=sst[p][:, 2:3], in1=sst[p][:, 2:3], op=ALU.mult),
                              reads=[sst[p]], writes=[sst[p]])
                        tk.op("dve", lambda e, p=p: e.tensor_tensor(out=sst[p][:, 5:6], in0=sst[p][:, 3:4], in1=sst[p][:, 4:5], op=ALU.subtract),
                              reads=[sst[p]], writes=[sst[p]])
                        tk.op("dve", lambda e, p=p: e.tensor_scalar(out=sst[p][:, 6:7], in0=sst[p][:, 5:6], scalar1=LN_EPS, scalar2=None,
                                                                    op0=ALU.add), reads=[sst[p]], writes=[sst[p]])
                        tk.op("act", lambda e, p=p: e.activation(out=sst[p][:, 7:8], in_=sst[p][:, 6:7], func=AF.Sqrt),
                              reads=[sst[p]], writes=[sst[p]])
                        tk.op("dve", lambda e, p=p: e.reciprocal(out=sst[p][:, 6:7], in_=sst[p][:, 7:8]), reads=[sst[p]], writes=[sst[p]])
                        tk.op("dve", lambda e, bank=bank, p=p: e.tensor_scalar(out=vn0[p][:, :], in0=bank[:, 256:512], scalar1=sst[p][:, 2:3],
                                                                               scalar2=sst[p][:, 6:7], op0=ALU.subtract, op1=ALU.mult),
                              reads=[bank, sst[p]], writes=[vn0[p]])
                        tk.op("pool", lambda e, p=p: e.tensor_tensor(out=vn0[p][:, :], in0=vn0[p][:, :], in1=gvn[:, :], op=ALU.mult),
                              reads=[vn0[p], gvn], writes=[vn0[p]])
                        tk.op("pool", lambda e, p=p: e.tensor_tensor(out=vnb[p][:, :], in0=vn0[p][:, :], in1=bvn[:, :], op=ALU.add),
                              reads=[vn0[p], bvn], writes=[vnb[p]])
                        bank2 = PS[2 + p]
                        for h in range(4):
                            tk.op("pe", lambda e, h=h, bank2=bank2, p=p: e.matmul(bank2[:, h * 64:(h + 1) * 64], swT[:, h, :], vnb[p][:, h * 64:(h + 1) * 64],
                                                                              start=True, stop=True),
                                  reads=[swT, vnb[p]], writes=[bank2])
                        tk.op("dve", lambda e, bank2=bank2, p=p: e.tensor_tensor(out=so[p][:, :], in0=bank2[:, 0:256], in1=btab[:, :], op=ALU.add),
                              reads=[bank2, btab], writes=[so[p]])
                        tk.op("dve", lambda e, p=p: e.tensor_tensor(out=so[p][:, :], in0=so[p][:, :], in1=ub[p][:, :], op=ALU.mult),
                              reads=[so[p], ub[p]], writes=[so[p]])
                        for cc in range(2):
                            bank3 = PS[4 + cc + 2 * (bi_ % 2)]
                            tk.op("pe", lambda e, cc=cc, bank3=bank3, p=p, j=j: e.transpose(bank3[:, j * 128:(j + 1) * 128], so[p][:, cc * 128:(cc + 1) * 128],
                                                                                        ident[:, :]),
                                  reads=[so[p], ident], writes=[bank3])
                    for cc in range(2):
                        bank3 = PS[4 + cc + 2 * (bi_ % 2)]
                        if cc == 0:
                            tk.op("act", lambda e, cc=cc, bank3=bank3: e.activation(out=soT[s][:, cc, :], in_=bank3[:, :], func=AF.Copy),
                                  reads=[bank3], writes=[soT[s]])
                        else:
                            tk.op("dve", lambda e, cc=cc, bank3=bank3: e.tensor_copy(out=soT[s][:, cc, :], in_=bank3[:, :]),
                                  reads=[bank3], writes=[soT[s]])
                    tk.dma("pool", f"cst{s}", SOd.rearrange("(c p) t -> p c t", p=128)[:, :, tq:tq + BLK], soT[s][:, :, :],
                           reads=[soT[s]])
            tk.barrier()

            phase_end()
            with ExitStack() as st5:
                Wg = sb(st5, [128, 8, 3072], BF16, "Wg")
                load_w(Wg, 0, W2, D, C_G, 3072, "ldw")
                Wfo = sb(st5, [128, 2, D], BF16, "Wfo")
                load_w(Wfo, 0, w_fourier[l], 256, 0, D, "ldw")
                Wsg = sb(st5, [128, 2, D], BF16, "Wsg")
                load_w(Wsg, 0, w_sgu[l], 256, 0, D, "ldw")
                Wdf = sb(st5, [128, 4, D], BF16, "Wdf")
                load_w(Wdf, 0, w_diff[l], 512, 0, D, "ldw")
                XTe = [sb(st5, [128, 8, BLK], BF16, "XTe") for _ in range(2)]
                dob = [sb(st5, [128, 4, BLK], BF16, "dob") for _ in range(2)]
                sob = [sb(st5, [128, 2, BLK], BF16, "sob") for _ in range(2)]
                fob = [sb(st5, [128, 2, BLK], BF16, "fob") for _ in range(2)]
                sgt = [[sb(st5, [128, BLK], F32, "sgt") for _ in range(3)] for _ in range(2)]
                mt = [sb(st5, [128, BLK], F32, "mt") for _ in range(2)]
                mT = [sb(st5, [128, 8, BLK], BF16, "mT") for _ in range(2)]
                djobs = list(cjobs)

                def d_load(bi_):
                    s = bi_ % 2
                    tq = djobs[bi_]
                    tk.dma("sp", f"dx{s}", XTe[s][:, :, :], XTd.rearrange("(kc p) t -> p kc t", p=128)[:, :, tq:tq + BLK], writes=[XTe[s]])
                    tk.dma("sp", f"dx{s}", dob[s][:, :, :], DOd.rearrange("(h p) t -> p h t", p=128)[:, :, tq:tq + BLK], writes=[dob[s]])
                    tk.dma("sp", f"dx{s}", sob[s][:, :, :], SOd.rearrange("(c p) t -> p c t", p=128)[:, :, tq:tq + BLK], writes=[sob[s]])
                    tk.dma("sp", f"dx{s}", fob[s][:, :, :], FOd.rearrange("(c p) t -> p c t", p=128)[:, :, tq:tq + BLK], writes=[fob[s]])

                d_load(0)
                for bi_, tq in enumerate(djobs):
                    s = bi_ % 2
                    if bi_ + 1 < len(djobs):
                        d_load(bi_ + 1)
                    for n in range(8):
                        p = n % 2
                        for b in range(3):
                            bank = PS[b]
                            for kc in range(8):
                                tk.op("pe", lambda e, kc=kc, b=b, bank=bank, n=n: e.matmul(
                                    bank[:, :], Wg[:, kc, b * D + n * 128:b * D + (n + 1) * 128], XTe[s][:, kc, :],
                                    start=(kc == 0), stop=(kc == 7)), reads=[Wg, XTe[s]], writes=[bank])
                            tk.op("act", lambda e, b=b, bank=bank, p=p: e.activation(out=sgt[p][b][:, :], in_=bank[:, :], func=AF.Sigmoid),
                                  reads=[bank], writes=[sgt[p][b]])
                        brs = [(Wfo, fob[s], 2), (Wsg, sob[s], 2), (Wdf, dob[s], 4)]
                        for b, (Wb, xb_, nkc) in enumerate(brs):
                            bank = PS[3 + b]
                            for kc in range(nkc):
                                tk.op("pe", lambda e, kc=kc, bank=bank, Wb=Wb, xb_=xb_, nkc=nkc, n=n: e.matmul(
                                    bank[:, :], Wb[:, kc, n * 128:(n + 1) * 128], xb_[:, kc, :], start=(kc == 0), stop=(kc == nkc - 1)),
                                    reads=[Wb, xb_], writes=[bank])
                        tk.op("dve", lambda e, p=p: e.tensor_tensor(out=mt[0][:, :], in0=PS[3][:, :], in1=sgt[p][0][:, :], op=ALU.mult),
                              reads=[PS[3], sgt[p][0]], writes=[mt[0]])
                        tk.op("dve", lambda e, p=p: e.tensor_tensor(out=mt[1][:, :], in0=PS[4][:, :], in1=sgt[p][1][:, :], op=ALU.mult),
                              reads=[PS[4], sgt[p][1]], writes=[mt[1]])
                        tk.op("pool", lambda e: e.tensor_tensor(out=mt[0][:, :], in0=mt[0][:, :], in1=mt[1][:, :], op=ALU.add),
                              reads=[mt[0], mt[1]], writes=[mt[0]])
                        tk.op("dve", lambda e, p=p: e.tensor_tensor(out=mt[1][:, :], in0=PS[5][:, :], in1=sgt[p][2][:, :], op=ALU.mult),
                              reads=[PS[5], sgt[p][2]], writes=[mt[1]])
                        tk.op("pool", lambda e, n=n: e.tensor_tensor(out=mT[s][:, n, :], in0=mt[0][:, :], in1=mt[1][:, :], op=ALU.add),
                              reads=[mt[0], mt[1]], writes=[mT[s]])
                    tk.dma("pool", f"dst{s}", MTd.rearrange("(kc p) t -> p kc t", p=128)[:, :, tq:tq + BLK], mT[s][:, :, :], reads=[mT[s]])
            tk.barrier()

            phase_end()
            with ExitStack() as st6:
                Wo = sb(st6, [128, 8, D], BF16, "Wo")
                load_w(Wo, 0, w_out[l], D, 0, D, "ldw")
                g1 = sb(st6, [128, D], F32, "g1")
                b1 = sb(st6, [128, D], F32, "b1")
                bcast_load(g1, ln1_g[l, :])
                bcast_load(b1, ln1_b[l, :])
                moe = (l % 2 == 1)
                if moe:
                    Wr = sb(st6, [128, 8, NE], F32, "Wr")
                    tk.dma("sp", "ld_const", Wr[:, :, :], w_router[l // 2].rearrange("(kc p) e -> p kc e", p=128), writes=[Wr])
                MTb = [sb(st6, [128, 8, BLK], BF16, "MTb") for _ in range(2)]
                xsb = [sb(st6, [128, 4, D], F32, "xsb") for _ in range(2)]
                yb = [sb(st6, [128, D], F32, "yb") for _ in range(2)]
                x1s = [sb(st6, [128, D], F32, "x1s") for _ in range(2)]
                stt = [sb(st6, [128, 12], F32, "stt") for _ in range(2)]
                mvt = [sb(st6, [128, 4], F32, "mvt") for _ in range(2)]
                x1T32 = [sb(st6, [128, 8, 128], F32, "x1T32") for _ in range(2)]
                x1Tb = [sb(st6, [128, 8, BLK], BF16, "x1Tb") for _ in range(2)]
                rt = [sb(st6, [128, 64], F32, "rt") for _ in range(2)]
                ejobs = list(cjobs)

                def e_load(bi_):
                    s = bi_ % 2
                    tq = ejobs[bi_]
                    tk.dma("sp", f"ex{s}", MTb[s][:, :, :], MTd.rearrange("(kc p) t -> p kc t", p=128)[:, :, tq:tq + BLK], writes=[MTb[s]])
                    tk.dma("sp", f"ex{s}", xsb[s][:, :, :], Xsrc[tq:tq + BLK, :].rearrange("(j p) d -> p j d", p=128), writes=[xsb[s]])

                e_load(0)
                for bi_, tq in enumerate(ejobs):
                    s = bi_ % 2
                    if bi_ + 1 < len(ejobs):
                        e_load(bi_ + 1)
                    for j in range(4):
                        p = j % 2
                        for nh in range(2):
                            bank = PS[(2 * j + nh) % 4]
                            for kc in range(8):
                                tk.op("pe", lambda e, kc=kc, bank=bank, nh=nh, j=j: e.matmul(
                                    bank[:, :], MTb[s][:, kc, j * 128:(j + 1) * 128], Wo[:, kc, nh * 512:(nh + 1) * 512],
                                    start=(kc == 0), stop=(kc == 7)), reads=[MTb[s], Wo], writes=[bank])
                            tk.op("dve", lambda e, bank=bank, nh=nh, j=j, p=p: e.scalar_tensor_tensor(
                                out=yb[p][:, nh * 512:(nh + 1) * 512], in0=xsb[s][:, j, nh * 512:(nh + 1) * 512], scalar=ALPHA,
                                in1=bank[:, :], op0=ALU.mult, op1=ALU.add), reads=[xsb[s], bank], writes=[yb[p]])
                        layer_norm_tile((stt[p], mvt[p]), yb[p], g1, b1, x1s[p])
                        tk.dma("pool", f"est{p}", X1d[tq + j * 128:tq + (j + 1) * 128, :], x1s[p][:, :], reads=[x1s[p]])
                        for g in range(2):
                            bank = PS[4 + g]
                            for k4 in range(4):
                                kc = g * 4 + k4
                                tk.op("pe", lambda e, bank=bank, k4=k4, kc=kc, p=p: e.transpose(
                                    bank[:, k4 * 128:(k4 + 1) * 128], x1s[p][:, kc * 128:(kc + 1) * 128], ident[:, :]),
                                    reads=[x1s[p], ident], writes=[bank])
                            if not moe:
                                tk.op("act", lambda e, bank=bank, g=g, j=j: e.activation(
                                    out=x1Tb[s][:, g * 4:(g + 1) * 4, j * 128:(j + 1) * 128],
                                    in_=bank[:, :].rearrange("p (k t) -> p k t", k=4), func=AF.Copy),
                                    reads=[bank], writes=[x1Tb[s]])
                            else:
                                tk.op("act", lambda e, bank=bank, g=g, p=p: e.activation(
                                    out=x1T32[p][:, g * 4:(g + 1) * 4, :],
                                    in_=bank[:, :].rearrange("p (k t) -> p k t", k=4), func=AF.Copy),
                                    reads=[bank], writes=[x1T32[p]])
                                tk.op("pool", lambda e, g=g, j=j, p=p: e.tensor_copy(
                                    out=x1Tb[s][:, g * 4:(g + 1) * 4, j * 128:(j + 1) * 128],
                                    in_=x1T32[p][:, g * 4:(g + 1) * 4, :]),
                                    reads=[x1T32[p]], writes=[x1Tb[s]])
                        if moe:
                            bank = PS[6 + p]
                            for kc in range(8):
                                tk.op("pe", lambda e, kc=kc, bank=bank, p=p: e.matmul(bank[:, 0:NE], x1T32[p][:, kc, :], Wr[:, kc, :],
                                                                                  start=(kc == 0), stop=(kc == 7)),
                                      reads=[x1T32[p], Wr], writes=[bank])
                            r = rt[p]
                            tk.op("dve", lambda e, bank=bank, r=r: e.tensor_copy(out=r[:, 0:8], in_=bank[:, 0:NE]), reads=[bank], writes=[r])
                            tk.op("dve", lambda e, r=r: e.reduce_max(out=r[:, 32:33], in_=r[:, 0:8], axis=AX.X), reads=[r], writes=[r])
                            tk.op("dve", lambda e, r=r: e.tensor_scalar(out=r[:, 8:16], in0=r[:, 0:8], scalar1=r[:, 32:33], scalar2=None,
                                                                        op0=ALU.is_equal), reads=[r], writes=[r])
                            tk.op("dve", lambda e, r=r: e.scalar_tensor_tensor(out=r[:, 16:24], in0=r[:, 8:16], scalar=-1e30, in1=r[:, 0:8],
                                                                               op0=ALU.mult, op1=ALU.add), reads=[r], writes=[r])
                            tk.op("dve", lambda e, r=r: e.reduce_max(out=r[:, 33:34], in_=r[:, 16:24], axis=AX.X), reads=[r], writes=[r])
                            tk.op("dve", lambda e, r=r: e.tensor_scalar(out=r[:, 24:32], in0=r[:, 16:24], scalar1=r[:, 33:34], scalar2=None,
                                                                        op0=ALU.is_equal), reads=[r], writes=[r])
                            tk.op("dve", lambda e, r=r: e.tensor_tensor(out=r[:, 34:35], in0=r[:, 33:34], in1=r[:, 32:33], op=ALU.subtract),
                                  reads=[r], writes=[r])
                            tk.op("act", lambda e, r=r: e.activation(out=r[:, 35:36], in_=r[:, 34:35], func=AF.Sigmoid), reads=[r], writes=[r])
                            tk.op("dve", lambda e, r=r: e.tensor_scalar(out=r[:, 36:37], in0=r[:, 35:36], scalar1=-1.0, scalar2=1.0,
                                                                        op0=ALU.mult, op1=ALU.add), reads=[r], writes=[r])
                            tk.op("dve", lambda e, r=r: e.tensor_scalar(out=r[:, 48:56], in0=r[:, 8:16], scalar1=r[:, 36:37], scalar2=None,
                                                                        op0=ALU.mult), reads=[r], writes=[r])
                            tk.op("dve", lambda e, r=r: e.scalar_tensor_tensor(out=r[:, 40:48], in0=r[:, 24:32], scalar=r[:, 35:36], in1=r[:, 48:56],
                                                                               op0=ALU.mult, op1=ALU.add), reads=[r], writes=[r])
                            tk.dma("pool", f"est{p}", CMBd[tq + j * 128:tq + (j + 1) * 128, :], r[:, 40:48], reads=[r])
                    tk.dma("pool", f"est2{s}", X1Td.rearrange("(kc p) t -> p kc t", p=128)[:, :, tq:tq + BLK], x1Tb[s][:, :, :], reads=[x1Tb[s]])
            tk.barrier()

            phase_end()
            moe = (l % 2 == 1)
            if moe:
                passes = [(moe_w_gate[l // 2, e_], moe_w_up[l // 2, e_], moe_w_down[l // 2, e_], e_) for e_ in range(NE)]
            else:
                passes = [(ffn_w_gate[l // 2], ffn_w_up[l // 2], ffn_w_down[l // 2], None)]
            with ExitStack() as st7:
                Wgs = sb(st7, [128, 8, DFF], BF16, "Wgs")
                Wus = sb(st7, [128, 8, DFF], BF16, "Wus")
                Wds = sb(st7, [128, NFC, D], BF16, "Wds")
                XTf = [sb(st7, [128, 8, BLK], BF16, "XTf") for _ in range(2)]
                cmbb = [sb(st7, [128, 4, NE], F32, "cmbb") for _ in range(2)]
                hT = sb(st7, [128, NFC, BLK], BF16, "hT")
                sgf = [sb(st7, [128, BLK], F32, "sgf") for _ in range(2)]
                ysb = [sb(st7, [128, D], F32, "ysb") for _ in range(2)]
                fjobs = list(cjobs)
                for (wg_ap, wu_ap, wd_ap, ex) in passes:
                    load_w(Wgs, 0, wg_ap, D, 0, DFF, "ldw")
                    load_w(Wus, 0, wu_ap, D, 0, DFF, "ldw")
                    load_w(Wds, 0, wd_ap, DFF, 0, D, "ldw")

                    def g_load(bi_):
                        s = bi_ % 2
                        tq = fjobs[bi_]
                        tk.dma("sp", f"gx{s}", XTf[s][:, :, :], X1Td.rearrange("(kc p) t -> p kc t", p=128)[:, :, tq:tq + BLK], writes=[XTf[s]])
                        if ex is not None:
                            tk.dma("sp", f"gx{s}", cmbb[s][:, :, :], CMBd[tq:tq + BLK, :].rearrange("(j p) e -> p j e", p=128), writes=[cmbb[s]])

                    g_load(0)
                    for bi_, tq in enumerate(fjobs):
                        s = bi_ % 2
                        if bi_ + 1 < len(fjobs):
                            g_load(bi_ + 1)
                        for c in range(NFC):
                            rows = min(128, DFF - c * 128)
                            bg = PS[(2 * c) % 4]
                            bu = PS[(2 * c + 1) % 4]
                            for kc in range(8):
                                tk.op("pe", lambda e, kc=kc, bg=bg, c=c, rows=rows: e.matmul(bg[0:rows, :], Wgs[:, kc, c * 128:c * 128 + rows], XTf[s][:, kc, :],
                                                                                         start=(kc == 0), stop=(kc == 7)),
                                      reads=[Wgs, XTf[s]], writes=[bg])
                            for kc in range(8):
                                tk.op("pe", lambda e, kc=kc, bu=bu, c=c, rows=rows: e.matmul(bu[0:rows, :], Wus[:, kc, c * 128:c * 128 + rows], XTf[s][:, kc, :],
                                                                                         start=(kc == 0), stop=(kc == 7)),
                                      reads=[Wus, XTf[s]], writes=[bu])
                            sg_ = sgf[c % 2]
                            tk.op("act", lambda e, bg=bg, sg_=sg_, rows=rows: e.activation(out=sg_[0:rows, :], in_=bg[0:rows, :], func=AF.Silu),
                                  reads=[bg], writes=[sg_])
                            tk.op("dve", lambda e, bu=bu, sg_=sg_, rows=rows, c=c: e.tensor_tensor(out=hT[0:rows, c, :], in0=bu[0:rows, :], in1=sg_[0:rows, :],
                                                                                               op=ALU.mult),
                                  reads=[bu, sg_], writes=[hT])
                        for j in range(4):
                            p = j % 2
                            for nh in range(2):
                                bank = PS[4 + (2 * j + nh) % 4]
                                for c in range(NFC):
                                    rows = min(128, DFF - c * 128)
                                    tk.op("pe", lambda e, c=c, bank=bank, rows=rows, nh=nh, j=j: e.matmul(
                                        bank[:, :], hT[0:rows, c, j * 128:(j + 1) * 128], Wds[0:rows, c, nh * 512:(nh + 1) * 512],
                                        start=(c == 0), stop=(c == NFC - 1)), reads=[hT, Wds], writes=[bank])
                                if ex is None:
                                    tk.op("act", lambda e, bank=bank, nh=nh, p=p: e.activation(out=ysb[p][:, nh * 512:(nh + 1) * 512], in_=bank[:, :],
                                                                                           func=AF.Copy), reads=[bank], writes=[ysb[p]])
                                else:
                                    tk.op("act", lambda e, bank=bank, nh=nh, p=p, j=j: e.activation(
                                        out=ysb[p][:, nh * 512:(nh + 1) * 512], in_=bank[:, :], func=AF.Copy, scale=cmbb[s][:, j, ex:ex + 1]),
                                        reads=[bank, cmbb[s]], writes=[ysb[p]])
                            ydst = Yd[tq + j * 128:tq + (j + 1) * 128, :]
                            if ex is None or ex == 0:
                                tk.dma("pool", f"gst{p}", ydst, ysb[p][:, :], reads=[ysb[p]])
                            else:
                                tk.dma("pool", f"gst{p}", ydst, ysb[p][:, :], reads=[ysb[p]], accum_op=ALU.add)
                    tk.barrier()

            phase_end()
            with ExitStack() as st8:
                g2 = sb(st8, [128, D], F32, "g2")
                b2 = sb(st8, [128, D], F32, "b2")
                bcast_load(g2, ln2_g[l, :])
                bcast_load(b2, ln2_b[l, :])
                xa = [sb(st8, [128, 4, D], F32, "xa") for _ in range(2)]
                ya = [sb(st8, [128, 4, D], F32, "ya") for _ in range(2)]
                tb_ = [sb(st8, [128, D], F32, "tb") for _ in range(2)]
                ob = [sb(st8, [128, D], F32, "ob") for _ in range(2)]
                stt2 = [sb(st8, [128, 12], F32, "stt2") for _ in range(2)]
                mvt2 = [sb(st8, [128, 4], F32, "mvt2") for _ in range(2)]
                hjobs4 = list(cjobs)
                dest = Xmid if l + 1 < NLAYER else out

                def h_load(bi_):
                    s = bi_ % 2
                    tq = hjobs4[bi_]
                    tk.dma("sp", f"hx{s}", xa[s][:, :, :], X1d[tq:tq + BLK, :].rearrange("(j p) d -> p j d", p=128), writes=[xa[s]])
                    tk.dma("sp", f"hx{s}", ya[s][:, :, :], Yd[tq:tq + BLK, :].rearrange("(j p) d -> p j d", p=128), writes=[ya[s]])

                h_load(0)
                for bi_, tq in enumerate(hjobs4):
                    s = bi_ % 2
                    if bi_ + 1 < len(hjobs4):
                        h_load(bi_ + 1)
                    for j in range(4):
                        p = j % 2
                        tk.op("dve", lambda e, j=j, p=p: e.scalar_tensor_tensor(out=tb_[p][:, :], in0=xa[s][:, j, :], scalar=ALPHA, in1=ya[s][:, j, :],
                                                                            op0=ALU.mult, op1=ALU.add), reads=[xa[s], ya[s]], writes=[tb_[p]])
                        layer_norm_tile((stt2[p], mvt2[p]), tb_[p], g2, b2, ob[p])
                        tk.dma("pool", f"hst{p}", dest[tq + j * 128:tq + (j + 1) * 128, :], ob[p][:, :], reads=[ob[p]])
            tk.barrier()
            phase_end()
      except _StopBuild:
        pass
    return nc


def _rope_tables(pos):
    inv = ROPE_THETA ** (-np.arange(0, 64, 2, dtype=np.float64) / 64.0)
    ang = pos.astype(np.float64)[None, :] * inv[:, None]
    c = np.cos(ang)
    s = np.sin(ang)
    d = np.arange(128) % 64
    j = d % 32
    sign = np.where(d < 32, -1.0, 1.0)
    return c[j].astype(np.float32), (s[j] * sign[:, None]).astype(np.float32)


def _dft_mats(pos_rows, pos_cols, S):
    k = (pos_rows.astype(np.int64)[:, None] * pos_cols.astype(np.int64)[None, :]) % S
    ang = (2.0 * np.pi / S) * k.astype(np.float64)
    sc = 1.0 / math.sqrt(S * 64.0)
    return (np.cos(ang) * sc).astype(ml_dtypes.bfloat16), (np.sin(ang) * sc).astype(ml_dtypes.bfloat16)


_PROG_CACHE = {}


def run_module(inputs, SS, SP, NLAYER=2, debug=None, stop_phase=None, run_cores=8):
    f32 = np.float32
    xp = np.asarray(inputs["x_prompt"], f32)
    xs = np.asarray(inputs["x_sample"], f32)
    ncores = 8
    HP = SP // 2
    assert xp.shape[0] * 2 == ncores and xs.shape[0] == 2 * ncores
    w_in = np.asarray(inputs["w_in"], f32)
    perm = np.arange(512).reshape(8, 2, 32)[:, ::-1, :].reshape(512)
    q = w_in[:, :, 768:1280]
    k = w_in[:, :, 1280:1792]
    w_in_e = np.ascontiguousarray(np.concatenate([w_in, q[:, :, perm], k[:, :, perm]], axis=2))
    sgu_w = np.asarray(inputs["sgu_w"], f32)
    sgu_b = np.asarray(inputs["sgu_b"], f32)
    sgu_wT = np.ascontiguousarray(np.transpose(sgu_w, (0, 1, 3, 2)))
    sgu_bt = np.ascontiguousarray(np.repeat(np.transpose(sgu_b, (0, 2, 1))[:, :, :, None], 64, axis=3).reshape(sgu_b.shape[0], 128, 256))
    ident = np.eye(128, dtype=f32)
    kk = np.arange(64)
    ang = 2.0 * np.pi * ((kk[:, None] * kk[None, :]) % 64) / 64.0
    bdc = np.zeros((128, 128), f32)
    bdsn = np.zeros((128, 128), f32)
    for g in range(2):
        bdc[g * 64:(g + 1) * 64, g * 64:(g + 1) * 64] = np.cos(ang)
        bdsn[g * 64:(g + 1) * 64, g * 64:(g + 1) * 64] = -np.sin(ang)
    pos_s = np.arange(SS)
    dcs, dss = _dft_mats(pos_s, pos_s, SS)
    shared = {
        "w_in_e": w_in_e, "sgu_wT": sgu_wT, "sgu_bt": sgu_bt, "ident": ident, "bdc": bdc, "bdsn": bdsn,
        "dft_cs": dcs, "dft_ss": dss,
    }
    for nm in ["w_fourier", "w_sgu", "w_diff", "w_out", "vn_g", "vn_b", "lam_q1", "lam_k1", "lam_q2", "lam_k2", "subln_g",
               "ln1_g", "ln1_b", "ln2_g", "ln2_b", "ffn_w_gate", "ffn_w_up", "ffn_w_down", "w_router",
               "moe_w_gate", "moe_w_up", "moe_w_down"]:
        shared[nm] = np.ascontiguousarray(np.asarray(inputs[nm], f32))
    par = {}
    for parity in range(2):
        pos_p = np.concatenate([np.arange(parity * HP, (parity + 1) * HP), np.arange((1 - parity) * HP, (2 - parity) * HP)])
        dcp, dsp = _dft_mats(pos_p, pos_p, SP)
        pos_all = np.concatenate([pos_s, pos_s, pos_p])
        rc, rs = _rope_tables(pos_all)
        par[parity] = dict(pos_p=pos_p, dft_cp=dcp, dft_sp=dsp, ropec=rc, ropes=rs)
    in_maps = []
    for c in range(ncores):
        parity = c % 2
        P = par[parity]
        xin = np.concatenate([xs[2 * c], xs[2 * c + 1], xp[c // 2][P["pos_p"]]], axis=0)
        m = dict(shared)
        m.update(xin=np.ascontiguousarray(xin), dft_cp=P["dft_cp"], dft_sp=P["dft_sp"], ropec=P["ropec"], ropes=P["ropes"])
        in_maps.append(m)
    key = (SS, SP, NLAYER, tuple(sorted(debug)) if debug else None, stop_phase)
    if key not in _PROG_CACHE:
        _PROG_CACHE[key] = build_program(SS, SP, NLAYER, debug, stop_phase)
    nc = _PROG_CACHE[key]
    res = run_bass_kernel_spmd(nc, in_maps[:run_cores], core_ids=list(range(run_cores)))
    outs = [r["out"] for r in res.results]
    outs = outs + [outs[0]] * (ncores - run_cores)
    y_sample = np.stack([outs[c][i * SS:(i + 1) * SS] for c in range(ncores) for i in range(2)], axis=0)
    y_prompt = np.stack([np.concatenate([outs[2 * b][2 * SS:2 * SS + HP], outs[2 * b + 1][2 * SS:2 * SS + HP]], axis=0)
                         for b in range(ncores // 2)], axis=0)
    if debug:
        return (y_prompt.astype(f32), y_sample.astype(f32)), res.results
    return (y_prompt.astype(f32), y_sample.astype(f32))


def kernel(**inputs):
    return run_module(inputs, SS=4096, SP=8192)
```

```python
import math
from contextlib import ExitStack

import numpy as np
import ml_dtypes

import concourse.bass as bass
import concourse.mybir as mybir
from concourse.bass_utils import run_bass_kernel_spmd

F32 = mybir.dt.float32
BF16 = mybir.dt.bfloat16
AF = mybir.ActivationFunctionType
ALU = mybir.AluOpType
AX = mybir.AxisListType

D = 1024
DFF = 2752
NE = 8
NFC = 22
DEPTH = 2
ALPHA = (2 * DEPTH) ** 0.25
LN_EPS = 1e-5
RMS_EPS = 1e-5
ROPE_THETA = 10000.0
C_F, C_U, C_V, C_Q, C_K, C_VA, C_G, C_QS, C_KS = 0, 256, 512, 768, 1280, 1792, 2304, 5376, 5888
NW = 6400
BLK = 512


class Buf:
    def __init__(self, ap, name=""):
        self.ap = ap
        self.name = name
        self.w = {}
        self.r = {}

    def __getitem__(self, idx):
        return self.ap[idx]


class Tracker:
    def __init__(self, nc, es):
        self.nc = nc
        self.es = es
        self.eng = {"pe": nc.tensor, "act": nc.scalar, "dve": nc.vector, "pool": nc.gpsimd, "sp": nc.sync}
        self.sems = {}
        self.cnt = {}
        self.waited = {k: {} for k in self.eng}
        self.nconst = 0
        for k in self.eng:
            self._mksem(k)

    def _mksem(self, key):
        self.sems[key] = self.es.enter_context(self.nc.semaphore("s_" + key))
        self.cnt[key] = 0

    def _wait(self, e, key, val):
        if key == "pe" and e == "pe":
            return
        if key not in self.eng:
            val = self.cnt[key]
        if self.waited[e].get(key, 0) >= val:
            return
        self.eng[e].wait_ge(self.sems[key], val)
        self.waited[e][key] = val

    def _deps(self, e, reads, writes):
        for b in reads:
            for k, v in b.w.items():
                self._wait(e, k, v)
        for b in writes:
            for k, v in b.w.items():
                self._wait(e, k, v)
            for k, v in b.r.items():
                self._wait(e, k, v)

    def _mark(self, key, val, reads, writes):
        for b in reads:
            if b.r.get(key, 0) < val:
                b.r[key] = val
        for b in writes:
            if b.w.get(key, 0) < val:
                b.w[key] = val

    def op(self, e, fn, reads=(), writes=()):
        self._deps(e, reads, writes)
        ins = fn(self.eng[e])
        ins.then_inc(self.sems[e], 1)
        self.cnt[e] += 1
        self._mark(e, self.cnt[e], reads, writes)

    def dma(self, q, semkey, out, in_, reads=(), writes=(), **kw):
        if semkey == "ld_const":
            semkey = f"ldc{self.nconst}"
            self.nconst += 1
        if semkey not in self.sems:
            self._mksem(semkey)
        self._deps(q, reads, writes)
        ins = self.eng[q].dma_start(out=out, in_=in_, **kw)
        ins.then_inc(self.sems[semkey], 16)
        self.cnt[semkey] += 16
        self._mark(semkey, self.cnt[semkey], reads, writes)

    def barrier(self):
        self.nconst = 0
        for e in self.eng:
            for k in self.sems:
                if self.cnt[k] > 0:
                    self._wait(e, k, self.cnt[k])


class _StopBuild(Exception):
    pass


def build_program(SS, SP, NLAYER=2, debug=None, stop_phase=None):
    T0 = 2 * SS + SP
    HP = SP // 2
    T1 = 2 * SS + HP
    units = [(0, SS), (SS, SS), (2 * SS, SP)]

    nc = bass.Bass("TRN2", target_bir_lowering=False)

    def din(name, shape, dt=F32):
        return nc.dram_tensor(name, list(shape), dt, kind="ExternalInput").ap()

    def dscr(name, shape, dt):
        kind = {}
        if debug and name in debug:
            kind = dict(kind="ExternalOutput")
        return nc.dram_tensor(name, list(shape), dt, **kind).ap()

    xin = din("xin", [T0, D])
    w_in = din("w_in_e", [2, D, NW])
    w_fourier = din("w_fourier", [2, 256, D])
    w_sgu = din("w_sgu", [2, 256, D])
    w_diff = din("w_diff", [2, 512, D])
    w_out = din("w_out", [2, D, D])
    vn_g = din("vn_g", [2, 256])
    vn_b = din("vn_b", [2, 256])
    sgu_wT = din("sgu_wT", [2, 4, 128, 128])
    sgu_bt = din("sgu_bt", [2, 128, 256])
    lam_q1 = din("lam_q1", [2, 64])
    lam_k1 = din("lam_k1", [2, 64])
    lam_q2 = din("lam_q2", [2, 64])
    lam_k2 = din("lam_k2", [2, 64])
    subln_g = din("subln_g", [2, 128])
    ln1_g = din("ln1_g", [2, D])
    ln1_b = din("ln1_b", [2, D])
    ln2_g = din("ln2_g", [2, D])
    ln2_b = din("ln2_b", [2, D])
    ffn_w_gate = din("ffn_w_gate", [1, D, DFF])
    ffn_w_up = din("ffn_w_up", [1, D, DFF])
    ffn_w_down = din("ffn_w_down", [1, DFF, D])
    w_router = din("w_router", [1, D, NE])
    moe_w_gate = din("moe_w_gate", [1, NE, D, DFF])
    moe_w_up = din("moe_w_up", [1, NE, D, DFF])
    moe_w_down = din("moe_w_down", [1, NE, DFF, D])
    ident_d = din("ident", [128, 128])
    ropec = din("ropec", [128, T0])
    ropes = din("ropes", [128, T0])
    dft_cs = din("dft_cs", [SS, SS], BF16)
    dft_ss = din("dft_ss", [SS, SS], BF16)
    dft_cp = din("dft_cp", [SP, SP], BF16)
    dft_sp = din("dft_sp", [SP, SP], BF16)
    bdc_d = din("bdc", [128, 128])
    bdsn_d = din("bdsn", [128, 128])
    out = nc.dram_tensor("out", [T1, D], F32, kind="ExternalOutput").ap()

    XTd = dscr("XTd", [D, T0], BF16)
    KTd = dscr("KTd", [512, T0], BF16)
    Vd = dscr("Vd", [4, 128, T0 // 128, 128], BF16)
    Fd = dscr("Fd", [T0, 256], BF16)
    DOd = dscr("DOd", [512, T0], BF16)
    SOd = dscr("SOd", [256, T0], BF16)
    FOd = dscr("FOd", [256, T0], BF16)
    MTd = dscr("MTd", [D, T0], BF16)
    X1d = dscr("X1d", [T0, D], F32)
    X1Td = dscr("X1Td", [D, T0], BF16)
    CMBd = dscr("CMBd", [T0, NE], F32)
    Yd = dscr("Yd", [T0, D], F32)
    Xmid = dscr("Xmid", [T0, D], F32)

    es = ExitStack()
    with es:
      try:
        tk = Tracker(nc, es)
        uid = [0]

        def sb(stack, shape, dt, name):
            uid[0] += 1
            t = stack.enter_context(nc.sbuf_tensor(f"{name}_{uid[0]}", list(shape), dt))
            return Buf(t, name)

        PS = [Buf(es.enter_context(nc.psum_tensor(f"psum{i}", [128, 512], F32)), f"ps{i}") for i in range(8)]

        ident = sb(es, [128, 128], F32, "ident")
        tk.dma("sp", "ld_const", ident[:], ident_d[:, :], writes=[ident])
        ones_bf = sb(es, [128, 128], BF16, "ones_bf")
        onesm_bf = sb(es, [128, 128], BF16, "onesm_bf")
        tk.op("dve", lambda e: e.memset(ones_bf[:], 1.0), writes=[ones_bf])
        tk.op("dve", lambda e: e.memset(onesm_bf[:], 1.0 / 128.0), writes=[onesm_bf])
        bdc = sb(es, [128, 128], BF16, "bdc")
        bdsn = sb(es, [128, 128], BF16, "bdsn")
        tk.dma("pool", "ld_constp", bdc[:], bdc_d[:, :], writes=[bdc])
        tk.dma("pool", "ld_constp", bdsn[:], bdsn_d[:, :], writes=[bdsn])

        def load_w_ops(dst, dst_c0, src2d, K, c0, ncols, semkey):
            ops = []
            nk = (K + 127) // 128
            for kc in range(nk):
                rows = min(128, K - kc * 128)
                cc = 0
                while cc < ncols:
                    n = min(2048, ncols - cc)
                    ops.append(lambda kc=kc, rows=rows, cc=cc, n=n: tk.dma(
                        "pool", semkey, dst[0:rows, kc, dst_c0 + cc:dst_c0 + cc + n],
                        src2d[kc * 128:kc * 128 + rows, c0 + cc:c0 + cc + n], writes=[dst]))
                    cc += n
            return ops

        def load_w(dst, dst_c0, src2d, K, c0, ncols, semkey):
            for op_ in load_w_ops(dst, dst_c0, src2d, K, c0, ncols, semkey):
                op_()

        def layer_norm_tile(stack_bufs, y, gtab, btab, outb, l_eng="pool"):
            st, mv = stack_bufs
            tk.op("dve", lambda e: e.bn_stats(out=st[:, 0:6], in_=y[:, 0:512]), reads=[y], writes=[st])
            tk.op("dve", lambda e: e.bn_stats(out=st[:, 6:12], in_=y[:, 512:1024]), reads=[y], writes=[st])
            tk.op("dve", lambda e: e.bn_aggr(out=mv[:, 0:2], in_=st[:, 0:12]), reads=[st], writes=[mv])
            tk.op("dve", lambda e: e.tensor_scalar(out=mv[:, 2:3], in0=mv[:, 1:2], scalar1=LN_EPS, scalar2=None,
                                                   op0=ALU.add), reads=[mv], writes=[mv])
            tk.op("act", lambda e: e.activation(out=mv[:, 3:4], in_=mv[:, 2:3], func=AF.Sqrt), reads=[mv], writes=[mv])
            tk.op("dve", lambda e: e.reciprocal(out=mv[:, 2:3], in_=mv[:, 3:4]), reads=[mv], writes=[mv])
            tk.op("dve", lambda e: e.tensor_scalar(out=y[:, :], in0=y[:, :], scalar1=mv[:, 0:1], scalar2=mv[:, 2:3],
                                                   op0=ALU.subtract, op1=ALU.mult), reads=[y, mv], writes=[y])
            tk.op("dve", lambda e: e.tensor_tensor(out=y[:, :], in0=y[:, :], in1=gtab[:, :], op=ALU.mult),
                  reads=[y, gtab], writes=[y])
            tk.op(l_eng, lambda e: e.tensor_tensor(out=outb[:, :], in0=y[:, :], in1=btab[:, :], op=ALU.add),
                  reads=[y, btab], writes=[outb])

        def bcast_load(dst, vec_ap, semkey="ld_const"):
            tk.dma("sp", semkey, dst[:, :], vec_ap.partition_broadcast(128), writes=[dst])

        phc = [0]

        def phase_end():
            phc[0] += 1
            if stop_phase is not None and phc[0] >= stop_phase:
                raise _StopBuild()

        for l in range(NLAYER):
            Xsrc = xin if l == 0 else Xmid
            TQ = T0 if l == 0 else T1
            qunits = [(t0, Sk, (Sk if l == 0 else min(Sk, SS if t0 < 2 * SS else HP))) for (t0, Sk) in units]
            lambda_init = 0.8 - 0.6 * math.exp(-0.3 * l)
            W2 = w_in[l]

            with ExitStack() as st1:
                Wk = sb(st1, [128, 8, 1792], BF16, "Wk")
                load_w(Wk, 0, W2, D, C_K, 512, "ldw")
                load_w(Wk, 512, W2, D, C_KS, 512, "ldw")
                load_w(Wk, 1024, W2, D, C_VA, 512, "ldw")
                load_w(Wk, 1536, W2, D, C_F, 256, "ldw")
                xs = [sb(st1, [128, 4, D], F32, "xs") for _ in range(2)]
                cs = [sb(st1, [128, 2, BLK], F32, "cs") for _ in range(2)]
                XT = [sb(st1, [128, 8, BLK], BF16, "XT") for _ in range(2)]
                kst = [sb(st1, [128, 4, BLK], BF16, "kst") for _ in range(2)]
                vst = [sb(st1, [128, 4, 4, 128], BF16, "vst") for _ in range(2)]
                fst = [sb(st1, [128, 4, 256], BF16, "fst") for _ in range(2)]
                tmp = [sb(st1, [128, BLK], F32, "tmp") for _ in range(4)]
                nb = T0 // BLK

                def p1_load(bi):
                    s = bi % 2
                    t0 = bi * BLK
                    tk.dma("sp", f"p1x{s}", xs[s][:, :, :], Xsrc[t0:t0 + BLK, :].rearrange("(j p) d -> p j d", p=128),
                           writes=[xs[s]])
                    tk.dma("sp", f"p1x{s}", cs[s][:, 0, :], ropec[:, t0:t0 + BLK], writes=[cs[s]])
                    tk.dma("sp", f"p1x{s}", cs[s][:, 1, :], ropes[:, t0:t0 + BLK], writes=[cs[s]])

                p1_load(0)
                for bi in range(nb):
                    s = bi % 2
                    t0 = bi * BLK
                    if bi + 1 < nb:
                        p1_load(bi + 1)
                    for kc in range(8):
                        bank = PS[kc % 2]
                        for j in range(4):
                            tk.op("pe", lambda e, j=j, kc=kc, bank=bank: e.transpose(
                                bank[:, j * 128:(j + 1) * 128], xs[s][:, j, kc * 128:(kc + 1) * 128], ident[:, :]),
                                reads=[xs[s], ident], writes=[bank])
                        if kc % 2 == 0:
                            tk.op("act", lambda e, kc=kc, bank=bank: e.activation(out=XT[s][:, kc, :], in_=bank[:, :], func=AF.Copy),
                                  reads=[bank], writes=[XT[s]])
                        else:
                            tk.op("dve", lambda e, kc=kc, bank=bank: e.tensor_copy(out=XT[s][:, kc, :], in_=bank[:, :]),
                                  reads=[bank], writes=[XT[s]])
                    tk.dma("pool", f"p1s{s}", XTd.rearrange("(kc p) t -> p kc t", p=128)[:, :, t0:t0 + BLK], XT[s][:, :, :],
                           reads=[XT[s]])
                    for h in range(4):
                        A = PS[2 + 2 * (h % 2)]
                        B = PS[3 + 2 * (h % 2)]
                        for kc in range(8):
                            tk.op("pe", lambda e, kc=kc, A=A: e.matmul(A[:, :], Wk[:, kc, h * 128:(h + 1) * 128], XT[s][:, kc, :],
                                                                    start=(kc == 0), stop=(kc == 7)),
                                  reads=[Wk, XT[s]], writes=[A])
                        for kc in range(8):
                            tk.op("pe", lambda e, kc=kc, B=B: e.matmul(B[:, :], Wk[:, kc, 512 + h * 128:512 + (h + 1) * 128], XT[s][:, kc, :],
                                                                    start=(kc == 0), stop=(kc == 7)),
                                  reads=[Wk, XT[s]], writes=[B])
                        ta, tb = tmp[2 * (h % 2)], tmp[2 * (h % 2) + 1]
                        tk.op("dve", lambda e, A=A, ta=ta: e.tensor_tensor(out=ta[:, :], in0=A[:, :], in1=cs[s][:, 0, :], op=ALU.mult),
                              reads=[A, cs[s]], writes=[ta])
                        tk.op("dve", lambda e, B=B, tb=tb: e.tensor_tensor(out=tb[:, :], in0=B[:, :], in1=cs[s][:, 1, :], op=ALU.mult),
                              reads=[B, cs[s]], writes=[tb])
                        tk.op("pool", lambda e, ta=ta, tb=tb: e.tensor_tensor(out=kst[s][:, h, :], in0=ta[:, :], in1=tb[:, :], op=ALU.add),
                              reads=[ta, tb], writes=[kst[s]])
                    tk.dma("pool", f"p1s{s}", KTd.rearrange("(h p) t -> p h t", p=128)[:, :, t0:t0 + BLK], kst[s][:, :, :],
                           reads=[kst[s]])
                    for st_ in range(8):
                        j = st_ % 4
                        bank = PS[6 + st_ % 2]
                        if st_ < 4:
                            for kc in range(8):
                                tk.op("pe", lambda e, kc=kc, bank=bank, j=j: e.matmul(bank[:, :], XT[s][:, kc, j * 128:(j + 1) * 128],
                                                                                  Wk[:, kc, 1024:1536], start=(kc == 0), stop=(kc == 7)),
                                      reads=[Wk, XT[s]], writes=[bank])
                            tk.op("act", lambda e, bank=bank, j=j: e.activation(
                                out=vst[s][:, :, j, :], in_=bank[:, :].rearrange("p (h e) -> p h e", h=4), func=AF.Copy),
                                reads=[bank], writes=[vst[s]])
                        else:
                            for kc in range(8):
                                tk.op("pe", lambda e, kc=kc, bank=bank, j=j: e.matmul(bank[:, 0:256], XT[s][:, kc, j * 128:(j + 1) * 128],
                                                                                  Wk[:, kc, 1536:1792], start=(kc == 0), stop=(kc == 7)),
                                      reads=[Wk, XT[s]], writes=[bank])
                            tk.op("dve", lambda e, bank=bank, j=j: e.tensor_copy(out=fst[s][:, j, :], in_=bank[:, 0:256]),
                                  reads=[bank], writes=[fst[s]])
                    c0 = t0 // 128
                    tk.dma("pool", f"p1s{s}", Vd[:, :, c0:c0 + 4, :].rearrange("h p c e -> p h c e"), vst[s][:, :, :, :],
                           reads=[vst[s]])
                    tk.dma("pool", f"p1s{s}", Fd[t0:t0 + BLK, :].rearrange("(j p) f -> p j f", p=128), fst[s][:, :, :],
                           reads=[fst[s]])
            tk.barrier()

            phase_end()
            with ExitStack() as st2:
                Wq = sb(st2, [128, 8, 1024], BF16, "Wq")
                load_w(Wq, 0, W2, D, C_Q, 512, "ldw")
                load_w(Wq, 512, W2, D, C_QS, 512, "ldw")
                lv = [sb(st2, [128, 64], F32, "lv") for _ in range(4)]
                bcast_load(lv[0], lam_q1[l, :])
                bcast_load(lv[1], lam_k1[l, :])
                bcast_load(lv[2], lam_q2[l, :])
                bcast_load(lv[3], lam_k2[l, :])
                sm = sb(st2, [128, 8], F32, "sm")
                lt = sb(st2, [128, 64], F32, "lt")
                tk.op("dve", lambda e: e.tensor_tensor(out=lt[:, :], in0=lv[0][:, :], in1=lv[1][:, :], op=ALU.mult),
                      reads=[lv[0], lv[1]], writes=[lt])
                tk.op("dve", lambda e: e.reduce_sum(out=sm[:, 0:1], in_=lt[:, :], axis=AX.X), reads=[lt], writes=[sm])
                tk.op("dve", lambda e: e.tensor_tensor(out=lt[:, :], in0=lv[2][:, :], in1=lv[3][:, :], op=ALU.mult),
                      reads=[lv[2], lv[3], sm], writes=[lt])
                tk.op("dve", lambda e: e.reduce_sum(out=sm[:, 1:2], in_=lt[:, :], axis=AX.X), reads=[lt], writes=[sm])
                tk.op("act", lambda e: e.activation(out=sm[:, 2:4], in_=sm[:, 0:2], func=AF.Exp), reads=[sm], writes=[sm])
                tk.op("dve", lambda e: e.tensor_tensor(out=sm[:, 4:5], in0=sm[:, 3:4], in1=sm[:, 2:3], op=ALU.subtract),
                      reads=[sm], writes=[sm])
                tk.op("dve", lambda e: e.tensor_scalar(out=sm[:, 5:6], in0=sm[:, 4:5], scalar1=-lambda_init, scalar2=None,
                                                       op0=ALU.add), reads=[sm], writes=[sm])
                negl = sm
                gcol = sb(st2, [128, 2], F32, "gcol")
                tk.dma("sp", "ld_const", gcol[:, 0:1], subln_g[l, :].rearrange("(p o) -> p o", o=1), writes=[gcol])
                tk.op("dve", lambda e: e.tensor_scalar(out=gcol[:, 1:2], in0=gcol[:, 0:1], scalar1=(1.0 - lambda_init),
                                                       scalar2=None, op0=ALU.mult), reads=[gcol], writes=[gcol])

                SKM = max(Sk for _, Sk in units)
                XTb = [sb(st2, [128, 8, BLK], BF16, "XTb") for _ in range(2)]
                csb = [sb(st2, [128, 2, BLK], F32, "csb") for _ in range(2)]
                QT = [sb(st2, [128, 4, 2, BLK], BF16, "QT") for _ in range(2)]
                for s_ in range(2):
                    tk.op("pool", lambda e, s_=s_: e.memset(QT[s_][:, :, :, :], 0.0), writes=[QT[s_]])
                Kh = [sb(st2, [128, SKM], BF16, "Kh") for _ in range(2)]
                Vh = [sb(st2, [128, SKM // 128, 128], BF16, "Vh") for _ in range(2)]
                NPT = 6
                pT = [sb(st2, [128, BLK], BF16, "pT") for _ in range(NPT)]
                tmpq = [sb(st2, [128, BLK], F32, "tmpq") for _ in range(2)]
                ep_r = [sb(st2, [128, BLK], F32, "ep_r") for _ in range(2)]
                ep_t = [sb(st2, [128, BLK], F32, "ep_t") for _ in range(2)]
                ep_a = [sb(st2, [128, BLK], F32, "ep_a") for _ in range(2)]
                ep_sq = [sb(st2, [128, BLK], BF16, "ep_sq") for _ in range(2)]
                ep_sd = [sb(st2, [128, BLK], F32, "ep_sd") for _ in range(2)]
                doT = [sb(st2, [128, 4, BLK], BF16, "doT") for _ in range(2)]

                jobs = []
                for (t0u, Sk, Sq) in qunits:
                    for qb in range(Sq // BLK):
                        jobs.append((t0u, Sk, t0u + qb * BLK))
                hjobs = [(ji, h) for ji in range(len(jobs)) for h in range(4)]

                def a_load_q(ji):
                    s = ji % 2
                    _, _, tq = jobs[ji]
                    tk.dma("sp", f"ax{s}", XTb[s][:, :, :], XTd.rearrange("(kc p) t -> p kc t", p=128)[:, :, tq:tq + BLK],
                           writes=[XTb[s]])
                    tk.dma("sp", f"ax{s}", csb[s][:, 0, :], ropec[:, tq:tq + BLK], writes=[csb[s]])
                    tk.dma("sp", f"ax{s}", csb[s][:, 1, :], ropes[:, tq:tq + BLK], writes=[csb[s]])

                def a_load_kv(hi):
                    ji, h = hjobs[hi]
                    t0u, Sk, _ = jobs[ji]
                    s = hi % 2
                    tk.dma("sp", f"akv{s}", Kh[s][:, 0:Sk], KTd[h * 128:(h + 1) * 128, t0u:t0u + Sk], writes=[Kh[s]])
                    c0 = t0u // 128
                    tk.dma("sp", f"akv{s}", Vh[s][:, 0:Sk // 128, :], Vd[h, :, c0:c0 + Sk // 128, :], writes=[Vh[s]])

                pending_b = []

                def epi_a(hi, O, Ssum):
                    p = hi % 2
                    for m in range(2):
                        tk.op("dve", lambda e, m=m: e.reciprocal(out=ep_r[m][:, :], in_=Ssum[m][:, :]),
                              reads=[Ssum[m]], writes=[ep_r[m]])
                        tk.op("dve", lambda e, m=m: e.tensor_tensor(out=ep_t[m][:, :], in0=O[m][:, :], in1=ep_r[m][:, :], op=ALU.mult),
                              reads=[O[m], ep_r[m]], writes=[ep_t[m]])
                    tk.op("dve", lambda e: e.scalar_tensor_tensor(out=ep_a[p][:, :], in0=ep_t[1][:, :], scalar=negl[:, 5:6],
                                                                  in1=ep_t[0][:, :], op0=ALU.mult, op1=ALU.add),
                          reads=[ep_t[0], ep_t[1], negl], writes=[ep_a[p]])
                    tk.op("act", lambda e: e.activation(out=ep_sq[p][:, :], in_=ep_a[p][:, :], func=AF.Square),
                          reads=[ep_a[p]], writes=[ep_sq[p]])

                def epi_b(hi, bank):
                    p = hi % 2
                    ji, h = hjobs[hi]
                    s = ji % 2
                    tk.op("pe", lambda e: e.matmul(bank[:, :], onesm_bf[:, :], ep_sq[p][:, :], start=True, stop=True),
                          reads=[onesm_bf, ep_sq[p]], writes=[bank])
                    tk.op("dve", lambda e: e.tensor_scalar(out=ep_sd[p][:, :], in0=bank[:, :], scalar1=RMS_EPS, scalar2=None,
                                                           op0=ALU.add), reads=[bank], writes=[ep_sd[p]])
                    tk.op("act", lambda e: e.activation(out=ep_sd[p][:, :], in_=ep_sd[p][:, :], func=AF.Sqrt),
                          reads=[ep_sd[p]], writes=[ep_sd[p]])
                    tk.op("dve", lambda e: e.reciprocal(out=ep_sd[p][:, :], in_=ep_sd[p][:, :]), reads=[ep_sd[p]], writes=[ep_sd[p]])
                    tk.op("dve", lambda e: e.tensor_tensor(out=ep_a[p][:, :], in0=ep_a[p][:, :], in1=ep_sd[p][:, :], op=ALU.mult),
                          reads=[ep_a[p], ep_sd[p]], writes=[ep_a[p]])
                    tk.op("dve", lambda e: e.tensor_scalar(out=doT[s][:, h, :], in0=ep_a[p][:, :], scalar1=gcol[:, 1:2], scalar2=None,
                                                           op0=ALU.mult), reads=[ep_a[p], gcol], writes=[doT[s]])
                    if h == 3:
                        _, _, tq = jobs[ji]
                        tk.dma("pool", f"ast{s}", DOd.rearrange("(h p) t -> p h t", p=128)[:, :, tq:tq + BLK], doT[s][:, :, :],
                               reads=[doT[s]])

                a_load_q(0)
                a_load_kv(0)
                for hi, (ji, h) in enumerate(hjobs):
                    t0u, Sk, tq = jobs[ji]
                    s = ji % 2
                    ks = hi % 2
                    if h == 0:
                        if ji + 1 < len(jobs):
                            a_load_q(ji + 1)
                        for hh in range(4):
                            A = PS[4 + 2 * (hh % 2)]
                            B = PS[5 + 2 * (hh % 2)]
                            for kc in range(8):
                                tk.op("pe", lambda e, kc=kc, A=A, hh=hh: e.matmul(A[:, :], Wq[:, kc, hh * 128:(hh + 1) * 128], XTb[s][:, kc, :],
                                                                                start=(kc == 0), stop=(kc == 7)),
                                      reads=[Wq, XTb[s]], writes=[A])
                            for kc in range(8):
                                tk.op("pe", lambda e, kc=kc, B=B, hh=hh: e.matmul(B[:, :], Wq[:, kc, 512 + hh * 128:512 + (hh + 1) * 128], XTb[s][:, kc, :],
                                                                                start=(kc == 0), stop=(kc == 7)),
                                      reads=[Wq, XTb[s]], writes=[B])
                            tk.op("dve", lambda e, A=A: e.tensor_tensor(out=tmpq[0][:, :], in0=A[:, :], in1=csb[s][:, 0, :], op=ALU.mult),
                                  reads=[A, csb[s]], writes=[tmpq[0]])
                            tk.op("dve", lambda e, B=B: e.tensor_tensor(out=tmpq[1][:, :], in0=B[:, :], in1=csb[s][:, 1, :], op=ALU.mult),
                                  reads=[B, csb[s]], writes=[tmpq[1]])
                            for m_ in range(2):
                                tk.op("pool", lambda e, hh=hh, m_=m_: e.tensor_tensor(
                                    out=QT[s][m_ * 64:(m_ + 1) * 64, hh, m_, :], in0=tmpq[0][m_ * 64:(m_ + 1) * 64, :],
                                    in1=tmpq[1][m_ * 64:(m_ + 1) * 64, :], op=ALU.add),
                                    reads=[tmpq[0], tmpq[1]], writes=[QT[s]])
                    if hi + 1 < len(hjobs):
                        a_load_kv(hi + 1)
                    O = [PS[0], PS[1]]
                    Ssum = [PS[2], PS[3]]
                    nkc = Sk // 128
                    steps = [(kc, m) for kc in range(nkc) for m in range(2)]
                    LA = 2

                    def qk(i):
                        kc, m = steps[i]
                        sc = PS[4 + i % 4]
                        tk.op("pe", lambda e: e.matmul(sc[:, :], Kh[ks][:, kc * 128:(kc + 1) * 128],
                                                       QT[s][:, h, m, :], start=True, stop=True),
                              reads=[Kh[ks], QT[s]], writes=[sc])
                        pt = pT[i % NPT]
                        tk.op("act", lambda e: e.activation(out=pt[:, :], in_=sc[:, :], func=AF.Exp, scale=0.125),
                              reads=[sc], writes=[pt])

                    for i in range(min(LA, len(steps))):
                        qk(i)
                    for i, (kc, m) in enumerate(steps):
                        if i + LA < len(steps):
                            qk(i + LA)
                        pt = pT[i % NPT]
                        tk.op("pe", lambda e: e.matmul(O[m][:, :], Vh[ks][:, kc, :], pt[:, :], start=(kc == 0), stop=(kc == nkc - 1)),
                              reads=[Vh[ks], pt], writes=[O[m]])
                        tk.op("pe", lambda e: e.matmul(Ssum[m][:, :], ones_bf[:, :], pt[:, :], start=(kc == 0), stop=(kc == nkc - 1)),
                              reads=[ones_bf, pt], writes=[Ssum[m]])
                        if i == 8 and pending_b:
                            pending_b.pop(0)()
                    while pending_b:
                        pending_b.pop(0)()
                    epi_a(hi, O, Ssum)
                    pending_b.append(lambda hi=hi: epi_b(hi, PS[4 + (hi % 2)]))
                while pending_b:
                    pending_b.pop(0)()
            tk.barrier()

            phase_end()
            with ExitStack() as st3:
                GRP = 4
                Fg = [sb(st3, [128, GRP, 256], BF16, "Fg") for _ in range(2)]
                Cg = [sb(st3, [128, GRP, BLK], BF16, "Cg") for _ in range(2)]
                Sg = [sb(st3, [128, GRP, BLK], BF16, "Sg") for _ in range(2)]
                cfs = [sb(st3, [128, 4, BLK], BF16, "cfs") for _ in range(2)]
                foT = [sb(st3, [128, 2, BLK], BF16, "foT") for _ in range(2)]
                gj = []
                bjobs = []
                for (t0u, Sk, Sq) in qunits:
                    for qb in range(Sq // BLK):
                        bjobs.append((t0u, Sk, qb))
                for bi_, (t0u, Sk, qb) in enumerate(bjobs):
                    for g in range(Sk // (128 * GRP)):
                        gj.append((bi_, t0u, Sk, qb, g))

                def f_load(gi):
                    bi_, t0u, Sk, qb, g = gj[gi]
                    s = gi % 2
                    r0 = g * 128 * GRP
                    Cm, Sm = (dft_cs, dft_ss) if Sk == SS and t0u < 2 * SS else (dft_cp, dft_sp)
                    tk.dma("sp", f"fl{s}", Fg[s][:, :, :], Fd[t0u + r0:t0u + r0 + 128 * GRP, :].rearrange("(c p) f -> p c f", p=128),
                           writes=[Fg[s]])
                    tk.dma("sp", f"fl{s}", Cg[s][:, :, :], Cm[r0:r0 + 128 * GRP, qb * BLK:(qb + 1) * BLK].rearrange("(c p) n -> p c n", p=128),
                           writes=[Cg[s]])
                    tk.dma("sp", f"fl{s}", Sg[s][:, :, :], Sm[r0:r0 + 128 * GRP, qb * BLK:(qb + 1) * BLK].rearrange("(c p) n -> p c n", p=128),
                           writes=[Sg[s]])

                f_load(0)
                for gi, (bi_, t0u, Sk, qb, g) in enumerate(gj):
                    s = gi % 2
                    if gi + 1 < len(gj):
                        f_load(gi + 1)
                    ng = Sk // (128 * GRP)
                    for c in range(GRP):
                        first = (g == 0 and c == 0)
                        last = (g == ng - 1 and c == GRP - 1)
                        for a in range(4):
                            mat = Cg[s] if a < 2 else Sg[s]
                            tk.op("pe", lambda e, a=a, c=c, mat=mat: e.matmul(PS[a][:, :], Fg[s][:, c, (a % 2) * 128:(a % 2 + 1) * 128], mat[:, c, :],
                                                                          start=first, stop=last),
                                  reads=[Fg[s], mat], writes=[PS[a]])
                    if g == ng - 1:
                        bs = bi_ % 2
                        tq = t0u + qb * BLK
                        for a in range(4):
                            if a % 2 == 0:
                                tk.op("act", lambda e, a=a: e.activation(out=cfs[bs][:, a, :], in_=PS[a][:, :], func=AF.Copy),
                                      reads=[PS[a]], writes=[cfs[bs]])
                            else:
                                tk.op("dve", lambda e, a=a: e.tensor_copy(out=cfs[bs][:, a, :], in_=PS[a][:, :]),
                                      reads=[PS[a]], writes=[cfs[bs]])
                        for cc in range(2):
                            bank = PS[4 + cc + 2 * (bi_ % 2)]
                            tk.op("pe", lambda e, cc=cc, bank=bank: e.matmul(bank[:, :], bdc[:, :], cfs[bs][:, cc, :], start=True, stop=False),
                                  reads=[bdc, cfs[bs]], writes=[bank])
                            tk.op("pe", lambda e, cc=cc, bank=bank: e.matmul(bank[:, :], bdsn[:, :], cfs[bs][:, 2 + cc, :], start=False, stop=True),
                                  reads=[bdsn, cfs[bs]], writes=[bank])
                            if cc == 0:
                                tk.op("act", lambda e, cc=cc, bank=bank: e.activation(out=foT[bs][:, cc, :], in_=bank[:, :], func=AF.Copy),
                                      reads=[bank], writes=[foT[bs]])
                            else:
                                tk.op("dve", lambda e, cc=cc, bank=bank: e.tensor_copy(out=foT[bs][:, cc, :], in_=bank[:, :]),
                                      reads=[bank], writes=[foT[bs]])
                        tk.dma("pool", f"fst{bs}", FOd.rearrange("(c p) t -> p c t", p=128)[:, :, tq:tq + BLK], foT[bs][:, :, :],
                               reads=[foT[bs]])
            tk.barrier()

            phase_end()
            with ExitStack() as st4:
                Wuv = sb(st4, [128, 8, 512], BF16, "Wuv")
                load_w(Wuv, 0, W2, D, C_U, 512, "ldw")
                swT = sb(st4, [128, 4, 128], BF16, "swT")
                for h in range(4):
                    tk.dma("pool", "ldw", swT[:, h, :], sgu_wT[l, h, :, :], writes=[swT])
                btab = sb(st4, [128, 256], F32, "btab")
                tk.dma("sp", "ld_const", btab[:, :], sgu_bt[l, :, :], writes=[btab])
                gvn = sb(st4, [128, 256], F32, "gvn")
                bvn = sb(st4, [128, 256], F32, "bvn")
                bcast_load(gvn, vn_g[l, :])
                bcast_load(bvn, vn_b[l, :])
                XTc = [sb(st4, [128, 8, BLK], BF16, "XTc") for _ in range(2)]
                ub = [sb(st4, [128, 256], F32, "ub") for _ in range(2)]
                vn0 = [sb(st4, [128, 256], F32, "vn0") for _ in range(2)]
                vnb = [sb(st4, [128, 256], BF16, "vnb") for _ in range(2)]
                junk = [sb(st4, [128, 256], F32, "junk") for _ in range(2)]
                so = [sb(st4, [128, 256], F32, "so") for _ in range(2)]
                sst = [sb(st4, [128, 8], F32, "sst") for _ in range(2)]
                soT = [sb(st4, [128, 2, BLK], BF16, "soT") for _ in range(2)]
                cjobs = []
                for (t0u, Sk, Sq) in qunits:
                    for qb in range(Sq // BLK):
                        cjobs.append(t0u + qb * BLK)

                def c_load(bi_):
                    s = bi_ % 2
                    tq = cjobs[bi_]
                    tk.dma("sp", f"cx{s}", XTc[s][:, :, :], XTd.rearrange("(kc p) t -> p kc t", p=128)[:, :, tq:tq + BLK],
                           writes=[XTc[s]])

                def sgu_a(bi_, j):
                    s = bi_ % 2
                    p = j % 2
                    if j == 0 and bi_ + 1 < len(cjobs):
                        c_load(bi_ + 1)
                    bank = PS[p]
                    for kc in range(8):
                        tk.op("pe", lambda e, kc=kc, bank=bank, j=j: e.matmul(bank[:, :], XTc[s][:, kc, j * 128:(j + 1) * 128], Wuv[:, kc, :],
                                                                          start=(kc == 0), stop=(kc == 7)),
                              reads=[XTc[s], Wuv], writes=[bank])
                    tk.op("act", lambda e, bank=bank, p=p: e.activation(out=junk[p][:, :], in_=bank[:, 256:512], func=AF.Copy,
                                                                        accum_out=sst[p][:, 0:1]),
                          reads=[bank], writes=[junk[p], sst[p]])
                    tk.op("act", lambda e, bank=bank, p=p: e.activation(out=junk[p][:, :], in_=bank[:, 256:512], func=AF.Square,
                                                                        accum_out=sst[p][:, 1:2]),
                          reads=[bank], writes=[junk[p], sst[p]])
                    tk.op("act", lambda e, bank=bank, p=p: e.activation(out=ub[p][:, :], in_=bank[:, 0:256], func=AF.Copy),
                          reads=[bank], writes=[ub[p]])
                    tk.op("dve", lambda e, p=p: e.tensor_scalar(out=sst[p][:, 2:4], in0=sst[p][:, 0:2], scalar1=1.0 / 256.0, scalar2=None,
                                                                op0=ALU.mult), reads=[sst[p]], writes=[sst[p]])
                    tk.op("dve", lambda e, p=p: e.tensor_tensor(out=sst[p][:, 4:5], in0=sst[p][:, 2:3], in1=sst[p][:, 2:3], op=ALU.mult),
                          reads=[sst[p]], writes=[sst[p]])
                    tk.op("dve", lambda e, p=p: e.tensor_tensor(out=sst[p][:, 5:6], in0=sst[p][:, 3:4], in1=sst[p][:, 4:5], op=ALU.subtract),
                          reads=[sst[p]], writes=[sst[p]])
                    tk.op("dve", lambda e, p=p: e.tensor_scalar(out=sst[p][:, 6:7], in0=sst[p][:, 5:6], scalar1=LN_EPS, scalar2=None,
                                                                op0=ALU.add), reads=[sst[p]], writes=[sst[p]])
                    tk.op("act", lambda e, p=p: e.activation(out=sst[p][:, 7:8], in_=sst[p][:, 6:7], func=AF.Sqrt),
                          reads=[sst[p]], writes=[sst[p]])
                    tk.op("dve", lambda e, p=p: e.reciprocal(out=sst[p][:, 6:7], in_=sst[p][:, 7:8]), reads=[sst[p]], writes=[sst[p]])
                    tk.op("dve", lambda e, bank=bank, p=p: e.tensor_scalar(out=vn0[p][:, :], in0=bank[:, 256:512], scalar1=sst[p][:, 2:3],
                                                                           scalar2=sst[p][:, 6:7], op0=ALU.subtract, op1=ALU.mult),
                          reads=[bank, sst[p]], writes=[vn0[p]])
                    tk.op("dve", lambda e, p=p: e.tensor_tensor(out=vn0[p][:, :], in0=vn0[p][:, :], in1=gvn[:, :], op=ALU.mult),
                          reads=[vn0[p], gvn], writes=[vn0[p]])
                    tk.op("pool", lambda e, p=p: e.tensor_tensor(out=vnb[p][:, :], in0=vn0[p][:, :], in1=bvn[:, :], op=ALU.add),
                          reads=[vn0[p], bvn], writes=[vnb[p]])

                def sgu_b(bi_, j):
                    s = bi_ % 2
                    p = j % 2
                    tq = cjobs[bi_]
                    bank2 = PS[2 + p]
                    for h in range(4):
                        tk.op("pe", lambda e, h=h, bank2=bank2, p=p: e.matmul(bank2[:, h * 64:(h + 1) * 64], swT[:, h, :], vnb[p][:, h * 64:(h + 1) * 64],
                                                                          start=True, stop=True),
                              reads=[swT, vnb[p]], writes=[bank2])
                    tk.op("dve", lambda e, bank2=bank2, p=p: e.tensor_tensor(out=so[p][:, :], in0=bank2[:, 0:256], in1=btab[:, :], op=ALU.add),
                          reads=[bank2, btab], writes=[so[p]])
                    tk.op("dve", lambda e, p=p: e.tensor_tensor(out=so[p][:, :], in0=so[p][:, :], in1=ub[p][:, :], op=ALU.mult),
                          reads=[so[p], ub[p]], writes=[so[p]])
                    for cc in range(2):
                        bank3 = PS[4 + cc + 2 * (bi_ % 2)]
                        tk.op("pe", lambda e, cc=cc, bank3=bank3, p=p, j=j: e.transpose(bank3[:, j * 128:(j + 1) * 128], so[p][:, cc * 128:(cc + 1) * 128],
                                                                                    ident[:, :]),
                              reads=[so[p], ident], writes=[bank3])
                    if j == 3:
                        for cc in range(2):
                            bank3 = PS[4 + cc + 2 * (bi_ % 2)]
                            if cc == 0:
                                tk.op("act", lambda e, cc=cc, bank3=bank3: e.activation(out=soT[s][:, cc, :], in_=bank3[:, :], func=AF.Copy),
                                      reads=[bank3], writes=[soT[s]])
                            else:
                                tk.op("dve", lambda e, cc=cc, bank3=bank3: e.tensor_copy(out=soT[s][:, cc, :], in_=bank3[:, :]),
                                      reads=[bank3], writes=[soT[s]])
                        tk.dma("pool", f"cst{s}", SOd.rearrange("(c p) t -> p c t", p=128)[:, :, tq:tq + BLK], soT[s][:, :, :],
                               reads=[soT[s]])

                c_load(0)
                ctiles = [(bi_, j) for bi_ in range(len(cjobs)) for j in range(4)]
                for ti, (bi_, j) in enumerate(ctiles):
                    sgu_a(bi_, j)
                    if ti >= 1:
                        sgu_b(*ctiles[ti - 1])
                sgu_b(*ctiles[-1])
            tk.barrier()

            phase_end()
            with ExitStack() as st5:
                Wg = sb(st5, [128, 8, 3072], BF16, "Wg")
                load_w(Wg, 0, W2, D, C_G, 3072, "ldw")
                Wfo = sb(st5, [128, 2, D], BF16, "Wfo")
                load_w(Wfo, 0, w_fourier[l], 256, 0, D, "ldw")
                Wsg = sb(st5, [128, 2, D], BF16, "Wsg")
                load_w(Wsg, 0, w_sgu[l], 256, 0, D, "ldw")
                Wdf = sb(st5, [128, 4, D], BF16, "Wdf")
                load_w(Wdf, 0, w_diff[l], 512, 0, D, "ldw")
                XTe = [sb(st5, [128, 8, BLK], BF16, "XTe") for _ in range(2)]
                dob = [sb(st5, [128, 4, BLK], BF16, "dob") for _ in range(2)]
                sob = [sb(st5, [128, 2, BLK], BF16, "sob") for _ in range(2)]
                fob = [sb(st5, [128, 2, BLK], BF16, "fob") for _ in range(2)]
                sgt = [[sb(st5, [128, BLK], F32, "sgt") for _ in range(3)] for _ in range(2)]
                mt = [sb(st5, [128, BLK], F32, "mt") for _ in range(2)]
                mT = [sb(st5, [128, 8, BLK], BF16, "mT") for _ in range(2)]
                djobs = list(cjobs)

                def d_load(bi_):
                    s = bi_ % 2
                    tq = djobs[bi_]
                    tk.dma("sp", f"dx{s}", XTe[s][:, :, :], XTd.rearrange("(kc p) t -> p kc t", p=128)[:, :, tq:tq + BLK], writes=[XTe[s]])
                    tk.dma("sp", f"dx{s}", dob[s][:, :, :], DOd.rearrange("(h p) t -> p h t", p=128)[:, :, tq:tq + BLK], writes=[dob[s]])
                    tk.dma("sp", f"dx{s}", sob[s][:, :, :], SOd.rearrange("(c p) t -> p c t", p=128)[:, :, tq:tq + BLK], writes=[sob[s]])
                    tk.dma("sp", f"dx{s}", fob[s][:, :, :], FOd.rearrange("(c p) t -> p c t", p=128)[:, :, tq:tq + BLK], writes=[fob[s]])

                d_load(0)
                for bi_, tq in enumerate(djobs):
                    s = bi_ % 2
                    if bi_ + 1 < len(djobs):
                        d_load(bi_ + 1)
                    for n in range(8):
                        p = n % 2
                        for b in range(3):
                            bank = PS[b]
                            for kc in range(8):
                                tk.op("pe", lambda e, kc=kc, b=b, bank=bank, n=n: e.matmul(
                                    bank[:, :], Wg[:, kc, b * D + n * 128:b * D + (n + 1) * 128], XTe[s][:, kc, :],
                                    start=(kc == 0), stop=(kc == 7)), reads=[Wg, XTe[s]], writes=[bank])
                            tk.op("act", lambda e, b=b, bank=bank, p=p: e.activation(out=sgt[p][b][:, :], in_=bank[:, :], func=AF.Sigmoid),
                                  reads=[bank], writes=[sgt[p][b]])
                        brs = [(Wfo, fob[s], 2), (Wsg, sob[s], 2), (Wdf, dob[s], 4)]
                        for b, (Wb, xb_, nkc) in enumerate(brs):
                            bank = PS[3 + b]
                            for kc in range(nkc):
                                tk.op("pe", lambda e, kc=kc, bank=bank, Wb=Wb, xb_=xb_, nkc=nkc, n=n: e.matmul(
                                    bank[:, :], Wb[:, kc, n * 128:(n + 1) * 128], xb_[:, kc, :], start=(kc == 0), stop=(kc == nkc - 1)),
                                    reads=[Wb, xb_], writes=[bank])
                        tk.op("dve", lambda e, p=p: e.tensor_tensor(out=mt[0][:, :], in0=PS[3][:, :], in1=sgt[p][0][:, :], op=ALU.mult),
                              reads=[PS[3], sgt[p][0]], writes=[mt[0]])
                        tk.op("dve", lambda e, p=p: e.tensor_tensor(out=mt[1][:, :], in0=PS[4][:, :], in1=sgt[p][1][:, :], op=ALU.mult),
                              reads=[PS[4], sgt[p][1]], writes=[mt[1]])
                        tk.op("pool", lambda e: e.tensor_tensor(out=mt[0][:, :], in0=mt[0][:, :], in1=mt[1][:, :], op=ALU.add),
                              reads=[mt[0], mt[1]], writes=[mt[0]])
                        tk.op("dve", lambda e, p=p: e.tensor_tensor(out=mt[1][:, :], in0=PS[5][:, :], in1=sgt[p][2][:, :], op=ALU.mult),
                              reads=[PS[5], sgt[p][2]], writes=[mt[1]])
                        tk.op("pool", lambda e, n=n: e.tensor_tensor(out=mT[s][:, n, :], in0=mt[0][:, :], in1=mt[1][:, :], op=ALU.add),
                              reads=[mt[0], mt[1]], writes=[mT[s]])
                    tk.dma("pool", f"dst{s}", MTd.rearrange("(kc p) t -> p kc t", p=128)[:, :, tq:tq + BLK], mT[s][:, :, :], reads=[mT[s]])
            tk.barrier()

            phase_end()
            with ExitStack() as st6:
                Wo = sb(st6, [128, 8, D], BF16, "Wo")
                load_w(Wo, 0, w_out[l], D, 0, D, "ldw")
                g1 = sb(st6, [128, D], F32, "g1")
                b1 = sb(st6, [128, D], F32, "b1")
                bcast_load(g1, ln1_g[l, :])
                bcast_load(b1, ln1_b[l, :])
                moe = (l % 2 == 1)
                if moe:
                    Wr = sb(st6, [128, 8, NE], F32, "Wr")
                    tk.dma("sp", "ld_const", Wr[:, :, :], w_router[l // 2].rearrange("(kc p) e -> p kc e", p=128), writes=[Wr])
                MTb = [sb(st6, [128, 8, BLK], BF16, "MTb") for _ in range(2)]
                xsb = [sb(st6, [128, 4, D], F32, "xsb") for _ in range(2)]
                yb = [sb(st6, [128, D], F32, "yb") for _ in range(2)]
                x1s = [sb(st6, [128, D], F32, "x1s") for _ in range(2)]
                stt = [sb(st6, [128, 12], F32, "stt") for _ in range(2)]
                mvt = [sb(st6, [128, 4], F32, "mvt") for _ in range(2)]
                x1T32 = [sb(st6, [128, 8, 128], F32, "x1T32") for _ in range(2)]
                x1Tb = [sb(st6, [128, 8, BLK], BF16, "x1Tb") for _ in range(2)]
                rt = [sb(st6, [128, 64], F32, "rt") for _ in range(2)]
                ejobs = list(cjobs)

                def e_load(bi_):
                    s = bi_ % 2
                    tq = ejobs[bi_]
                    tk.dma("sp", f"ex{s}", MTb[s][:, :, :], MTd.rearrange("(kc p) t -> p kc t", p=128)[:, :, tq:tq + BLK], writes=[MTb[s]])
                    tk.dma("sp", f"ex{s}", xsb[s][:, :, :], Xsrc[tq:tq + BLK, :].rearrange("(j p) d -> p j d", p=128), writes=[xsb[s]])

                def stage_a(bi_, j):
                    s = bi_ % 2
                    tq = ejobs[bi_]
                    p = j % 2
                    if j == 0 and bi_ + 1 < len(ejobs):
                        e_load(bi_ + 1)
                    for nh in range(2):
                        bank = PS[(2 * j + nh) % 4]
                        for kc in range(8):
                            tk.op("pe", lambda e, kc=kc, bank=bank, nh=nh, j=j: e.matmul(
                                bank[:, :], MTb[s][:, kc, j * 128:(j + 1) * 128], Wo[:, kc, nh * 512:(nh + 1) * 512],
                                start=(kc == 0), stop=(kc == 7)), reads=[MTb[s], Wo], writes=[bank])
                        tk.op("dve", lambda e, bank=bank, nh=nh, j=j, p=p: e.scalar_tensor_tensor(
                            out=yb[p][:, nh * 512:(nh + 1) * 512], in0=xsb[s][:, j, nh * 512:(nh + 1) * 512], scalar=ALPHA,
                            in1=bank[:, :], op0=ALU.mult, op1=ALU.add), reads=[xsb[s], bank], writes=[yb[p]])
                    layer_norm_tile((stt[p], mvt[p]), yb[p], g1, b1, x1s[p])
                    tk.dma("pool", f"est{p}", X1d[tq + j * 128:tq + (j + 1) * 128, :], x1s[p][:, :], reads=[x1s[p]])

                def stage_b(bi_, j):
                    s = bi_ % 2
                    tq = ejobs[bi_]
                    p = j % 2
                    for g in range(2):
                        bank = PS[4 + g]
                        for k4 in range(4):
                            kc = g * 4 + k4
                            tk.op("pe", lambda e, bank=bank, k4=k4, kc=kc, p=p: e.transpose(
                                bank[:, k4 * 128:(k4 + 1) * 128], x1s[p][:, kc * 128:(kc + 1) * 128], ident[:, :]),
                                reads=[x1s[p], ident], writes=[bank])
                        if not moe:
                            tk.op("act", lambda e, bank=bank, g=g, j=j: e.activation(
                                out=x1Tb[s][:, g * 4:(g + 1) * 4, j * 128:(j + 1) * 128],
                                in_=bank[:, :].rearrange("p (k t) -> p k t", k=4), func=AF.Copy),
                                reads=[bank], writes=[x1Tb[s]])
                        else:
                            tk.op("act", lambda e, bank=bank, g=g, p=p: e.activation(
                                out=x1T32[p][:, g * 4:(g + 1) * 4, :],
                                in_=bank[:, :].rearrange("p (k t) -> p k t", k=4), func=AF.Copy),
                                reads=[bank], writes=[x1T32[p]])
                            tk.op("pool", lambda e, g=g, j=j, p=p: e.tensor_copy(
                                out=x1Tb[s][:, g * 4:(g + 1) * 4, j * 128:(j + 1) * 128],
                                in_=x1T32[p][:, g * 4:(g + 1) * 4, :]),
                                reads=[x1T32[p]], writes=[x1Tb[s]])
                    if moe:
                        bank = PS[6 + p]
                        for kc in range(8):
                            tk.op("pe", lambda e, kc=kc, bank=bank, p=p: e.matmul(bank[:, 0:NE], x1T32[p][:, kc, :], Wr[:, kc, :],
                                                                              start=(kc == 0), stop=(kc == 7)),
                                  reads=[x1T32[p], Wr], writes=[bank])
                        r = rt[p]
                        tk.op("dve", lambda e, bank=bank, r=r: e.tensor_copy(out=r[:, 0:8], in_=bank[:, 0:NE]), reads=[bank], writes=[r])
                        tk.op("dve", lambda e, r=r: e.reduce_max(out=r[:, 32:33], in_=r[:, 0:8], axis=AX.X), reads=[r], writes=[r])
                        tk.op("dve", lambda e, r=r: e.tensor_scalar(out=r[:, 8:16], in0=r[:, 0:8], scalar1=r[:, 32:33], scalar2=None,
                                                                    op0=ALU.is_equal), reads=[r], writes=[r])
                        tk.op("dve", lambda e, r=r: e.scalar_tensor_tensor(out=r[:, 16:24], in0=r[:, 8:16], scalar=-1e30, in1=r[:, 0:8],
                                                                           op0=ALU.mult, op1=ALU.add), reads=[r], writes=[r])
                        tk.op("dve", lambda e, r=r: e.reduce_max(out=r[:, 33:34], in_=r[:, 16:24], axis=AX.X), reads=[r], writes=[r])
                        tk.op("dve", lambda e, r=r: e.tensor_scalar(out=r[:, 24:32], in0=r[:, 16:24], scalar1=r[:, 33:34], scalar2=None,
                                                                    op0=ALU.is_equal), reads=[r], writes=[r])
                        tk.op("dve", lambda e, r=r: e.tensor_tensor(out=r[:, 34:35], in0=r[:, 33:34], in1=r[:, 32:33], op=ALU.subtract),
                              reads=[r], writes=[r])
                        tk.op("act", lambda e, r=r: e.activation(out=r[:, 35:36], in_=r[:, 34:35], func=AF.Sigmoid), reads=[r], writes=[r])
                        tk.op("dve", lambda e, r=r: e.tensor_scalar(out=r[:, 36:37], in0=r[:, 35:36], scalar1=-1.0, scalar2=1.0,
                                                                    op0=ALU.mult, op1=ALU.add), reads=[r], writes=[r])
                        tk.op("dve", lambda e, r=r: e.tensor_scalar(out=r[:, 48:56], in0=r[:, 8:16], scalar1=r[:, 36:37], scalar2=None,
                                                                    op0=ALU.mult), reads=[r], writes=[r])
                        tk.op("dve", lambda e, r=r: e.scalar_tensor_tensor(out=r[:, 40:48], in0=r[:, 24:32], scalar=r[:, 35:36], in1=r[:, 48:56],
                                                                           op0=ALU.mult, op1=ALU.add), reads=[r], writes=[r])
                        tk.dma("pool", f"est{p}", CMBd[tq + j * 128:tq + (j + 1) * 128, :], r[:, 40:48], reads=[r])
                    if j == 3:
                        tk.dma("pool", f"est2{s}", X1Td.rearrange("(kc p) t -> p kc t", p=128)[:, :, tq:tq + BLK], x1Tb[s][:, :, :], reads=[x1Tb[s]])

                e_load(0)
                tiles = [(bi_, j) for bi_ in range(len(ejobs)) for j in range(4)]
                for ti, (bi_, j) in enumerate(tiles):
                    stage_a(bi_, j)
                    if ti >= 1:
                        stage_b(*tiles[ti - 1])
                stage_b(*tiles[-1])
            tk.barrier()

            phase_end()
            moe = (l % 2 == 1)
            if moe:
                passes = [(moe_w_gate[l // 2, e_], moe_w_up[l // 2, e_], moe_w_down[l // 2, e_], e_) for e_ in range(NE)]
            else:
                passes = [(ffn_w_gate[l // 2], ffn_w_up[l // 2], ffn_w_down[l // 2], None)]
            HC = NFC // 2
            HCOLS = HC * 128
            hpasses = []
            for (wg_ap, wu_ap, wd_ap, ex) in passes:
                for half in range(2):
                    c0col = half * HCOLS
                    ncols = min(HCOLS, DFF - c0col)
                    hpasses.append((wg_ap, wu_ap, wd_ap, ex, c0col, ncols, (ex is None or ex == 0) and half == 0))
            with ExitStack() as st7:
                Wg2 = [sb(st7, [128, 8, HCOLS], BF16, "Wg2") for _ in range(2)]
                Wu2 = [sb(st7, [128, 8, HCOLS], BF16, "Wu2") for _ in range(2)]
                Wd2 = [sb(st7, [128, HC, D], BF16, "Wd2") for _ in range(2)]
                XTf = [sb(st7, [128, 8, BLK], BF16, "XTf") for _ in range(2)]
                cmbb = [sb(st7, [128, 4, NE], F32, "cmbb") for _ in range(2)]
                hT = sb(st7, [128, HC, BLK], BF16, "hT")
                sgf = [sb(st7, [128, BLK], F32, "sgf") for _ in range(2)]
                ysb = [sb(st7, [128, D], F32, "ysb") for _ in range(2)]
                fjobs = list(cjobs)

                def w_load_ops(pi):
                    wg_ap, wu_ap, wd_ap, ex, c0col, ncols, first = hpasses[pi]
                    ws = pi % 2
                    return (load_w_ops(Wg2[ws], 0, wg_ap, D, c0col, ncols, f"ldw{ws}")
                            + load_w_ops(Wu2[ws], 0, wu_ap, D, c0col, ncols, f"ldw{ws}")
                            + load_w_ops(Wd2[ws], 0, wd_ap[c0col:c0col + ncols, :], ncols, 0, D, f"ldw{ws}"))

                pend_w = []
                nblk_f = len(fjobs)

                gjobs = [(pi, bi_) for pi in range(len(hpasses)) for bi_ in range(len(fjobs))]

                def g_load(gi):
                    pi, bi_ = gjobs[gi]
                    ex = hpasses[pi][3]
                    s = gi % 2
                    tq = fjobs[bi_]
                    tk.dma("sp", f"gx{s}", XTf[s][:, :, :], X1Td.rearrange("(kc p) t -> p kc t", p=128)[:, :, tq:tq + BLK], writes=[XTf[s]])
                    if ex is not None:
                        tk.dma("sp", f"gx{s}", cmbb[s][:, :, :], CMBd[tq:tq + BLK, :].rearrange("(j p) e -> p j e", p=128), writes=[cmbb[s]])

                for op_ in w_load_ops(0):
                    op_()
                g_load(0)
                for gi, (pi, bi_) in enumerate(gjobs):
                    wg_ap, wu_ap, wd_ap, ex, c0col, ncols, first = hpasses[pi]
                    ws = pi % 2
                    Wgs, Wus, Wds = Wg2[ws], Wu2[ws], Wd2[ws]
                    nfc = (ncols + 127) // 128
                    s = gi % 2
                    tq = fjobs[bi_]
                    if gi + 1 < len(gjobs):
                        g_load(gi + 1)
                    for c in range(nfc):
                        rows = min(128, ncols - c * 128)
                        bg = PS[(2 * c) % 4]
                        bu = PS[(2 * c + 1) % 4]
                        for kc in range(8):
                            tk.op("pe", lambda e, kc=kc, bg=bg, c=c, rows=rows: e.matmul(bg[0:rows, :], Wgs[:, kc, c * 128:c * 128 + rows], XTf[s][:, kc, :],
                                                                                     start=(kc == 0), stop=(kc == 7)),
                                  reads=[Wgs, XTf[s]], writes=[bg])
                        for kc in range(8):
                            tk.op("pe", lambda e, kc=kc, bu=bu, c=c, rows=rows: e.matmul(bu[0:rows, :], Wus[:, kc, c * 128:c * 128 + rows], XTf[s][:, kc, :],
                                                                                     start=(kc == 0), stop=(kc == 7)),
                                  reads=[Wus, XTf[s]], writes=[bu])
                        sg_ = sgf[c % 2]
                        tk.op("act", lambda e, bg=bg, sg_=sg_, rows=rows: e.activation(out=sg_[0:rows, :], in_=bg[0:rows, :], func=AF.Silu),
                              reads=[bg], writes=[sg_])
                        tk.op("dve", lambda e, bu=bu, sg_=sg_, rows=rows, c=c: e.tensor_tensor(out=hT[0:rows, c, :], in0=bu[0:rows, :], in1=sg_[0:rows, :],
                                                                                           op=ALU.mult),
                              reads=[bu, sg_], writes=[hT])
                    if bi_ == 0 and pi + 1 < len(hpasses):
                        pend_w = w_load_ops(pi + 1)
                    if pend_w:
                        left = max(1, nblk_f - 2 - bi_)
                        for _ in range((len(pend_w) + left - 1) // left):
                            pend_w.pop(0)()
                    for j in range(4):
                        p = j % 2
                        for nh in range(2):
                            bank = PS[4 + (2 * j + nh) % 4]
                            for c in range(nfc):
                                rows = min(128, ncols - c * 128)
                                tk.op("pe", lambda e, c=c, bank=bank, rows=rows, nh=nh, j=j: e.matmul(
                                    bank[:, :], hT[0:rows, c, j * 128:(j + 1) * 128], Wds[0:rows, c, nh * 512:(nh + 1) * 512],
                                    start=(c == 0), stop=(c == nfc - 1)), reads=[hT, Wds], writes=[bank])
                            if ex is None:
                                tk.op("act", lambda e, bank=bank, nh=nh, p=p: e.activation(out=ysb[p][:, nh * 512:(nh + 1) * 512], in_=bank[:, :],
                                                                                       func=AF.Copy), reads=[bank], writes=[ysb[p]])
                            else:
                                tk.op("act", lambda e, bank=bank, nh=nh, p=p, j=j: e.activation(
                                    out=ysb[p][:, nh * 512:(nh + 1) * 512], in_=bank[:, :], func=AF.Copy, scale=cmbb[s][:, j, ex:ex + 1]),
                                    reads=[bank, cmbb[s]], writes=[ysb[p]])
                        ydst = Yd[tq + j * 128:tq + (j + 1) * 128, :]
                        if first:
                            tk.dma("pool", f"gst{p}", ydst, ysb[p][:, :], reads=[ysb[p]])
                        else:
                            tk.dma("pool", f"gst{p}", ydst, ysb[p][:, :], reads=[ysb[p]], accum_op=ALU.add)
            tk.barrier()

            phase_end()
            with ExitStack() as st8:
                g2 = sb(st8, [128, D], F32, "g2")
                b2 = sb(st8, [128, D], F32, "b2")
                bcast_load(g2, ln2_g[l, :])
                bcast_load(b2, ln2_b[l, :])
                xa = [sb(st8, [128, 4, D], F32, "xa") for _ in range(2)]
                ya = [sb(st8, [128, 4, D], F32, "ya") for _ in range(2)]
                tb_ = [sb(st8, [128, D], F32, "tb") for _ in range(2)]
                ob = [sb(st8, [128, D], F32, "ob") for _ in range(2)]
                stt2 = [sb(st8, [128, 12], F32, "stt2") for _ in range(2)]
                mvt2 = [sb(st8, [128, 4], F32, "mvt2") for _ in range(2)]
                hjobs4 = list(cjobs)
                dest = Xmid if l + 1 < NLAYER else out

                def h_load(bi_):
                    s = bi_ % 2
                    tq = hjobs4[bi_]
                    tk.dma("sp", f"hx{s}", xa[s][:, :, :], X1d[tq:tq + BLK, :].rearrange("(j p) d -> p j d", p=128), writes=[xa[s]])
                    tk.dma("sp", f"hx{s}", ya[s][:, :, :], Yd[tq:tq + BLK, :].rearrange("(j p) d -> p j d", p=128), writes=[ya[s]])

                h_load(0)
                for bi_, tq in enumerate(hjobs4):
                    s = bi_ % 2
                    if bi_ + 1 < len(hjobs4):
                        h_load(bi_ + 1)
                    for j in range(4):
                        p = j % 2
                        tk.op("dve", lambda e, j=j, p=p: e.scalar_tensor_tensor(out=tb_[p][:, :], in0=xa[s][:, j, :], scalar=ALPHA, in1=ya[s][:, j, :],
                                                                            op0=ALU.mult, op1=ALU.add), reads=[xa[s], ya[s]], writes=[tb_[p]])
                        layer_norm_tile((stt2[p], mvt2[p]), tb_[p], g2, b2, ob[p])
                        tk.dma("pool", f"hst{p}", dest[tq + j * 128:tq + (j + 1) * 128, :], ob[p][:, :], reads=[ob[p]])
            tk.barrier()
            phase_end()
      except _StopBuild:
        pass
    return nc


def _rope_tables(pos):
    inv = ROPE_THETA ** (-np.arange(0, 64, 2, dtype=np.float64) / 64.0)
    ang = pos.astype(np.float64)[None, :] * inv[:, None]
    c = np.cos(ang)
    s = np.sin(ang)
    d = np.arange(128) % 64
    j = d % 32
    sign = np.where(d < 32, -1.0, 1.0)
    return c[j].astype(np.float32), (s[j] * sign[:, None]).astype(np.float32)


def _dft_mats(pos_rows, pos_cols, S):
    k = (pos_rows.astype(np.int64)[:, None] * pos_cols.astype(np.int64)[None, :]) % S
    ang = (2.0 * np.pi / S) * k.astype(np.float64)
    sc = 1.0 / math.sqrt(S * 64.0)
    return (np.cos(ang) * sc).astype(ml_dtypes.bfloat16), (np.sin(ang) * sc).astype(ml_dtypes.bfloat16)


_PROG_CACHE = {}


def run_module(inputs, SS, SP, NLAYER=2, debug=None, stop_phase=None, run_cores=8):
    f32 = np.float32
    xp = np.asarray(inputs["x_prompt"], f32)
    xs = np.asarray(inputs["x_sample"], f32)
    ncores = 8
    HP = SP // 2
    assert xp.shape[0] * 2 == ncores and xs.shape[0] == 2 * ncores
    w_in = np.asarray(inputs["w_in"], f32)
    perm = np.arange(512).reshape(8, 2, 32)[:, ::-1, :].reshape(512)
    q = w_in[:, :, 768:1280]
    k = w_in[:, :, 1280:1792]
    w_in_e = np.ascontiguousarray(np.concatenate([w_in, q[:, :, perm], k[:, :, perm]], axis=2))
    sgu_w = np.asarray(inputs["sgu_w"], f32)
    sgu_b = np.asarray(inputs["sgu_b"], f32)
    sgu_wT = np.ascontiguousarray(np.transpose(sgu_w, (0, 1, 3, 2)))
    sgu_bt = np.ascontiguousarray(np.repeat(np.transpose(sgu_b, (0, 2, 1))[:, :, :, None], 64, axis=3).reshape(sgu_b.shape[0], 128, 256))
    ident = np.eye(128, dtype=f32)
    kk = np.arange(64)
    ang = 2.0 * np.pi * ((kk[:, None] * kk[None, :]) % 64) / 64.0
    bdc = np.zeros((128, 128), f32)
    bdsn = np.zeros((128, 128), f32)
    for g in range(2):
        bdc[g * 64:(g + 1) * 64, g * 64:(g + 1) * 64] = np.cos(ang)
        bdsn[g * 64:(g + 1) * 64, g * 64:(g + 1) * 64] = -np.sin(ang)
    pos_s = np.arange(SS)
    dcs, dss = _dft_mats(pos_s, pos_s, SS)
    shared = {
        "w_in_e": w_in_e, "sgu_wT": sgu_wT, "sgu_bt": sgu_bt, "ident": ident, "bdc": bdc, "bdsn": bdsn,
        "dft_cs": dcs, "dft_ss": dss,
    }
    for nm in ["w_fourier", "w_sgu", "w_diff", "w_out", "vn_g", "vn_b", "lam_q1", "lam_k1", "lam_q2", "lam_k2", "subln_g",
               "ln1_g", "ln1_b", "ln2_g", "ln2_b", "ffn_w_gate", "ffn_w_up", "ffn_w_down", "w_router",
               "moe_w_gate", "moe_w_up", "moe_w_down"]:
        shared[nm] = np.ascontiguousarray(np.asarray(inputs[nm], f32))
    par = {}
    for parity in range(2):
        pos_p = np.concatenate([np.arange(parity * HP, (parity + 1) * HP), np.arange((1 - parity) * HP, (2 - parity) * HP)])
        dcp, dsp = _dft_mats(pos_p, pos_p, SP)
        pos_all = np.concatenate([pos_s, pos_s, pos_p])
        rc, rs = _rope_tables(pos_all)
        par[parity] = dict(pos_p=pos_p, dft_cp=dcp, dft_sp=dsp, ropec=rc, ropes=rs)
    in_maps = []
    for c in range(ncores):
        parity = c % 2
        P = par[parity]
        xin = np.concatenate([xs[2 * c], xs[2 * c + 1], xp[c // 2][P["pos_p"]]], axis=0)
        m = dict(shared)
        m.update(xin=np.ascontiguousarray(xin), dft_cp=P["dft_cp"], dft_sp=P["dft_sp"], ropec=P["ropec"], ropes=P["ropes"])
        in_maps.append(m)
    key = (SS, SP, NLAYER, tuple(sorted(debug)) if debug else None, stop_phase)
    if key not in _PROG_CACHE:
        _PROG_CACHE[key] = build_program(SS, SP, NLAYER, debug, stop_phase)
    nc = _PROG_CACHE[key]
    res = run_bass_kernel_spmd(nc, in_maps[:run_cores], core_ids=list(range(run_cores)))
    outs = [r["out"] for r in res.results]
    outs = outs + [outs[0]] * (ncores - run_cores)
    y_sample = np.stack([outs[c][i * SS:(i + 1) * SS] for c in range(ncores) for i in range(2)], axis=0)
    y_prompt = np.stack([np.concatenate([outs[2 * b][2 * SS:2 * SS + HP], outs[2 * b + 1][2 * SS:2 * SS + HP]], axis=0)
                         for b in range(ncores // 2)], axis=0)
    if debug:
        return (y_prompt.astype(f32), y_sample.astype(f32)), res.results
    return (y_prompt.astype(f32), y_sample.astype(f32))


def kernel(**inputs):
    return run_module(inputs, SS=4096, SP=8192)
```

```python
import math
from contextlib import ExitStack

import numpy as np
import ml_dtypes

import concourse.bass as bass
import concourse.mybir as mybir
from concourse.bass_utils import run_bass_kernel_spmd

F32 = mybir.dt.float32
BF16 = mybir.dt.bfloat16
AF = mybir.ActivationFunctionType
ALU = mybir.AluOpType
AX = mybir.AxisListType

D = 1024
DFF = 2752
NE = 8
NFC = 22
DEPTH = 2
ALPHA = (2 * DEPTH) ** 0.25
LN_EPS = 1e-5
RMS_EPS = 1e-5
ROPE_THETA = 10000.0
C_F, C_U, C_V, C_Q, C_K, C_VA, C_G, C_QS, C_KS = 0, 256, 512, 768, 1280, 1792, 2304, 5376, 5888
NW = 6400
BLK = 512


class Buf:
    def __init__(self, ap, name=""):
        self.ap = ap
        self.name = name
        self.w = {}
        self.r = {}

    def __getitem__(self, idx):
        return self.ap[idx]


class Tracker:
    def __init__(self, nc, es):
        self.nc = nc
        self.es = es
        self.eng = {"pe": nc.tensor, "act": nc.scalar, "dve": nc.vector, "pool": nc.gpsimd, "sp": nc.sync}
        self.sems = {}
        self.cnt = {}
        self.waited = {k: {} for k in self.eng}
        self.nconst = 0
        for k in self.eng:
            self._mksem(k)

    def _mksem(self, key):
        self.sems[key] = self.es.enter_context(self.nc.semaphore("s_" + key))
        self.cnt[key] = 0

    def _wait(self, e, key, val):
        if key == "pe" and e == "pe":
            return
        if key not in self.eng:
            val = self.cnt[key]
        if self.waited[e].get(key, 0) >= val:
            return
        self.eng[e].wait_ge(self.sems[key], val)
        self.waited[e][key] = val

    def _deps(self, e, reads, writes):
        for b in reads:
            for k, v in b.w.items():
                self._wait(e, k, v)
        for b in writes:
            for k, v in b.w.items():
                self._wait(e, k, v)
            for k, v in b.r.items():
                self._wait(e, k, v)

    def _mark(self, key, val, reads, writes):
        for b in reads:
            if b.r.get(key, 0) < val:
                b.r[key] = val
        for b in writes:
            if b.w.get(key, 0) < val:
                b.w[key] = val

    def op(self, e, fn, reads=(), writes=()):
        self._deps(e, reads, writes)
        ins = fn(self.eng[e])
        ins.then_inc(self.sems[e], 1)
        self.cnt[e] += 1
        self._mark(e, self.cnt[e], reads, writes)

    def dma(self, q, semkey, out, in_, reads=(), writes=(), **kw):
        if semkey == "ld_const":
            semkey = f"ldc{self.nconst}"
            self.nconst += 1
        if semkey not in self.sems:
            self._mksem(semkey)
        self._deps(q, reads, writes)
        ins = self.eng[q].dma_start(out=out, in_=in_, **kw)
        ins.then_inc(self.sems[semkey], 16)
        self.cnt[semkey] += 16
        self._mark(semkey, self.cnt[semkey], reads, writes)

    def barrier(self):
        self.nconst = 0
        for e in self.eng:
            for k in self.sems:
                if self.cnt[k] > 0:
                    self._wait(e, k, self.cnt[k])


class _StopBuild(Exception):
    pass


def build_program(SS, SP, NLAYER=2, debug=None, stop_phase=None):
    T0 = 2 * SS + SP
    HP = SP // 2
    T1 = 2 * SS + HP
    units = [(0, SS), (SS, SS), (2 * SS, SP)]

    nc = bass.Bass("TRN2", target_bir_lowering=False)

    def din(name, shape, dt=F32):
        return nc.dram_tensor(name, list(shape), dt, kind="ExternalInput").ap()

    def dscr(name, shape, dt):
        kind = {}
        if debug and name in debug:
            kind = dict(kind="ExternalOutput")
        return nc.dram_tensor(name, list(shape), dt, **kind).ap()

    xin = din("xin", [T0, D])
    w_in = din("w_in_e", [2, D, NW])
    w_fourier = din("w_fourier", [2, 256, D])
    w_sgu = din("w_sgu", [2, 256, D])
    w_diff = din("w_diff", [2, 512, D])
    w_out = din("w_out", [2, D, D])
    vn_g = din("vn_g", [2, 256])
    vn_b = din("vn_b", [2, 256])
    sgu_wT = din("sgu_wT", [2, 4, 128, 128])
    sgu_bt = din("sgu_bt", [2, 128, 256])
    lam_q1 = din("lam_q1", [2, 64])
    lam_k1 = din("lam_k1", [2, 64])
    lam_q2 = din("lam_q2", [2, 64])
    lam_k2 = din("lam_k2", [2, 64])
    subln_g = din("subln_g", [2, 128])
    ln1_g = din("ln1_g", [2, D])
    ln1_b = din("ln1_b", [2, D])
    ln2_g = din("ln2_g", [2, D])
    ln2_b = din("ln2_b", [2, D])
    ffn_w_gate = din("ffn_w_gate", [1, D, DFF])
    ffn_w_up = din("ffn_w_up", [1, D, DFF])
    ffn_w_down = din("ffn_w_down", [1, DFF, D])
    w_router = din("w_router", [1, D, NE])
    moe_w_gate = din("moe_w_gate", [1, NE, D, DFF])
    moe_w_up = din("moe_w_up", [1, NE, D, DFF])
    moe_w_down = din("moe_w_down", [1, NE, DFF, D])
    ident_d = din("ident", [128, 128])
    ropec = din("ropec", [128, T0])
    ropes = din("ropes", [128, T0])
    dft_cs = din("dft_cs", [SS, SS], BF16)
    dft_ss = din("dft_ss", [SS, SS], BF16)
    dft_cp = din("dft_cp", [SP, SP], BF16)
    dft_sp = din("dft_sp", [SP, SP], BF16)
    bdc_d = din("bdc", [128, 128])
    bdsn_d = din("bdsn", [128, 128])
    out = nc.dram_tensor("out", [T1, D], F32, kind="ExternalOutput").ap()

    XTd = dscr("XTd", [D, T0], BF16)
    KTd = dscr("KTd", [512, T0], BF16)
    Vd = dscr("Vd", [4, 128, T0 // 128, 128], BF16)
    Fd = dscr("Fd", [T0, 256], BF16)
    DOd = dscr("DOd", [512, T0], BF16)
    SOd = dscr("SOd", [256, T0], BF16)
    FOd = dscr("FOd", [256, T0], BF16)
    MTd = dscr("MTd", [D, T0], BF16)
    X1d = dscr("X1d", [T0, D], F32)
    X1Td = dscr("X1Td", [D, T0], BF16)
    CMBd = dscr("CMBd", [T0, NE], F32)
    Yd = dscr("Yd", [T0, D], F32)
    Xmid = dscr("Xmid", [T0, D], F32)

    es = ExitStack()
    with es:
      try:
        tk = Tracker(nc, es)
        uid = [0]

        def sb(stack, shape, dt, name):
            uid[0] += 1
            t = stack.enter_context(nc.sbuf_tensor(f"{name}_{uid[0]}", list(shape), dt))
            return Buf(t, name)

        PS = [Buf(es.enter_context(nc.psum_tensor(f"psum{i}", [128, 512], F32)), f"ps{i}") for i in range(8)]

        ident = sb(es, [128, 128], F32, "ident")
        tk.dma("sp", "ld_const", ident[:], ident_d[:, :], writes=[ident])
        ones_bf = sb(es, [128, 128], BF16, "ones_bf")
        onesm_bf = sb(es, [128, 128], BF16, "onesm_bf")
        tk.op("dve", lambda e: e.memset(ones_bf[:], 1.0), writes=[ones_bf])
        tk.op("dve", lambda e: e.memset(onesm_bf[:], 1.0 / 128.0), writes=[onesm_bf])
        bdc = sb(es, [128, 128], BF16, "bdc")
        bdsn = sb(es, [128, 128], BF16, "bdsn")
        tk.dma("pool", "ld_constp", bdc[:], bdc_d[:, :], writes=[bdc])
        tk.dma("pool", "ld_constp", bdsn[:], bdsn_d[:, :], writes=[bdsn])

        def load_w_ops(dst, dst_c0, src2d, K, c0, ncols, semkey):
            ops = []
            nk = (K + 127) // 128
            for kc in range(nk):
                rows = min(128, K - kc * 128)
                cc = 0
                while cc < ncols:
                    n = min(2048, ncols - cc)
                    ops.append(lambda kc=kc, rows=rows, cc=cc, n=n: tk.dma(
                        "pool", semkey, dst[0:rows, kc, dst_c0 + cc:dst_c0 + cc + n],
                        src2d[kc * 128:kc * 128 + rows, c0 + cc:c0 + cc + n], writes=[dst]))
                    cc += n
            return ops

        def load_w(dst, dst_c0, src2d, K, c0, ncols, semkey):
            for op_ in load_w_ops(dst, dst_c0, src2d, K, c0, ncols, semkey):
                op_()

        def layer_norm_tile(stack_bufs, y, gtab, btab, outb, l_eng="pool"):
            st, mv = stack_bufs
            tk.op("dve", lambda e: e.bn_stats(out=st[:, 0:6], in_=y[:, 0:512]), reads=[y], writes=[st])
            tk.op("dve", lambda e: e.bn_stats(out=st[:, 6:12], in_=y[:, 512:1024]), reads=[y], writes=[st])
            tk.op("dve", lambda e: e.bn_aggr(out=mv[:, 0:2], in_=st[:, 0:12]), reads=[st], writes=[mv])
            tk.op("dve", lambda e: e.tensor_scalar(out=mv[:, 2:3], in0=mv[:, 1:2], scalar1=LN_EPS, scalar2=None,
                                                   op0=ALU.add), reads=[mv], writes=[mv])
            tk.op("act", lambda e: e.activation(out=mv[:, 3:4], in_=mv[:, 2:3], func=AF.Sqrt), reads=[mv], writes=[mv])
            tk.op("dve", lambda e: e.reciprocal(out=mv[:, 2:3], in_=mv[:, 3:4]), reads=[mv], writes=[mv])
            tk.op("dve", lambda e: e.tensor_scalar(out=y[:, :], in0=y[:, :], scalar1=mv[:, 0:1], scalar2=mv[:, 2:3],
                                                   op0=ALU.subtract, op1=ALU.mult), reads=[y, mv], writes=[y])
            tk.op("dve", lambda e: e.tensor_tensor(out=y[:, :], in0=y[:, :], in1=gtab[:, :], op=ALU.mult),
                  reads=[y, gtab], writes=[y])
            tk.op(l_eng, lambda e: e.tensor_tensor(out=outb[:, :], in0=y[:, :], in1=btab[:, :], op=ALU.add),
                  reads=[y, btab], writes=[outb])

        def bcast_load(dst, vec_ap, semkey="ld_const"):
            tk.dma("sp", semkey, dst[:, :], vec_ap.partition_broadcast(128), writes=[dst])

        phc = [0]

        def phase_end():
            phc[0] += 1
            if stop_phase is not None and phc[0] >= stop_phase:
                raise _StopBuild()

        for l in range(NLAYER):
            Xsrc = xin if l == 0 else Xmid
            TQ = T0 if l == 0 else T1
            qunits = [(t0, Sk, (Sk if l == 0 else min(Sk, SS if t0 < 2 * SS else HP))) for (t0, Sk) in units]
            lambda_init = 0.8 - 0.6 * math.exp(-0.3 * l)
            W2 = w_in[l]

            with ExitStack() as st1:
                Wk = sb(st1, [128, 8, 1792], BF16, "Wk")
                load_w(Wk, 0, W2, D, C_K, 512, "ldw")
                load_w(Wk, 512, W2, D, C_KS, 512, "ldw")
                load_w(Wk, 1024, W2, D, C_VA, 512, "ldw")
                load_w(Wk, 1536, W2, D, C_F, 256, "ldw")
                xs = [sb(st1, [128, 4, D], F32, "xs") for _ in range(2)]
                cs = [sb(st1, [128, 2, BLK], F32, "cs") for _ in range(2)]
                XT = [sb(st1, [128, 8, BLK], BF16, "XT") for _ in range(2)]
                kst = [sb(st1, [128, 4, BLK], BF16, "kst") for _ in range(2)]
                vst = [sb(st1, [128, 4, 4, 128], BF16, "vst") for _ in range(2)]
                fst = [sb(st1, [128, 4, 256], BF16, "fst") for _ in range(2)]
                tmp = [sb(st1, [128, BLK], F32, "tmp") for _ in range(4)]
                nb = T0 // BLK

                def p1_load(bi):
                    s = bi % 2
                    t0 = bi * BLK
                    tk.dma("sp", f"p1x{s}", xs[s][:, :, :], Xsrc[t0:t0 + BLK, :].rearrange("(j p) d -> p j d", p=128),
                           writes=[xs[s]])
                    tk.dma("sp", f"p1x{s}", cs[s][:, 0, :], ropec[:, t0:t0 + BLK], writes=[cs[s]])
                    tk.dma("sp", f"p1x{s}", cs[s][:, 1, :], ropes[:, t0:t0 + BLK], writes=[cs[s]])

                p1_load(0)
                for bi in range(nb):
                    s = bi % 2
                    t0 = bi * BLK
                    if bi + 1 < nb:
                        p1_load(bi + 1)
                    for kc in range(8):
                        bank = PS[kc % 2]
                        for j in range(4):
                            tk.op("pe", lambda e, j=j, kc=kc, bank=bank: e.transpose(
                                bank[:, j * 128:(j + 1) * 128], xs[s][:, j, kc * 128:(kc + 1) * 128], ident[:, :]),
                                reads=[xs[s], ident], writes=[bank])
                        if kc % 2 == 0:
                            tk.op("act", lambda e, kc=kc, bank=bank: e.activation(out=XT[s][:, kc, :], in_=bank[:, :], func=AF.Copy),
                                  reads=[bank], writes=[XT[s]])
                        else:
                            tk.op("dve", lambda e, kc=kc, bank=bank: e.tensor_copy(out=XT[s][:, kc, :], in_=bank[:, :]),
                                  reads=[bank], writes=[XT[s]])
                    tk.dma("pool", f"p1s{s}", XTd.rearrange("(kc p) t -> p kc t", p=128)[:, :, t0:t0 + BLK], XT[s][:, :, :],
                           reads=[XT[s]])
                    for h in range(4):
                        A = PS[2 + 2 * (h % 2)]
                        B = PS[3 + 2 * (h % 2)]
                        for kc in range(8):
                            tk.op("pe", lambda e, kc=kc, A=A: e.matmul(A[:, :], Wk[:, kc, h * 128:(h + 1) * 128], XT[s][:, kc, :],
                                                                    start=(kc == 0), stop=(kc == 7)),
                                  reads=[Wk, XT[s]], writes=[A])
                        for kc in range(8):
                            tk.op("pe", lambda e, kc=kc, B=B: e.matmul(B[:, :], Wk[:, kc, 512 + h * 128:512 + (h + 1) * 128], XT[s][:, kc, :],
                                                                    start=(kc == 0), stop=(kc == 7)),
                                  reads=[Wk, XT[s]], writes=[B])
                        ta, tb = tmp[2 * (h % 2)], tmp[2 * (h % 2) + 1]
                        tk.op("dve", lambda e, A=A, ta=ta: e.tensor_tensor(out=ta[:, :], in0=A[:, :], in1=cs[s][:, 0, :], op=ALU.mult),
                              reads=[A, cs[s]], writes=[ta])
                        tk.op("dve", lambda e, B=B, tb=tb: e.tensor_tensor(out=tb[:, :], in0=B[:, :], in1=cs[s][:, 1, :], op=ALU.mult),
                              reads=[B, cs[s]], writes=[tb])
                        tk.op("pool", lambda e, ta=ta, tb=tb: e.tensor_tensor(out=kst[s][:, h, :], in0=ta[:, :], in1=tb[:, :], op=ALU.add),
                              reads=[ta, tb], writes=[kst[s]])
                    tk.dma("pool", f"p1s{s}", KTd.rearrange("(h p) t -> p h t", p=128)[:, :, t0:t0 + BLK], kst[s][:, :, :],
                           reads=[kst[s]])
                    for st_ in range(8):
                        j = st_ % 4
                        bank = PS[6 + st_ % 2]
                        if st_ < 4:
                            for kc in range(8):
                                tk.op("pe", lambda e, kc=kc, bank=bank, j=j: e.matmul(bank[:, :], XT[s][:, kc, j * 128:(j + 1) * 128],
                                                                                  Wk[:, kc, 1024:1536], start=(kc == 0), stop=(kc == 7)),
                                      reads=[Wk, XT[s]], writes=[bank])
                            tk.op("act", lambda e, bank=bank, j=j: e.activation(
                                out=vst[s][:, :, j, :], in_=bank[:, :].rearrange("p (h e) -> p h e", h=4), func=AF.Copy),
                                reads=[bank], writes=[vst[s]])
                        else:
                            for kc in range(8):
                                tk.op("pe", lambda e, kc=kc, bank=bank, j=j: e.matmul(bank[:, 0:256], XT[s][:, kc, j * 128:(j + 1) * 128],
                                                                                  Wk[:, kc, 1536:1792], start=(kc == 0), stop=(kc == 7)),
                                      reads=[Wk, XT[s]], writes=[bank])
                            tk.op("dve", lambda e, bank=bank, j=j: e.tensor_copy(out=fst[s][:, j, :], in_=bank[:, 0:256]),
                                  reads=[bank], writes=[fst[s]])
                    c0 = t0 // 128
                    tk.dma("pool", f"p1s{s}", Vd[:, :, c0:c0 + 4, :].rearrange("h p c e -> p h c e"), vst[s][:, :, :, :],
                           reads=[vst[s]])
                    tk.dma("pool", f"p1s{s}", Fd[t0:t0 + BLK, :].rearrange("(j p) f -> p j f", p=128), fst[s][:, :, :],
                           reads=[fst[s]])
            tk.barrier()

            phase_end()
            with ExitStack() as st2:
                Wq = sb(st2, [128, 8, 1024], BF16, "Wq")
                load_w(Wq, 0, W2, D, C_Q, 512, "ldw")
                load_w(Wq, 512, W2, D, C_QS, 512, "ldw")
                lv = [sb(st2, [128, 64], F32, "lv") for _ in range(4)]
                bcast_load(lv[0], lam_q1[l, :])
                bcast_load(lv[1], lam_k1[l, :])
                bcast_load(lv[2], lam_q2[l, :])
                bcast_load(lv[3], lam_k2[l, :])
                sm = sb(st2, [128, 8], F32, "sm")
                lt = sb(st2, [128, 64], F32, "lt")
                tk.op("dve", lambda e: e.tensor_tensor(out=lt[:, :], in0=lv[0][:, :], in1=lv[1][:, :], op=ALU.mult),
                      reads=[lv[0], lv[1]], writes=[lt])
                tk.op("dve", lambda e: e.reduce_sum(out=sm[:, 0:1], in_=lt[:, :], axis=AX.X), reads=[lt], writes=[sm])
                tk.op("dve", lambda e: e.tensor_tensor(out=lt[:, :], in0=lv[2][:, :], in1=lv[3][:, :], op=ALU.mult),
                      reads=[lv[2], lv[3], sm], writes=[lt])
                tk.op("dve", lambda e: e.reduce_sum(out=sm[:, 1:2], in_=lt[:, :], axis=AX.X), reads=[lt], writes=[sm])
                tk.op("act", lambda e: e.activation(out=sm[:, 2:4], in_=sm[:, 0:2], func=AF.Exp), reads=[sm], writes=[sm])
                tk.op("dve", lambda e: e.tensor_tensor(out=sm[:, 4:5], in0=sm[:, 3:4], in1=sm[:, 2:3], op=ALU.subtract),
                      reads=[sm], writes=[sm])
                tk.op("dve", lambda e: e.tensor_scalar(out=sm[:, 5:6], in0=sm[:, 4:5], scalar1=-lambda_init, scalar2=None,
                                                       op0=ALU.add), reads=[sm], writes=[sm])
                negl = sm
                gcol = sb(st2, [128, 2], F32, "gcol")
                tk.dma("sp", "ld_const", gcol[:, 0:1], subln_g[l, :].rearrange("(p o) -> p o", o=1), writes=[gcol])
                tk.op("dve", lambda e: e.tensor_scalar(out=gcol[:, 1:2], in0=gcol[:, 0:1], scalar1=(1.0 - lambda_init),
                                                       scalar2=None, op0=ALU.mult), reads=[gcol], writes=[gcol])

                SKM = max(Sk for _, Sk in units)
                XTb = [sb(st2, [128, 8, BLK], BF16, "XTb") for _ in range(2)]
                csb = [sb(st2, [128, 2, BLK], F32, "csb") for _ in range(2)]
                QT = [sb(st2, [128, 4, 2, BLK], BF16, "QT") for _ in range(2)]
                for s_ in range(2):
                    tk.op("pool", lambda e, s_=s_: e.memset(QT[s_][:, :, :, :], 0.0), writes=[QT[s_]])
                Kh = [sb(st2, [128, SKM], BF16, "Kh") for _ in range(2)]
                Vh = [sb(st2, [128, SKM // 128, 128], BF16, "Vh") for _ in range(2)]
                NPT = 6
                pT = [sb(st2, [128, BLK], BF16, "pT") for _ in range(NPT)]
                tmpq = [sb(st2, [128, BLK], F32, "tmpq") for _ in range(2)]
                ep_r = [sb(st2, [128, BLK], F32, "ep_r") for _ in range(2)]
                ep_t = [sb(st2, [128, BLK], F32, "ep_t") for _ in range(2)]
                ep_a = [sb(st2, [128, BLK], F32, "ep_a") for _ in range(2)]
                ep_sq = [sb(st2, [128, BLK], BF16, "ep_sq") for _ in range(2)]
                ep_sd = [sb(st2, [128, BLK], F32, "ep_sd") for _ in range(2)]
                doT = [sb(st2, [128, 4, BLK], BF16, "doT") for _ in range(2)]

                jobs = []
                for (t0u, Sk, Sq) in qunits:
                    for qb in range(Sq // BLK):
                        jobs.append((t0u, Sk, t0u + qb * BLK))
                hjobs = [(ji, h) for ji in range(len(jobs)) for h in range(4)]

                def a_load_q(ji):
                    s = ji % 2
                    _, _, tq = jobs[ji]
                    tk.dma("sp", f"ax{s}", XTb[s][:, :, :], XTd.rearrange("(kc p) t -> p kc t", p=128)[:, :, tq:tq + BLK],
                           writes=[XTb[s]])
                    tk.dma("sp", f"ax{s}", csb[s][:, 0, :], ropec[:, tq:tq + BLK], writes=[csb[s]])
                    tk.dma("sp", f"ax{s}", csb[s][:, 1, :], ropes[:, tq:tq + BLK], writes=[csb[s]])

                def a_load_kv(hi):
                    ji, h = hjobs[hi]
                    t0u, Sk, _ = jobs[ji]
                    s = hi % 2
                    tk.dma("sp", f"akv{s}", Kh[s][:, 0:Sk], KTd[h * 128:(h + 1) * 128, t0u:t0u + Sk], writes=[Kh[s]])
                    c0 = t0u // 128
                    tk.dma("sp", f"akv{s}", Vh[s][:, 0:Sk // 128, :], Vd[h, :, c0:c0 + Sk // 128, :], writes=[Vh[s]])

                pending_b = []

                def epi_a(hi, O, Ssum):
                    p = hi % 2
                    for m in range(2):
                        tk.op("dve", lambda e, m=m: e.reciprocal(out=ep_r[m][:, :], in_=Ssum[m][:, :]),
                              reads=[Ssum[m]], writes=[ep_r[m]])
                        tk.op("dve", lambda e, m=m: e.tensor_tensor(out=ep_t[m][:, :], in0=O[m][:, :], in1=ep_r[m][:, :], op=ALU.mult),
                              reads=[O[m], ep_r[m]], writes=[ep_t[m]])
                    tk.op("dve", lambda e: e.scalar_tensor_tensor(out=ep_a[p][:, :], in0=ep_t[1][:, :], scalar=negl[:, 5:6],
                                                                  in1=ep_t[0][:, :], op0=ALU.mult, op1=ALU.add),
                          reads=[ep_t[0], ep_t[1], negl], writes=[ep_a[p]])
                    tk.op("act", lambda e: e.activation(out=ep_sq[p][:, :], in_=ep_a[p][:, :], func=AF.Square),
                          reads=[ep_a[p]], writes=[ep_sq[p]])

                def epi_b(hi, bank):
                    p = hi % 2
                    ji, h = hjobs[hi]
                    s = ji % 2
                    tk.op("pe", lambda e: e.matmul(bank[:, :], onesm_bf[:, :], ep_sq[p][:, :], start=True, stop=True),
                          reads=[onesm_bf, ep_sq[p]], writes=[bank])
                    tk.op("dve", lambda e: e.tensor_scalar(out=ep_sd[p][:, :], in0=bank[:, :], scalar1=RMS_EPS, scalar2=None,
                                                           op0=ALU.add), reads=[bank], writes=[ep_sd[p]])
                    tk.op("act", lambda e: e.activation(out=ep_sd[p][:, :], in_=ep_sd[p][:, :], func=AF.Sqrt),
                          reads=[ep_sd[p]], writes=[ep_sd[p]])
                    tk.op("dve", lambda e: e.reciprocal(out=ep_sd[p][:, :], in_=ep_sd[p][:, :]), reads=[ep_sd[p]], writes=[ep_sd[p]])
                    tk.op("dve", lambda e: e.tensor_tensor(out=ep_a[p][:, :], in0=ep_a[p][:, :], in1=ep_sd[p][:, :], op=ALU.mult),
                          reads=[ep_a[p], ep_sd[p]], writes=[ep_a[p]])
                    tk.op("dve", lambda e: e.tensor_scalar(out=doT[s][:, h, :], in0=ep_a[p][:, :], scalar1=gcol[:, 1:2], scalar2=None,
                                                           op0=ALU.mult), reads=[ep_a[p], gcol], writes=[doT[s]])
                    if h == 3:
                        _, _, tq = jobs[ji]
                        tk.dma("pool", f"ast{s}", DOd.rearrange("(h p) t -> p h t", p=128)[:, :, tq:tq + BLK], doT[s][:, :, :],
                               reads=[doT[s]])

                a_load_q(0)
                a_load_kv(0)
                for hi, (ji, h) in enumerate(hjobs):
                    t0u, Sk, tq = jobs[ji]
                    s = ji % 2
                    ks = hi % 2
                    if h == 0:
                        if ji + 1 < len(jobs):
                            a_load_q(ji + 1)
                        for hh in range(4):
                            A = PS[4 + 2 * (hh % 2)]
                            B = PS[5 + 2 * (hh % 2)]
                            for kc in range(8):
                                tk.op("pe", lambda e, kc=kc, A=A, hh=hh: e.matmul(A[:, :], Wq[:, kc, hh * 128:(hh + 1) * 128], XTb[s][:, kc, :],
                                                                                start=(kc == 0), stop=(kc == 7)),
                                      reads=[Wq, XTb[s]], writes=[A])
                            for kc in range(8):
                                tk.op("pe", lambda e, kc=kc, B=B, hh=hh: e.matmul(B[:, :], Wq[:, kc, 512 + hh * 128:512 + (hh + 1) * 128], XTb[s][:, kc, :],
                                                                                start=(kc == 0), stop=(kc == 7)),
                                      reads=[Wq, XTb[s]], writes=[B])
                            tk.op("dve", lambda e, A=A: e.tensor_tensor(out=tmpq[0][:, :], in0=A[:, :], in1=csb[s][:, 0, :], op=ALU.mult),
                                  reads=[A, csb[s]], writes=[tmpq[0]])
                            tk.op("dve", lambda e, B=B: e.tensor_tensor(out=tmpq[1][:, :], in0=B[:, :], in1=csb[s][:, 1, :], op=ALU.mult),
                                  reads=[B, csb[s]], writes=[tmpq[1]])
                            for m_ in range(2):
                                tk.op("pool", lambda e, hh=hh, m_=m_: e.tensor_tensor(
                                    out=QT[s][m_ * 64:(m_ + 1) * 64, hh, m_, :], in0=tmpq[0][m_ * 64:(m_ + 1) * 64, :],
                                    in1=tmpq[1][m_ * 64:(m_ + 1) * 64, :], op=ALU.add),
                                    reads=[tmpq[0], tmpq[1]], writes=[QT[s]])
                    if hi + 1 < len(hjobs):
                        a_load_kv(hi + 1)
                    O = [PS[0], PS[1]]
                    Ssum = [PS[2], PS[3]]
                    nkc = Sk // 128
                    steps = [(kc, m) for kc in range(nkc) for m in range(2)]
                    LA = 2

                    def qk(i):
                        kc, m = steps[i]
                        sc = PS[4 + i % 4]
                        tk.op("pe", lambda e: e.matmul(sc[:, :], Kh[ks][:, kc * 128:(kc + 1) * 128],
                                                       QT[s][:, h, m, :], start=True, stop=True),
                              reads=[Kh[ks], QT[s]], writes=[sc])
                        pt = pT[i % NPT]
                        tk.op("act", lambda e: e.activation(out=pt[:, :], in_=sc[:, :], func=AF.Exp, scale=0.125),
                              reads=[sc], writes=[pt])

                    for i in range(min(LA, len(steps))):
                        qk(i)
                    for i, (kc, m) in enumerate(steps):
                        if i + LA < len(steps):
                            qk(i + LA)
                        pt = pT[i % NPT]
                        tk.op("pe", lambda e: e.matmul(O[m][:, :], Vh[ks][:, kc, :], pt[:, :], start=(kc == 0), stop=(kc == nkc - 1)),
                              reads=[Vh[ks], pt], writes=[O[m]])
                        tk.op("pe", lambda e: e.matmul(Ssum[m][:, :], ones_bf[:, :], pt[:, :], start=(kc == 0), stop=(kc == nkc - 1)),
                              reads=[ones_bf, pt], writes=[Ssum[m]])
                        if i == 8 and pending_b:
                            pending_b.pop(0)()
                    while pending_b:
                        pending_b.pop(0)()
                    epi_a(hi, O, Ssum)
                    pending_b.append(lambda hi=hi: epi_b(hi, PS[4 + (hi % 2)]))
                while pending_b:
                    pending_b.pop(0)()
            tk.barrier()

            phase_end()
            with ExitStack() as st3:
                GRP = 4
                Fg = [sb(st3, [128, GRP, 256], BF16, "Fg") for _ in range(2)]
                Cg = [sb(st3, [128, GRP, BLK], BF16, "Cg") for _ in range(2)]
                Sg = [sb(st3, [128, GRP, BLK], BF16, "Sg") for _ in range(2)]
                cfs = [sb(st3, [128, 4, BLK], BF16, "cfs") for _ in range(2)]
                foT = [sb(st3, [128, 2, BLK], BF16, "foT") for _ in range(2)]
                gj = []
                bjobs = []
                for (t0u, Sk, Sq) in qunits:
                    for qb in range(Sq // BLK):
                        bjobs.append((t0u, Sk, qb))
                for bi_, (t0u, Sk, qb) in enumerate(bjobs):
                    for g in range(Sk // (128 * GRP)):
                        gj.append((bi_, t0u, Sk, qb, g))

                def f_load(gi):
                    bi_, t0u, Sk, qb, g = gj[gi]
                    s = gi % 2
                    r0 = g * 128 * GRP
                    Cm, Sm = (dft_cs, dft_ss) if Sk == SS and t0u < 2 * SS else (dft_cp, dft_sp)
                    tk.dma("pool", f"flf{s}", Fg[s][:, :, :], Fd[t0u + r0:t0u + r0 + 128 * GRP, :].rearrange("(c p) f -> p c f", p=128),
                           writes=[Fg[s]])
                    tk.dma("sp", f"flc{s}", Cg[s][:, :, :], Cm[r0:r0 + 128 * GRP, qb * BLK:(qb + 1) * BLK].rearrange("(c p) n -> p c n", p=128),
                           writes=[Cg[s]])
                    tk.dma("act", f"fls{s}", Sg[s][:, :, :], Sm[r0:r0 + 128 * GRP, qb * BLK:(qb + 1) * BLK].rearrange("(c p) n -> p c n", p=128),
                           writes=[Sg[s]])

                f_load(0)
                for gi, (bi_, t0u, Sk, qb, g) in enumerate(gj):
                    s = gi % 2
                    if gi + 1 < len(gj):
                        f_load(gi + 1)
                    ng = Sk // (128 * GRP)
                    for c in range(GRP):
                        first = (g == 0 and c == 0)
                        last = (g == ng - 1 and c == GRP - 1)
                        for a in range(4):
                            mat = Cg[s] if a < 2 else Sg[s]
                            tk.op("pe", lambda e, a=a, c=c, mat=mat: e.matmul(PS[a][:, :], Fg[s][:, c, (a % 2) * 128:(a % 2 + 1) * 128], mat[:, c, :],
                                                                          start=first, stop=last),
                                  reads=[Fg[s], mat], writes=[PS[a]])
                    if g == ng - 1:
                        bs = bi_ % 2
                        tq = t0u + qb * BLK
                        for a in range(4):
                            if a % 2 == 0:
                                tk.op("act", lambda e, a=a: e.activation(out=cfs[bs][:, a, :], in_=PS[a][:, :], func=AF.Copy),
                                      reads=[PS[a]], writes=[cfs[bs]])
                            else:
                                tk.op("dve", lambda e, a=a: e.tensor_copy(out=cfs[bs][:, a, :], in_=PS[a][:, :]),
                                      reads=[PS[a]], writes=[cfs[bs]])
                        for cc in range(2):
                            bank = PS[4 + cc + 2 * (bi_ % 2)]
                            tk.op("pe", lambda e, cc=cc, bank=bank: e.matmul(bank[:, :], bdc[:, :], cfs[bs][:, cc, :], start=True, stop=False),
                                  reads=[bdc, cfs[bs]], writes=[bank])
                            tk.op("pe", lambda e, cc=cc, bank=bank: e.matmul(bank[:, :], bdsn[:, :], cfs[bs][:, 2 + cc, :], start=False, stop=True),
                                  reads=[bdsn, cfs[bs]], writes=[bank])
                            if cc == 0:
                                tk.op("act", lambda e, cc=cc, bank=bank: e.activation(out=foT[bs][:, cc, :], in_=bank[:, :], func=AF.Copy),
                                      reads=[bank], writes=[foT[bs]])
                            else:
                                tk.op("dve", lambda e, cc=cc, bank=bank: e.tensor_copy(out=foT[bs][:, cc, :], in_=bank[:, :]),
                                      reads=[bank], writes=[foT[bs]])
                        tk.dma("pool", f"fst{bs}", FOd.rearrange("(c p) t -> p c t", p=128)[:, :, tq:tq + BLK], foT[bs][:, :, :],
                               reads=[foT[bs]])
            tk.barrier()

            phase_end()
            with ExitStack() as st4:
                Wuv = sb(st4, [128, 8, 512], BF16, "Wuv")
                load_w(Wuv, 0, W2, D, C_U, 512, "ldw")
                swT = sb(st4, [128, 4, 128], BF16, "swT")
                for h in range(4):
                    tk.dma("pool", "ldw", swT[:, h, :], sgu_wT[l, h, :, :], writes=[swT])
                btab = sb(st4, [128, 256], F32, "btab")
                tk.dma("sp", "ld_const", btab[:, :], sgu_bt[l, :, :], writes=[btab])
                gvn = sb(st4, [128, 256], F32, "gvn")
                bvn = sb(st4, [128, 256], F32, "bvn")
                bcast_load(gvn, vn_g[l, :])
                bcast_load(bvn, vn_b[l, :])
                XTc = [sb(st4, [128, 8, BLK], BF16, "XTc") for _ in range(2)]
                ub = [sb(st4, [128, 256], F32, "ub") for _ in range(2)]
                vn0 = [sb(st4, [128, 256], F32, "vn0") for _ in range(2)]
                vnb = [sb(st4, [128, 256], BF16, "vnb") for _ in range(2)]
                junk = [sb(st4, [128, 256], F32, "junk") for _ in range(2)]
                so = [sb(st4, [128, 256], F32, "so") for _ in range(2)]
                sst = [sb(st4, [128, 8], F32, "sst") for _ in range(2)]
                soT = [sb(st4, [128, 2, BLK], BF16, "soT") for _ in range(2)]
                cjobs = []
                for (t0u, Sk, Sq) in qunits:
                    for qb in range(Sq // BLK):
                        cjobs.append(t0u + qb * BLK)

                def c_load(bi_):
                    s = bi_ % 2
                    tq = cjobs[bi_]
                    tk.dma("sp", f"cx{s}", XTc[s][:, :, :], XTd.rearrange("(kc p) t -> p kc t", p=128)[:, :, tq:tq + BLK],
                           writes=[XTc[s]])

                def sgu_a(bi_, j):
                    s = bi_ % 2
                    p = j % 2
                    if j == 0 and bi_ + 1 < len(cjobs):
                        c_load(bi_ + 1)
                    bank = PS[p]
                    for kc in range(8):
                        tk.op("pe", lambda e, kc=kc, bank=bank, j=j: e.matmul(bank[:, :], XTc[s][:, kc, j * 128:(j + 1) * 128], Wuv[:, kc, :],
                                                                          start=(kc == 0), stop=(kc == 7)),
                              reads=[XTc[s], Wuv], writes=[bank])
                    tk.op("act", lambda e, bank=bank, p=p: e.activation(out=junk[p][:, :], in_=bank[:, 256:512], func=AF.Copy,
                                                                        accum_out=sst[p][:, 0:1]),
                          reads=[bank], writes=[junk[p], sst[p]])
                    tk.op("act", lambda e, bank=bank, p=p: e.activation(out=junk[p][:, :], in_=bank[:, 256:512], func=AF.Square,
                                                                        accum_out=sst[p][:, 1:2]),
                          reads=[bank], writes=[junk[p], sst[p]])
                    tk.op("act", lambda e, bank=bank, p=p: e.activation(out=ub[p][:, :], in_=bank[:, 0:256], func=AF.Copy),
                          reads=[bank], writes=[ub[p]])
                    tk.op("dve", lambda e, p=p: e.tensor_scalar(out=sst[p][:, 2:4], in0=sst[p][:, 0:2], scalar1=1.0 / 256.0, scalar2=None,
                                                                op0=ALU.mult), reads=[sst[p]], writes=[sst[p]])
                    tk.op("dve", lambda e, p=p: e.tensor_tensor(out=sst[p][:, 4:5], in0=sst[p][:, 2:3], in1=sst[p][:, 2:3], op=ALU.mult),
                          reads=[sst[p]], writes=[sst[p]])
                    tk.op("dve", lambda e, p=p: e.tensor_tensor(out=sst[p][:, 5:6], in0=sst[p][:, 3:4], in1=sst[p][:, 4:5], op=ALU.subtract),
                          reads=[sst[p]], writes=[sst[p]])
                    tk.op("dve", lambda e, p=p: e.tensor_scalar(out=sst[p][:, 6:7], in0=sst[p][:, 5:6], scalar1=LN_EPS, scalar2=None,
                                                                op0=ALU.add), reads=[sst[p]], writes=[sst[p]])
                    tk.op("act", lambda e, p=p: e.activation(out=sst[p][:, 7:8], in_=sst[p][:, 6:7], func=AF.Sqrt),
                          reads=[sst[p]], writes=[sst[p]])
                    tk.op("dve", lambda e, p=p: e.reciprocal(out=sst[p][:, 6:7], in_=sst[p][:, 7:8]), reads=[sst[p]], writes=[sst[p]])
                    tk.op("dve", lambda e, bank=bank, p=p: e.tensor_scalar(out=vn0[p][:, :], in0=bank[:, 256:512], scalar1=sst[p][:, 2:3],
                                                                           scalar2=sst[p][:, 6:7], op0=ALU.subtract, op1=ALU.mult),
                          reads=[bank, sst[p]], writes=[vn0[p]])
                    tk.op("dve", lambda e, p=p: e.tensor_tensor(out=vn0[p][:, :], in0=vn0[p][:, :], in1=gvn[:, :], op=ALU.mult),
                          reads=[vn0[p], gvn], writes=[vn0[p]])
                    tk.op("pool", lambda e, p=p: e.tensor_tensor(out=vnb[p][:, :], in0=vn0[p][:, :], in1=bvn[:, :], op=ALU.add),
                          reads=[vn0[p], bvn], writes=[vnb[p]])

                def sgu_b(bi_, j):
                    s = bi_ % 2
                    p = j % 2
                    tq = cjobs[bi_]
                    bank2 = PS[2 + p]
                    for h in range(4):
                        tk.op("pe", lambda e, h=h, bank2=bank2, p=p: e.matmul(bank2[:, h * 64:(h + 1) * 64], swT[:, h, :], vnb[p][:, h * 64:(h + 1) * 64],
                                                                          start=True, stop=True),
                              reads=[swT, vnb[p]], writes=[bank2])
                    tk.op("dve", lambda e, bank2=bank2, p=p: e.tensor_tensor(out=so[p][:, :], in0=bank2[:, 0:256], in1=btab[:, :], op=ALU.add),
                          reads=[bank2, btab], writes=[so[p]])
                    tk.op("dve", lambda e, p=p: e.tensor_tensor(out=so[p][:, :], in0=so[p][:, :], in1=ub[p][:, :], op=ALU.mult),
                          reads=[so[p], ub[p]], writes=[so[p]])
                    for cc in range(2):
                        bank3 = PS[4 + cc + 2 * (bi_ % 2)]
                        tk.op("pe", lambda e, cc=cc, bank3=bank3, p=p, j=j: e.transpose(bank3[:, j * 128:(j + 1) * 128], so[p][:, cc * 128:(cc + 1) * 128],
                                                                                    ident[:, :]),
                              reads=[so[p], ident], writes=[bank3])
                    if j == 3:
                        for cc in range(2):
                            bank3 = PS[4 + cc + 2 * (bi_ % 2)]
                            if cc == 0:
                                tk.op("act", lambda e, cc=cc, bank3=bank3: e.activation(out=soT[s][:, cc, :], in_=bank3[:, :], func=AF.Copy),
                                      reads=[bank3], writes=[soT[s]])
                            else:
                                tk.op("dve", lambda e, cc=cc, bank3=bank3: e.tensor_copy(out=soT[s][:, cc, :], in_=bank3[:, :]),
                                      reads=[bank3], writes=[soT[s]])
                        tk.dma("pool", f"cst{s}", SOd.rearrange("(c p) t -> p c t", p=128)[:, :, tq:tq + BLK], soT[s][:, :, :],
                               reads=[soT[s]])

                c_load(0)
                ctiles = [(bi_, j) for bi_ in range(len(cjobs)) for j in range(4)]
                for ti, (bi_, j) in enumerate(ctiles):
                    sgu_a(bi_, j)
                    if ti >= 1:
                        sgu_b(*ctiles[ti - 1])
                sgu_b(*ctiles[-1])
            tk.barrier()

            phase_end()
            with ExitStack() as st5:
                Wg = sb(st5, [128, 8, 3072], BF16, "Wg")
                load_w(Wg, 0, W2, D, C_G, 3072, "ldw")
                Wfo = sb(st5, [128, 2, D], BF16, "Wfo")
                load_w(Wfo, 0, w_fourier[l], 256, 0, D, "ldw")
                Wsg = sb(st5, [128, 2, D], BF16, "Wsg")
                load_w(Wsg, 0, w_sgu[l], 256, 0, D, "ldw")
                Wdf = sb(st5, [128, 4, D], BF16, "Wdf")
                load_w(Wdf, 0, w_diff[l], 512, 0, D, "ldw")
                XTe = [sb(st5, [128, 8, BLK], BF16, "XTe") for _ in range(2)]
                dob = [sb(st5, [128, 4, BLK], BF16, "dob") for _ in range(2)]
                sob = [sb(st5, [128, 2, BLK], BF16, "sob") for _ in range(2)]
                fob = [sb(st5, [128, 2, BLK], BF16, "fob") for _ in range(2)]
                sgt = [[sb(st5, [128, BLK], F32, "sgt") for _ in range(3)] for _ in range(2)]
                mt = [sb(st5, [128, BLK], F32, "mt") for _ in range(2)]
                mT = [sb(st5, [128, 8, BLK], BF16, "mT") for _ in range(2)]
                djobs = list(cjobs)

                def d_load(bi_):
                    s = bi_ % 2
                    tq = djobs[bi_]
                    tk.dma("sp", f"dx{s}", XTe[s][:, :, :], XTd.rearrange("(kc p) t -> p kc t", p=128)[:, :, tq:tq + BLK], writes=[XTe[s]])
                    tk.dma("sp", f"dx{s}", dob[s][:, :, :], DOd.rearrange("(h p) t -> p h t", p=128)[:, :, tq:tq + BLK], writes=[dob[s]])
                    tk.dma("sp", f"dx{s}", sob[s][:, :, :], SOd.rearrange("(c p) t -> p c t", p=128)[:, :, tq:tq + BLK], writes=[sob[s]])
                    tk.dma("sp", f"dx{s}", fob[s][:, :, :], FOd.rearrange("(c p) t -> p c t", p=128)[:, :, tq:tq + BLK], writes=[fob[s]])

                d_load(0)
                for bi_, tq in enumerate(djobs):
                    s = bi_ % 2
                    if bi_ + 1 < len(djobs):
                        d_load(bi_ + 1)
                    for n in range(8):
                        p = n % 2
                        for b in range(3):
                            bank = PS[b]
                            for kc in range(8):
                                tk.op("pe", lambda e, kc=kc, b=b, bank=bank, n=n: e.matmul(
                                    bank[:, :], Wg[:, kc, b * D + n * 128:b * D + (n + 1) * 128], XTe[s][:, kc, :],
                                    start=(kc == 0), stop=(kc == 7)), reads=[Wg, XTe[s]], writes=[bank])
                            tk.op("act", lambda e, b=b, bank=bank, p=p: e.activation(out=sgt[p][b][:, :], in_=bank[:, :], func=AF.Sigmoid),
                                  reads=[bank], writes=[sgt[p][b]])
                        brs = [(Wfo, fob[s], 2), (Wsg, sob[s], 2), (Wdf, dob[s], 4)]
                        for b, (Wb, xb_, nkc) in enumerate(brs):
                            bank = PS[3 + b]
                            for kc in range(nkc):
                                tk.op("pe", lambda e, kc=kc, bank=bank, Wb=Wb, xb_=xb_, nkc=nkc, n=n: e.matmul(
                                    bank[:, :], Wb[:, kc, n * 128:(n + 1) * 128], xb_[:, kc, :], start=(kc == 0), stop=(kc == nkc - 1)),
                                    reads=[Wb, xb_], writes=[bank])
                        tk.op("dve", lambda e, p=p: e.tensor_tensor(out=mt[0][:, :], in0=PS[3][:, :], in1=sgt[p][0][:, :], op=ALU.mult),
                              reads=[PS[3], sgt[p][0]], writes=[mt[0]])
                        tk.op("dve", lambda e, p=p: e.tensor_tensor(out=mt[1][:, :], in0=PS[4][:, :], in1=sgt[p][1][:, :], op=ALU.mult),
                              reads=[PS[4], sgt[p][1]], writes=[mt[1]])
                        tk.op("pool", lambda e: e.tensor_tensor(out=mt[0][:, :], in0=mt[0][:, :], in1=mt[1][:, :], op=ALU.add),
                              reads=[mt[0], mt[1]], writes=[mt[0]])
                        tk.op("dve", lambda e, p=p: e.tensor_tensor(out=mt[1][:, :], in0=PS[5][:, :], in1=sgt[p][2][:, :], op=ALU.mult),
                              reads=[PS[5], sgt[p][2]], writes=[mt[1]])
                        tk.op("pool", lambda e, n=n: e.tensor_tensor(out=mT[s][:, n, :], in0=mt[0][:, :], in1=mt[1][:, :], op=ALU.add),
                              reads=[mt[0], mt[1]], writes=[mT[s]])
                    tk.dma("pool", f"dst{s}", MTd.rearrange("(kc p) t -> p kc t", p=128)[:, :, tq:tq + BLK], mT[s][:, :, :], reads=[mT[s]])
            tk.barrier()

            phase_end()
            with ExitStack() as st6:
                Wo = sb(st6, [128, 8, D], BF16, "Wo")
                load_w(Wo, 0, w_out[l], D, 0, D, "ldw")
                g1 = sb(st6, [128, D], F32, "g1")
                b1 = sb(st6, [128, D], F32, "b1")
                bcast_load(g1, ln1_g[l, :])
                bcast_load(b1, ln1_b[l, :])
                moe = (l % 2 == 1)
                if moe:
                    Wr = sb(st6, [128, 8, NE], F32, "Wr")
                    tk.dma("sp", "ld_const", Wr[:, :, :], w_router[l // 2].rearrange("(kc p) e -> p kc e", p=128), writes=[Wr])
                MTb = [sb(st6, [128, 8, BLK], BF16, "MTb") for _ in range(2)]
                xsb = [sb(st6, [128, 4, D], F32, "xsb") for _ in range(2)]
                yb = [sb(st6, [128, D], F32, "yb") for _ in range(2)]
                x1s = [sb(st6, [128, D], F32, "x1s") for _ in range(2)]
                stt = [sb(st6, [128, 12], F32, "stt") for _ in range(2)]
                mvt = [sb(st6, [128, 4], F32, "mvt") for _ in range(2)]
                x1T32 = [sb(st6, [128, 8, 128], F32, "x1T32") for _ in range(2)]
                x1Tb = [sb(st6, [128, 8, BLK], BF16, "x1Tb") for _ in range(2)]
                rt = [sb(st6, [128, 64], F32, "rt") for _ in range(2)]
                ejobs = list(cjobs)

                def e_load(bi_):
                    s = bi_ % 2
                    tq = ejobs[bi_]
                    tk.dma("sp", f"ex{s}", MTb[s][:, :, :], MTd.rearrange("(kc p) t -> p kc t", p=128)[:, :, tq:tq + BLK], writes=[MTb[s]])
                    tk.dma("sp", f"ex{s}", xsb[s][:, :, :], Xsrc[tq:tq + BLK, :].rearrange("(j p) d -> p j d", p=128), writes=[xsb[s]])

                def stage_a(bi_, j):
                    s = bi_ % 2
                    tq = ejobs[bi_]
                    p = j % 2
                    if j == 0 and bi_ + 1 < len(ejobs):
                        e_load(bi_ + 1)
                    for nh in range(2):
                        bank = PS[(2 * j + nh) % 4]
                        for kc in range(8):
                            tk.op("pe", lambda e, kc=kc, bank=bank, nh=nh, j=j: e.matmul(
                                bank[:, :], MTb[s][:, kc, j * 128:(j + 1) * 128], Wo[:, kc, nh * 512:(nh + 1) * 512],
                                start=(kc == 0), stop=(kc == 7)), reads=[MTb[s], Wo], writes=[bank])
                        tk.op("dve", lambda e, bank=bank, nh=nh, j=j, p=p: e.scalar_tensor_tensor(
                            out=yb[p][:, nh * 512:(nh + 1) * 512], in0=xsb[s][:, j, nh * 512:(nh + 1) * 512], scalar=ALPHA,
                            in1=bank[:, :], op0=ALU.mult, op1=ALU.add), reads=[xsb[s], bank], writes=[yb[p]])
                    layer_norm_tile((stt[p], mvt[p]), yb[p], g1, b1, x1s[p])
                    tk.dma("pool", f"est{p}", X1d[tq + j * 128:tq + (j + 1) * 128, :], x1s[p][:, :], reads=[x1s[p]])

                def stage_b(bi_, j):
                    s = bi_ % 2
                    tq = ejobs[bi_]
                    p = j % 2
                    for g in range(2):
                        bank = PS[4 + g]
                        for k4 in range(4):
                            kc = g * 4 + k4
                            tk.op("pe", lambda e, bank=bank, k4=k4, kc=kc, p=p: e.transpose(
                                bank[:, k4 * 128:(k4 + 1) * 128], x1s[p][:, kc * 128:(kc + 1) * 128], ident[:, :]),
                                reads=[x1s[p], ident], writes=[bank])
                        if not moe:
                            tk.op("act", lambda e, bank=bank, g=g, j=j: e.activation(
                                out=x1Tb[s][:, g * 4:(g + 1) * 4, j * 128:(j + 1) * 128],
                                in_=bank[:, :].rearrange("p (k t) -> p k t", k=4), func=AF.Copy),
                                reads=[bank], writes=[x1Tb[s]])
                        else:
                            tk.op("act", lambda e, bank=bank, g=g, p=p: e.activation(
                                out=x1T32[p][:, g * 4:(g + 1) * 4, :],
                                in_=bank[:, :].rearrange("p (k t) -> p k t", k=4), func=AF.Copy),
                                reads=[bank], writes=[x1T32[p]])
                            tk.op("pool", lambda e, g=g, j=j, p=p: e.tensor_copy(
                                out=x1Tb[s][:, g * 4:(g + 1) * 4, j * 128:(j + 1) * 128],
                                in_=x1T32[p][:, g * 4:(g + 1) * 4, :]),
                                reads=[x1T32[p]], writes=[x1Tb[s]])
                    if j == 3:
                        tk.dma("pool", f"est2{s}", X1Td.rearrange("(kc p) t -> p kc t", p=128)[:, :, tq:tq + BLK], x1Tb[s][:, :, :], reads=[x1Tb[s]])

                def stage_c(bi_, j):
                    s = bi_ % 2
                    tq = ejobs[bi_]
                    p = j % 2
                    if moe:
                        bank = PS[6 + p]
                        for kc in range(8):
                            tk.op("pe", lambda e, kc=kc, bank=bank, p=p: e.matmul(bank[:, 0:NE], x1T32[p][:, kc, :], Wr[:, kc, :],
                                                                              start=(kc == 0), stop=(kc == 7)),
                                  reads=[x1T32[p], Wr], writes=[bank])
                        r = rt[p]
                        tk.op("dve", lambda e, bank=bank, r=r: e.tensor_copy(out=r[:, 0:8], in_=bank[:, 0:NE]), reads=[bank], writes=[r])
                        tk.op("dve", lambda e, r=r: e.reduce_max(out=r[:, 32:33], in_=r[:, 0:8], axis=AX.X), reads=[r], writes=[r])
                        tk.op("dve", lambda e, r=r: e.tensor_scalar(out=r[:, 8:16], in0=r[:, 0:8], scalar1=r[:, 32:33], scalar2=None,
                                                                    op0=ALU.is_equal), reads=[r], writes=[r])
                        tk.op("dve", lambda e, r=r: e.scalar_tensor_tensor(out=r[:, 16:24], in0=r[:, 8:16], scalar=-1e30, in1=r[:, 0:8],
                                                                           op0=ALU.mult, op1=ALU.add), reads=[r], writes=[r])
                        tk.op("dve", lambda e, r=r: e.reduce_max(out=r[:, 33:34], in_=r[:, 16:24], axis=AX.X), reads=[r], writes=[r])
                        tk.op("dve", lambda e, r=r: e.tensor_scalar(out=r[:, 24:32], in0=r[:, 16:24], scalar1=r[:, 33:34], scalar2=None,
                                                                    op0=ALU.is_equal), reads=[r], writes=[r])
                        tk.op("dve", lambda e, r=r: e.tensor_tensor(out=r[:, 34:35], in0=r[:, 33:34], in1=r[:, 32:33], op=ALU.subtract),
                              reads=[r], writes=[r])
                        tk.op("act", lambda e, r=r: e.activation(out=r[:, 35:36], in_=r[:, 34:35], func=AF.Sigmoid), reads=[r], writes=[r])
                        tk.op("dve", lambda e, r=r: e.tensor_scalar(out=r[:, 36:37], in0=r[:, 35:36], scalar1=-1.0, scalar2=1.0,
                                                                    op0=ALU.mult, op1=ALU.add), reads=[r], writes=[r])
                        tk.op("dve", lambda e, r=r: e.tensor_scalar(out=r[:, 48:56], in0=r[:, 8:16], scalar1=r[:, 36:37], scalar2=None,
                                                                    op0=ALU.mult), reads=[r], writes=[r])
                        tk.op("dve", lambda e, r=r: e.scalar_tensor_tensor(out=r[:, 40:48], in0=r[:, 24:32], scalar=r[:, 35:36], in1=r[:, 48:56],
                                                                           op0=ALU.mult, op1=ALU.add), reads=[r], writes=[r])
                        tk.dma("pool", f"est{p}", CMBd[tq + j * 128:tq + (j + 1) * 128, :], r[:, 40:48], reads=[r])

                e_load(0)
                tiles = [(bi_, j) for bi_ in range(len(ejobs)) for j in range(4)]
                for ti, (bi_, j) in enumerate(tiles):
                    stage_a(bi_, j)
                    if ti >= 1:
                        stage_b(*tiles[ti - 1])
                    if ti >= 2:
                        stage_c(*tiles[ti - 2])
                stage_b(*tiles[-1])
                if len(tiles) >= 2:
                    stage_c(*tiles[-2])
                stage_c(*tiles[-1])
            tk.barrier()

            phase_end()
            moe = (l % 2 == 1)
            if moe:
                passes = [(moe_w_gate[l // 2, e_], moe_w_up[l // 2, e_], moe_w_down[l // 2, e_], e_) for e_ in range(NE)]
            else:
                passes = [(ffn_w_gate[l // 2], ffn_w_up[l // 2], ffn_w_down[l // 2], None)]
            HC = NFC // 2
            HCOLS = HC * 128
            hpasses = []
            for (wg_ap, wu_ap, wd_ap, ex) in passes:
                for half in range(2):
                    c0col = half * HCOLS
                    ncols = min(HCOLS, DFF - c0col)
                    hpasses.append((wg_ap, wu_ap, wd_ap, ex, c0col, ncols, (ex is None or ex == 0) and half == 0))
            with ExitStack() as st7:
                Wg2 = [sb(st7, [128, 8, HCOLS], BF16, "Wg2") for _ in range(2)]
                Wu2 = [sb(st7, [128, 8, HCOLS], BF16, "Wu2") for _ in range(2)]
                Wd2 = [sb(st7, [128, HC, D], BF16, "Wd2") for _ in range(2)]
                XTf = [sb(st7, [128, 8, BLK], BF16, "XTf") for _ in range(2)]
                cmbb = [sb(st7, [128, 4, NE], F32, "cmbb") for _ in range(2)]
                hT = sb(st7, [128, HC, BLK], BF16, "hT")
                sgf = [sb(st7, [128, BLK], F32, "sgf") for _ in range(2)]
                ysb = [sb(st7, [128, D], F32, "ysb") for _ in range(2)]
                fjobs = list(cjobs)

                def w_load_ops(pi):
                    wg_ap, wu_ap, wd_ap, ex, c0col, ncols, first = hpasses[pi]
                    ws = pi % 2
                    return (load_w_ops(Wg2[ws], 0, wg_ap, D, c0col, ncols, f"ldw{ws}")
                            + load_w_ops(Wu2[ws], 0, wu_ap, D, c0col, ncols, f"ldw{ws}")
                            + load_w_ops(Wd2[ws], 0, wd_ap[c0col:c0col + ncols, :], ncols, 0, D, f"ldw{ws}"))

                pend_w = []
                nblk_f = len(fjobs)

                gjobs = [(pi, bi_) for pi in range(len(hpasses)) for bi_ in range(len(fjobs))]

                def g_load(gi):
                    pi, bi_ = gjobs[gi]
                    ex = hpasses[pi][3]
                    s = gi % 2
                    tq = fjobs[bi_]
                    tk.dma("sp", f"gx{s}", XTf[s][:, :, :], X1Td.rearrange("(kc p) t -> p kc t", p=128)[:, :, tq:tq + BLK], writes=[XTf[s]])
                    if ex is not None:
                        tk.dma("sp", f"gx{s}", cmbb[s][:, :, :], CMBd[tq:tq + BLK, :].rearrange("(j p) e -> p j e", p=128), writes=[cmbb[s]])

                for op_ in w_load_ops(0):
                    op_()
                g_load(0)
                for gi, (pi, bi_) in enumerate(gjobs):
                    wg_ap, wu_ap, wd_ap, ex, c0col, ncols, first = hpasses[pi]
                    ws = pi % 2
                    Wgs, Wus, Wds = Wg2[ws], Wu2[ws], Wd2[ws]
                    nfc = (ncols + 127) // 128
                    s = gi % 2
                    tq = fjobs[bi_]
                    if gi + 1 < len(gjobs):
                        g_load(gi + 1)
                    for c in range(nfc):
                        rows = min(128, ncols - c * 128)
                        bg = PS[(2 * c) % 4]
                        bu = PS[(2 * c + 1) % 4]
                        for kc in range(8):
                            tk.op("pe", lambda e, kc=kc, bg=bg, c=c, rows=rows: e.matmul(bg[0:rows, :], Wgs[:, kc, c * 128:c * 128 + rows], XTf[s][:, kc, :],
                                                                                     start=(kc == 0), stop=(kc == 7)),
                                  reads=[Wgs, XTf[s]], writes=[bg])
                        for kc in range(8):
                            tk.op("pe", lambda e, kc=kc, bu=bu, c=c, rows=rows: e.matmul(bu[0:rows, :], Wus[:, kc, c * 128:c * 128 + rows], XTf[s][:, kc, :],
                                                                                     start=(kc == 0), stop=(kc == 7)),
                                  reads=[Wus, XTf[s]], writes=[bu])
                        sg_ = sgf[c % 2]
                        tk.op("act", lambda e, bg=bg, sg_=sg_, rows=rows: e.activation(out=sg_[0:rows, :], in_=bg[0:rows, :], func=AF.Silu),
                              reads=[bg], writes=[sg_])
                        tk.op("dve", lambda e, bu=bu, sg_=sg_, rows=rows, c=c: e.tensor_tensor(out=hT[0:rows, c, :], in0=bu[0:rows, :], in1=sg_[0:rows, :],
                                                                                           op=ALU.mult),
                              reads=[bu, sg_], writes=[hT])
                    if bi_ == 0 and pi + 1 < len(hpasses):
                        pend_w = w_load_ops(pi + 1)
                    if pend_w:
                        left = max(1, nblk_f - 2 - bi_)
                        for _ in range((len(pend_w) + left - 1) // left):
                            pend_w.pop(0)()
                    for j in range(4):
                        p = j % 2
                        for nh in range(2):
                            bank = PS[4 + (2 * j + nh) % 4]
                            for c in range(nfc):
                                rows = min(128, ncols - c * 128)
                                tk.op("pe", lambda e, c=c, bank=bank, rows=rows, nh=nh, j=j: e.matmul(
                                    bank[:, :], hT[0:rows, c, j * 128:(j + 1) * 128], Wds[0:rows, c, nh * 512:(nh + 1) * 512],
                                    start=(c == 0), stop=(c == nfc - 1)), reads=[hT, Wds], writes=[bank])
                            if ex is None:
                                tk.op("act", lambda e, bank=bank, nh=nh, p=p: e.activation(out=ysb[p][:, nh * 512:(nh + 1) * 512], in_=bank[:, :],
                                                                                       func=AF.Copy), reads=[bank], writes=[ysb[p]])
                            else:
                                tk.op("act", lambda e, bank=bank, nh=nh, p=p, j=j: e.activation(
                                    out=ysb[p][:, nh * 512:(nh + 1) * 512], in_=bank[:, :], func=AF.Copy, scale=cmbb[s][:, j, ex:ex + 1]),
                                    reads=[bank, cmbb[s]], writes=[ysb[p]])
                        ydst = Yd[tq + j * 128:tq + (j + 1) * 128, :]
                        if first:
                            tk.dma("pool", f"gst{p}", ydst, ysb[p][:, :], reads=[ysb[p]])
                        else:
                            tk.dma("pool", f"gst{p}", ydst, ysb[p][:, :], reads=[ysb[p]], accum_op=ALU.add)
            tk.barrier()

            phase_end()
            with ExitStack() as st8:
                g2 = sb(st8, [128, D], F32, "g2")
                b2 = sb(st8, [128, D], F32, "b2")
                bcast_load(g2, ln2_g[l, :])
                bcast_load(b2, ln2_b[l, :])
                xa = [sb(st8, [128, 4, D], F32, "xa") for _ in range(2)]
                ya = [sb(st8, [128, 4, D], F32, "ya") for _ in range(2)]
                tb_ = [sb(st8, [128, D], F32, "tb") for _ in range(2)]
                ob = [sb(st8, [128, D], F32, "ob") for _ in range(2)]
                stt2 = [sb(st8, [128, 12], F32, "stt2") for _ in range(2)]
                mvt2 = [sb(st8, [128, 4], F32, "mvt2") for _ in range(2)]
                hjobs4 = list(cjobs)
                dest = Xmid if l + 1 < NLAYER else out

                def h_load(bi_):
                    s = bi_ % 2
                    tq = hjobs4[bi_]
                    tk.dma("sp", f"hx{s}", xa[s][:, :, :], X1d[tq:tq + BLK, :].rearrange("(j p) d -> p j d", p=128), writes=[xa[s]])
                    tk.dma("sp", f"hx{s}", ya[s][:, :, :], Yd[tq:tq + BLK, :].rearrange("(j p) d -> p j d", p=128), writes=[ya[s]])

                h_load(0)
                for bi_, tq in enumerate(hjobs4):
                    s = bi_ % 2
                    if bi_ + 1 < len(hjobs4):
                        h_load(bi_ + 1)
                    for j in range(4):
                        p = j % 2
                        tk.op("dve", lambda e, j=j, p=p: e.scalar_tensor_tensor(out=tb_[p][:, :], in0=xa[s][:, j, :], scalar=ALPHA, in1=ya[s][:, j, :],
                                                                            op0=ALU.mult, op1=ALU.add), reads=[xa[s], ya[s]], writes=[tb_[p]])
                        layer_norm_tile((stt2[p], mvt2[p]), tb_[p], g2, b2, ob[p])
                        tk.dma("pool", f"hst{p}", dest[tq + j * 128:tq + (j + 1) * 128, :], ob[p][:, :], reads=[ob[p]])
            tk.barrier()
            phase_end()
      except _StopBuild:
        pass
    return nc


def _rope_tables(pos):
    inv = ROPE_THETA ** (-np.arange(0, 64, 2, dtype=np.float64) / 64.0)
    ang = pos.astype(np.float64)[None, :] * inv[:, None]
    c = np.cos(ang)
    s = np.sin(ang)
    d = np.arange(128) % 64
    j = d % 32
    sign = np.where(d < 32, -1.0, 1.0)
    return c[j].astype(np.float32), (s[j] * sign[:, None]).astype(np.float32)


def _dft_mats(pos_rows, pos_cols, S):
    k = (pos_rows.astype(np.int64)[:, None] * pos_cols.astype(np.int64)[None, :]) % S
    ang = (2.0 * np.pi / S) * k.astype(np.float64)
    sc = 1.0 / math.sqrt(S * 64.0)
    return (np.cos(ang) * sc).astype(ml_dtypes.bfloat16), (np.sin(ang) * sc).astype(ml_dtypes.bfloat16)


_PROG_CACHE = {}


def run_module(inputs, SS, SP, NLAYER=2, debug=None, stop_phase=None, run_cores=8):
    f32 = np.float32
    xp = np.asarray(inputs["x_prompt"], f32)
    xs = np.asarray(inputs["x_sample"], f32)
    ncores = 8
    HP = SP // 2
    assert xp.shape[0] * 2 == ncores and xs.shape[0] == 2 * ncores
    w_in = np.asarray(inputs["w_in"], f32)
    perm = np.arange(512).reshape(8, 2, 32)[:, ::-1, :].reshape(512)
    q = w_in[:, :, 768:1280]
    k = w_in[:, :, 1280:1792]
    w_in_e = np.ascontiguousarray(np.concatenate([w_in, q[:, :, perm], k[:, :, perm]], axis=2))
    sgu_w = np.asarray(inputs["sgu_w"], f32)
    sgu_b = np.asarray(inputs["sgu_b"], f32)
    sgu_wT = np.ascontiguousarray(np.transpose(sgu_w, (0, 1, 3, 2)))
    sgu_bt = np.ascontiguousarray(np.repeat(np.transpose(sgu_b, (0, 2, 1))[:, :, :, None], 64, axis=3).reshape(sgu_b.shape[0], 128, 256))
    ident = np.eye(128, dtype=f32)
    kk = np.arange(64)
    ang = 2.0 * np.pi * ((kk[:, None] * kk[None, :]) % 64) / 64.0
    bdc = np.zeros((128, 128), f32)
    bdsn = np.zeros((128, 128), f32)
    for g in range(2):
        bdc[g * 64:(g + 1) * 64, g * 64:(g + 1) * 64] = np.cos(ang)
        bdsn[g * 64:(g + 1) * 64, g * 64:(g + 1) * 64] = -np.sin(ang)
    pos_s = np.arange(SS)
    dcs, dss = _dft_mats(pos_s, pos_s, SS)
    shared = {
        "w_in_e": w_in_e, "sgu_wT": sgu_wT, "sgu_bt": sgu_bt, "ident": ident, "bdc": bdc, "bdsn": bdsn,
        "dft_cs": dcs, "dft_ss": dss,
    }
    for nm in ["w_fourier", "w_sgu", "w_diff", "w_out", "vn_g", "vn_b", "lam_q1", "lam_k1", "lam_q2", "lam_k2", "subln_g",
               "ln1_g", "ln1_b", "ln2_g", "ln2_b", "ffn_w_gate", "ffn_w_up", "ffn_w_down", "w_router",
               "moe_w_gate", "moe_w_up", "moe_w_down"]:
        shared[nm] = np.ascontiguousarray(np.asarray(inputs[nm], f32))
    par = {}
    for parity in range(2):
        pos_p = np.concatenate([np.arange(parity * HP, (parity + 1) * HP), np.arange((1 - parity) * HP, (2 - parity) * HP)])
        dcp, dsp = _dft_mats(pos_p, pos_p, SP)
        pos_all = np.concatenate([pos_s, pos_s, pos_p])
        rc, rs = _rope_tables(pos_all)
        par[parity] = dict(pos_p=pos_p, dft_cp=dcp, dft_sp=dsp, ropec=rc, ropes=rs)
    in_maps = []
    for c in range(ncores):
        parity = c % 2
        P = par[parity]
        xin = np.concatenate([xs[2 * c], xs[2 * c + 1], xp[c // 2][P["pos_p"]]], axis=0)
        m = dict(shared)
        m.update(xin=np.ascontiguousarray(xin), dft_cp=P["dft_cp"], dft_sp=P["dft_sp"], ropec=P["ropec"], ropes=P["ropes"])
        in_maps.append(m)
    key = (SS, SP, NLAYER, tuple(sorted(debug)) if debug else None, stop_phase)
    if key not in _PROG_CACHE:
        _PROG_CACHE[key] = build_program(SS, SP, NLAYER, debug, stop_phase)
    nc = _PROG_CACHE[key]
    res = run_bass_kernel_spmd(nc, in_maps[:run_cores], core_ids=list(range(run_cores)))
    outs = [r["out"] for r in res.results]
    outs = outs + [outs[0]] * (ncores - run_cores)
    y_sample = np.stack([outs[c][i * SS:(i + 1) * SS] for c in range(ncores) for i in range(2)], axis=0)
    y_prompt = np.stack([np.concatenate([outs[2 * b][2 * SS:2 * SS + HP], outs[2 * b + 1][2 * SS:2 * SS + HP]], axis=0)
                         for b in range(ncores // 2)], axis=0)
    if debug:
        return (y_prompt.astype(f32), y_sample.astype(f32)), res.results
    return (y_prompt.astype(f32), y_sample.astype(f32))


def kernel(**inputs):
    return run_module(inputs, SS=4096, SP=8192)
```

```python
import math
from contextlib import ExitStack

import numpy as np
import ml_dtypes

import concourse.bass as bass
import concourse.mybir as mybir
from concourse.bass_utils import run_bass_kernel_spmd

F32 = mybir.dt.float32
BF16 = mybir.dt.bfloat16
AF = mybir.ActivationFunctionType
ALU = mybir.AluOpType
AX = mybir.AxisListType

D = 1024
DFF = 2752
NE = 8
NFC = 22
DEPTH = 2
ALPHA = (2 * DEPTH) ** 0.25
LN_EPS = 1e-5
RMS_EPS = 1e-5
ROPE_THETA = 10000.0
C_F, C_U, C_V, C_Q, C_K, C_VA, C_G, C_QS, C_KS = 0, 256, 512, 768, 1280, 1792, 2304, 5376, 5888
NW = 6400
BLK = 512


class Buf:
    def __init__(self, ap, name=""):
        self.ap = ap
        self.name = name
        self.w = {}
        self.r = {}

    def __getitem__(self, idx):
        return self.ap[idx]


class Tracker:
    def __init__(self, nc, es):
        self.nc = nc
        self.es = es
        self.eng = {"pe": nc.tensor, "act": nc.scalar, "dve": nc.vector, "pool": nc.gpsimd, "sp": nc.sync}
        self.sems = {}
        self.cnt = {}
        self.waited = {k: {} for k in self.eng}
        self.nconst = 0
        for k in self.eng:
            self._mksem(k)

    def _mksem(self, key):
        self.sems[key] = self.es.enter_context(self.nc.semaphore("s_" + key))
        self.cnt[key] = 0

    def _wait(self, e, key, val):
        if key == "pe" and e == "pe":
            return
        if key not in self.eng:
            val = self.cnt[key]
        if self.waited[e].get(key, 0) >= val:
            return
        self.eng[e].wait_ge(self.sems[key], val)
        self.waited[e][key] = val

    def _deps(self, e, reads, writes):
        for b in reads:
            for k, v in b.w.items():
                self._wait(e, k, v)
        for b in writes:
            for k, v in b.w.items():
                self._wait(e, k, v)
            for k, v in b.r.items():
                self._wait(e, k, v)

    def _mark(self, key, val, reads, writes):
        for b in reads:
            if b.r.get(key, 0) < val:
                b.r[key] = val
        for b in writes:
            if b.w.get(key, 0) < val:
                b.w[key] = val

    def op(self, e, fn, reads=(), writes=()):
        self._deps(e, reads, writes)
        ins = fn(self.eng[e])
        ins.then_inc(self.sems[e], 1)
        self.cnt[e] += 1
        self._mark(e, self.cnt[e], reads, writes)

    def dma(self, q, semkey, out, in_, reads=(), writes=(), **kw):
        if semkey == "ld_const":
            semkey = f"ldc{self.nconst}"
            self.nconst += 1
        if semkey not in self.sems:
            self._mksem(semkey)
        self._deps(q, reads, writes)
        ins = self.eng[q].dma_start(out=out, in_=in_, **kw)
        ins.then_inc(self.sems[semkey], 16)
        self.cnt[semkey] += 16
        self._mark(semkey, self.cnt[semkey], reads, writes)

    def barrier(self):
        self.nconst = 0
        for e in self.eng:
            for k in self.sems:
                if self.cnt[k] > 0:
                    self._wait(e, k, self.cnt[k])


class _StopBuild(Exception):
    pass


def build_program(SS, SP, NLAYER=2, debug=None, stop_phase=None):
    T0 = 2 * SS + SP
    HP = SP // 2
    T1 = 2 * SS + HP
    units = [(0, SS), (SS, SS), (2 * SS, SP)]

    nc = bass.Bass("TRN2", target_bir_lowering=False)

    def din(name, shape, dt=F32):
        return nc.dram_tensor(name, list(shape), dt, kind="ExternalInput").ap()

    def dscr(name, shape, dt):
        kind = {}
        if debug and name in debug:
            kind = dict(kind="ExternalOutput")
        return nc.dram_tensor(name, list(shape), dt, **kind).ap()

    xin = din("xin", [T0, D])
    w_in = din("w_in_e", [2, D, NW])
    w_fourier = din("w_fourier", [2, 256, D])
    w_sgu = din("w_sgu", [2, 256, D])
    w_diff = din("w_diff", [2, 512, D])
    w_out = din("w_out", [2, D, D])
    vn_g = din("vn_g", [2, 256])
    vn_b = din("vn_b", [2, 256])
    sgu_wT = din("sgu_wT", [2, 4, 128, 128])
    sgu_bt = din("sgu_bt", [2, 128, 256])
    lam_q1 = din("lam_q1", [2, 64])
    lam_k1 = din("lam_k1", [2, 64])
    lam_q2 = din("lam_q2", [2, 64])
    lam_k2 = din("lam_k2", [2, 64])
    subln_g = din("subln_g", [2, 128])
    ln1_g = din("ln1_g", [2, D])
    ln1_b = din("ln1_b", [2, D])
    ln2_g = din("ln2_g", [2, D])
    ln2_b = din("ln2_b", [2, D])
    ffn_w_gate = din("ffn_w_gate", [1, D, DFF])
    ffn_w_up = din("ffn_w_up", [1, D, DFF])
    ffn_w_down = din("ffn_w_down", [1, DFF, D])
    w_router = din("w_router", [1, D, NE])
    moe_w_gate = din("moe_w_gate", [1, NE, D, DFF])
    moe_w_up = din("moe_w_up", [1, NE, D, DFF])
    moe_w_down = din("moe_w_down", [1, NE, DFF, D])
    ident_d = din("ident", [128, 128])
    ropec = din("ropec", [128, T0])
    ropes = din("ropes", [128, T0])
    dft_cs = din("dft_cs", [SS, SS], BF16)
    dft_ss = din("dft_ss", [SS, SS], BF16)
    dft_cp = din("dft_cp", [SP, SP], BF16)
    dft_sp = din("dft_sp", [SP, SP], BF16)
    bdc_d = din("bdc", [128, 128])
    bdsn_d = din("bdsn", [128, 128])
    out = nc.dram_tensor("out", [T1, D], F32, kind="ExternalOutput").ap()

    XTd = dscr("XTd", [D, T0], BF16)
    KTd = dscr("KTd", [512, T0], BF16)
    Vd = dscr("Vd", [4, 128, T0 // 128, 128], BF16)
    Fd = dscr("Fd", [T0, 256], BF16)
    DOd = dscr("DOd", [512, T0], BF16)
    SOd = dscr("SOd", [256, T0], BF16)
    FOd = dscr("FOd", [256, T0], BF16)
    MTd = dscr("MTd", [D, T0], BF16)
    X1d = dscr("X1d", [T0, D], F32)
    X1Td = dscr("X1Td", [D, T0], BF16)
    CMBd = dscr("CMBd", [T0, NE], F32)
    Yd = dscr("Yd", [T0, D], F32)
    Xmid = dscr("Xmid", [T0, D], F32)

    es = ExitStack()
    with es:
      try:
        tk = Tracker(nc, es)
        uid = [0]

        def sb(stack, shape, dt, name):
            uid[0] += 1
            t = stack.enter_context(nc.sbuf_tensor(f"{name}_{uid[0]}", list(shape), dt))
            return Buf(t, name)

        PS = [Buf(es.enter_context(nc.psum_tensor(f"psum{i}", [128, 512], F32)), f"ps{i}") for i in range(8)]

        ident = sb(es, [128, 128], F32, "ident")
        tk.dma("sp", "ld_const", ident[:], ident_d[:, :], writes=[ident])
        ones_bf = sb(es, [128, 128], BF16, "ones_bf")
        onesm_bf = sb(es, [128, 128], BF16, "onesm_bf")
        tk.op("dve", lambda e: e.memset(ones_bf[:], 1.0), writes=[ones_bf])
        tk.op("dve", lambda e: e.memset(onesm_bf[:], 1.0 / 128.0), writes=[onesm_bf])
        bdc = sb(es, [128, 128], BF16, "bdc")
        bdsn = sb(es, [128, 128], BF16, "bdsn")
        tk.dma("pool", "ld_constp", bdc[:], bdc_d[:, :], writes=[bdc])
        tk.dma("pool", "ld_constp", bdsn[:], bdsn_d[:, :], writes=[bdsn])

        def load_w_ops(dst, dst_c0, src2d, K, c0, ncols, semkey):
            ops = []
            nk = (K + 127) // 128
            for kc in range(nk):
                rows = min(128, K - kc * 128)
                cc = 0
                while cc < ncols:
                    n = min(2048, ncols - cc)
                    ops.append(lambda kc=kc, rows=rows, cc=cc, n=n: tk.dma(
                        "pool", semkey, dst[0:rows, kc, dst_c0 + cc:dst_c0 + cc + n],
                        src2d[kc * 128:kc * 128 + rows, c0 + cc:c0 + cc + n], writes=[dst]))
                    cc += n
            return ops

        def load_w(dst, dst_c0, src2d, K, c0, ncols, semkey):
            for op_ in load_w_ops(dst, dst_c0, src2d, K, c0, ncols, semkey):
                op_()

        def layer_norm_tile(stack_bufs, y, gtab, btab, outb, l_eng="pool"):
            st, mv = stack_bufs
            tk.op("dve", lambda e: e.bn_stats(out=st[:, 0:6], in_=y[:, 0:512]), reads=[y], writes=[st])
            tk.op("dve", lambda e: e.bn_stats(out=st[:, 6:12], in_=y[:, 512:1024]), reads=[y], writes=[st])
            tk.op("dve", lambda e: e.bn_aggr(out=mv[:, 0:2], in_=st[:, 0:12]), reads=[st], writes=[mv])
            tk.op("dve", lambda e: e.tensor_scalar(out=mv[:, 2:3], in0=mv[:, 1:2], scalar1=LN_EPS, scalar2=None,
                                                   op0=ALU.add), reads=[mv], writes=[mv])
            tk.op("act", lambda e: e.activation(out=mv[:, 3:4], in_=mv[:, 2:3], func=AF.Sqrt), reads=[mv], writes=[mv])
            tk.op("dve", lambda e: e.reciprocal(out=mv[:, 2:3], in_=mv[:, 3:4]), reads=[mv], writes=[mv])
            tk.op("dve", lambda e: e.tensor_scalar(out=y[:, :], in0=y[:, :], scalar1=mv[:, 0:1], scalar2=mv[:, 2:3],
                                                   op0=ALU.subtract, op1=ALU.mult), reads=[y, mv], writes=[y])
            tk.op("dve", lambda e: e.tensor_tensor(out=y[:, :], in0=y[:, :], in1=gtab[:, :], op=ALU.mult),
                  reads=[y, gtab], writes=[y])
            tk.op(l_eng, lambda e: e.tensor_tensor(out=outb[:, :], in0=y[:, :], in1=btab[:, :], op=ALU.add),
                  reads=[y, btab], writes=[outb])

        def bcast_load(dst, vec_ap, semkey="ld_const"):
            tk.dma("sp", semkey, dst[:, :], vec_ap.partition_broadcast(128), writes=[dst])

        phc = [0]

        def phase_end():
            phc[0] += 1
            if stop_phase is not None and phc[0] >= stop_phase:
                raise _StopBuild()

        for l in range(NLAYER):
            Xsrc = xin if l == 0 else Xmid
            TQ = T0 if l == 0 else T1
            qunits = [(t0, Sk, (Sk if l == 0 else min(Sk, SS if t0 < 2 * SS else HP))) for (t0, Sk) in units]
            lambda_init = 0.8 - 0.6 * math.exp(-0.3 * l)
            W2 = w_in[l]

            with ExitStack() as st1:
                Wk = sb(st1, [128, 8, 1792], BF16, "Wk")
                load_w(Wk, 0, W2, D, C_K, 512, "ldw")
                load_w(Wk, 512, W2, D, C_KS, 512, "ldw")
                load_w(Wk, 1024, W2, D, C_VA, 512, "ldw")
                load_w(Wk, 1536, W2, D, C_F, 256, "ldw")
                xs = [sb(st1, [128, 4, D], F32, "xs") for _ in range(2)]
                cs = [sb(st1, [128, 2, BLK], F32, "cs") for _ in range(2)]
                XT = [sb(st1, [128, 8, BLK], BF16, "XT") for _ in range(2)]
                kst = [sb(st1, [128, 4, BLK], BF16, "kst") for _ in range(2)]
                vst = [sb(st1, [128, 4, 4, 128], BF16, "vst") for _ in range(2)]
                fst = [sb(st1, [128, 4, 256], BF16, "fst") for _ in range(2)]
                tmp = [sb(st1, [128, BLK], F32, "tmp") for _ in range(4)]
                nb = T0 // BLK

                def p1_load(bi):
                    s = bi % 2
                    t0 = bi * BLK
                    tk.dma("sp", f"p1x{s}", xs[s][:, :, :], Xsrc[t0:t0 + BLK, :].rearrange("(j p) d -> p j d", p=128),
                           writes=[xs[s]])
                    tk.dma("sp", f"p1x{s}", cs[s][:, 0, :], ropec[:, t0:t0 + BLK], writes=[cs[s]])
                    tk.dma("sp", f"p1x{s}", cs[s][:, 1, :], ropes[:, t0:t0 + BLK], writes=[cs[s]])

                p1_load(0)
                for bi in range(nb):
                    s = bi % 2
                    t0 = bi * BLK
                    if bi + 1 < nb:
                        p1_load(bi + 1)
                    for kc in range(8):
                        bank = PS[kc % 2]
                        for j in range(4):
                            tk.op("pe", lambda e, j=j, kc=kc, bank=bank: e.transpose(
                                bank[:, j * 128:(j + 1) * 128], xs[s][:, j, kc * 128:(kc + 1) * 128], ident[:, :]),
                                reads=[xs[s], ident], writes=[bank])
                        if kc % 2 == 0:
                            tk.op("act", lambda e, kc=kc, bank=bank: e.activation(out=XT[s][:, kc, :], in_=bank[:, :], func=AF.Copy),
                                  reads=[bank], writes=[XT[s]])
                        else:
                            tk.op("dve", lambda e, kc=kc, bank=bank: e.tensor_copy(out=XT[s][:, kc, :], in_=bank[:, :]),
                                  reads=[bank], writes=[XT[s]])
                    tk.dma("pool", f"p1s{s}", XTd.rearrange("(kc p) t -> p kc t", p=128)[:, :, t0:t0 + BLK], XT[s][:, :, :],
                           reads=[XT[s]])
                    for h in range(4):
                        A = PS[2 + 2 * (h % 2)]
                        B = PS[3 + 2 * (h % 2)]
                        for kc in range(8):
                            tk.op("pe", lambda e, kc=kc, A=A: e.matmul(A[:, :], Wk[:, kc, h * 128:(h + 1) * 128], XT[s][:, kc, :],
                                                                    start=(kc == 0), stop=(kc == 7)),
                                  reads=[Wk, XT[s]], writes=[A])
                        for kc in range(8):
                            tk.op("pe", lambda e, kc=kc, B=B: e.matmul(B[:, :], Wk[:, kc, 512 + h * 128:512 + (h + 1) * 128], XT[s][:, kc, :],
                                                                    start=(kc == 0), stop=(kc == 7)),
                                  reads=[Wk, XT[s]], writes=[B])
                        ta, tb = tmp[2 * (h % 2)], tmp[2 * (h % 2) + 1]
                        tk.op("dve", lambda e, A=A, ta=ta: e.tensor_tensor(out=ta[:, :], in0=A[:, :], in1=cs[s][:, 0, :], op=ALU.mult),
                              reads=[A, cs[s]], writes=[ta])
                        tk.op("dve", lambda e, B=B, tb=tb: e.tensor_tensor(out=tb[:, :], in0=B[:, :], in1=cs[s][:, 1, :], op=ALU.mult),
                              reads=[B, cs[s]], writes=[tb])
                        tk.op("pool", lambda e, ta=ta, tb=tb: e.tensor_tensor(out=kst[s][:, h, :], in0=ta[:, :], in1=tb[:, :], op=ALU.add),
                              reads=[ta, tb], writes=[kst[s]])
                    tk.dma("pool", f"p1s{s}", KTd.rearrange("(h p) t -> p h t", p=128)[:, :, t0:t0 + BLK], kst[s][:, :, :],
                           reads=[kst[s]])
                    for st_ in range(8):
                        j = st_ % 4
                        bank = PS[6 + st_ % 2]
                        if st_ < 4:
                            for kc in range(8):
                                tk.op("pe", lambda e, kc=kc, bank=bank, j=j: e.matmul(bank[:, :], XT[s][:, kc, j * 128:(j + 1) * 128],
                                                                                  Wk[:, kc, 1024:1536], start=(kc == 0), stop=(kc == 7)),
                                      reads=[Wk, XT[s]], writes=[bank])
                            tk.op("act", lambda e, bank=bank, j=j: e.activation(
                                out=vst[s][:, :, j, :], in_=bank[:, :].rearrange("p (h e) -> p h e", h=4), func=AF.Copy),
                                reads=[bank], writes=[vst[s]])
                        else:
                            for kc in range(8):
                                tk.op("pe", lambda e, kc=kc, bank=bank, j=j: e.matmul(bank[:, 0:256], XT[s][:, kc, j * 128:(j + 1) * 128],
                                                                                  Wk[:, kc, 1536:1792], start=(kc == 0), stop=(kc == 7)),
                                      reads=[Wk, XT[s]], writes=[bank])
                            tk.op("dve", lambda e, bank=bank, j=j: e.tensor_copy(out=fst[s][:, j, :], in_=bank[:, 0:256]),
                                  reads=[bank], writes=[fst[s]])
                    c0 = t0 // 128
                    tk.dma("pool", f"p1s{s}", Vd[:, :, c0:c0 + 4, :].rearrange("h p c e -> p h c e"), vst[s][:, :, :, :],
                           reads=[vst[s]])
                    tk.dma("pool", f"p1s{s}", Fd[t0:t0 + BLK, :].rearrange("(j p) f -> p j f", p=128), fst[s][:, :, :],
                           reads=[fst[s]])
            tk.barrier()

            phase_end()
            with ExitStack() as st2:
                Wq = sb(st2, [128, 8, 1024], BF16, "Wq")
                load_w(Wq, 0, W2, D, C_Q, 512, "ldw")
                load_w(Wq, 512, W2, D, C_QS, 512, "ldw")
                lv = [sb(st2, [128, 64], F32, "lv") for _ in range(4)]
                bcast_load(lv[0], lam_q1[l, :])
                bcast_load(lv[1], lam_k1[l, :])
                bcast_load(lv[2], lam_q2[l, :])
                bcast_load(lv[3], lam_k2[l, :])
                sm = sb(st2, [128, 8], F32, "sm")
                lt = sb(st2, [128, 64], F32, "lt")
                tk.op("dve", lambda e: e.tensor_tensor(out=lt[:, :], in0=lv[0][:, :], in1=lv[1][:, :], op=ALU.mult),
                      reads=[lv[0], lv[1]], writes=[lt])
                tk.op("dve", lambda e: e.reduce_sum(out=sm[:, 0:1], in_=lt[:, :], axis=AX.X), reads=[lt], writes=[sm])
                tk.op("dve", lambda e: e.tensor_tensor(out=lt[:, :], in0=lv[2][:, :], in1=lv[3][:, :], op=ALU.mult),
                      reads=[lv[2], lv[3], sm], writes=[lt])
                tk.op("dve", lambda e: e.reduce_sum(out=sm[:, 1:2], in_=lt[:, :], axis=AX.X), reads=[lt], writes=[sm])
                tk.op("act", lambda e: e.activation(out=sm[:, 2:4], in_=sm[:, 0:2], func=AF.Exp), reads=[sm], writes=[sm])
                tk.op("dve", lambda e: e.tensor_tensor(out=sm[:, 4:5], in0=sm[:, 3:4], in1=sm[:, 2:3], op=ALU.subtract),
                      reads=[sm], writes=[sm])
                tk.op("dve", lambda e: e.tensor_scalar(out=sm[:, 5:6], in0=sm[:, 4:5], scalar1=-lambda_init, scalar2=None,
                                                       op0=ALU.add), reads=[sm], writes=[sm])
                negl = sm
                gcol = sb(st2, [128, 2], F32, "gcol")
                tk.dma("sp", "ld_const", gcol[:, 0:1], subln_g[l, :].rearrange("(p o) -> p o", o=1), writes=[gcol])
                tk.op("dve", lambda e: e.tensor_scalar(out=gcol[:, 1:2], in0=gcol[:, 0:1], scalar1=(1.0 - lambda_init),
                                                       scalar2=None, op0=ALU.mult), reads=[gcol], writes=[gcol])

                SKM = max(Sk for _, Sk in units)
                XTb = [sb(st2, [128, 8, BLK], BF16, "XTb") for _ in range(2)]
                csb = [sb(st2, [128, 2, BLK], F32, "csb") for _ in range(2)]
                QT = [sb(st2, [128, 4, 2, BLK], BF16, "QT") for _ in range(2)]
                for s_ in range(2):
                    tk.op("pool", lambda e, s_=s_: e.memset(QT[s_][:, :, :, :], 0.0), writes=[QT[s_]])
                Kh = [sb(st2, [128, SKM], BF16, "Kh") for _ in range(2)]
                Vh = [sb(st2, [128, SKM // 128, 128], BF16, "Vh") for _ in range(2)]
                NPT = 6
                pT = [sb(st2, [128, BLK], BF16, "pT") for _ in range(NPT)]
                tmpq = [sb(st2, [128, BLK], F32, "tmpq") for _ in range(2)]
                ep_r = [sb(st2, [128, BLK], F32, "ep_r") for _ in range(2)]
                ep_t = [sb(st2, [128, BLK], F32, "ep_t") for _ in range(2)]
                ep_a = [sb(st2, [128, BLK], F32, "ep_a") for _ in range(2)]
                ep_sq = [sb(st2, [128, BLK], BF16, "ep_sq") for _ in range(2)]
                ep_sd = [sb(st2, [128, BLK], F32, "ep_sd") for _ in range(2)]
                doT = [sb(st2, [128, 4, BLK], BF16, "doT") for _ in range(2)]

                jobs = []
                for (t0u, Sk, Sq) in qunits:
                    for qb in range(Sq // BLK):
                        jobs.append((t0u, Sk, t0u + qb * BLK))
                hjobs = [(ji, h) for ji in range(len(jobs)) for h in range(4)]

                def a_load_q(ji):
                    s = ji % 2
                    _, _, tq = jobs[ji]
                    tk.dma("sp", f"ax{s}", XTb[s][:, :, :], XTd.rearrange("(kc p) t -> p kc t", p=128)[:, :, tq:tq + BLK],
                           writes=[XTb[s]])
                    tk.dma("sp", f"ax{s}", csb[s][:, 0, :], ropec[:, tq:tq + BLK], writes=[csb[s]])
                    tk.dma("sp", f"ax{s}", csb[s][:, 1, :], ropes[:, tq:tq + BLK], writes=[csb[s]])

                def a_load_kv(hi):
                    ji, h = hjobs[hi]
                    t0u, Sk, _ = jobs[ji]
                    s = hi % 2
                    tk.dma("sp", f"akv{s}", Kh[s][:, 0:Sk], KTd[h * 128:(h + 1) * 128, t0u:t0u + Sk], writes=[Kh[s]])
                    c0 = t0u // 128
                    tk.dma("sp", f"akv{s}", Vh[s][:, 0:Sk // 128, :], Vd[h, :, c0:c0 + Sk // 128, :], writes=[Vh[s]])

                pending_b = []

                def epi_a(hi, O, Ssum):
                    p = hi % 2
                    for m in range(2):
                        tk.op("dve", lambda e, m=m: e.reciprocal(out=ep_r[m][:, :], in_=Ssum[m][:, :]),
                              reads=[Ssum[m]], writes=[ep_r[m]])
                        tk.op("dve", lambda e, m=m: e.tensor_tensor(out=ep_t[m][:, :], in0=O[m][:, :], in1=ep_r[m][:, :], op=ALU.mult),
                              reads=[O[m], ep_r[m]], writes=[ep_t[m]])
                    tk.op("dve", lambda e: e.scalar_tensor_tensor(out=ep_a[p][:, :], in0=ep_t[1][:, :], scalar=negl[:, 5:6],
                                                                  in1=ep_t[0][:, :], op0=ALU.mult, op1=ALU.add),
                          reads=[ep_t[0], ep_t[1], negl], writes=[ep_a[p]])
                    tk.op("act", lambda e: e.activation(out=ep_sq[p][:, :], in_=ep_a[p][:, :], func=AF.Square),
                          reads=[ep_a[p]], writes=[ep_sq[p]])

                def epi_b(hi, bank):
                    p = hi % 2
                    ji, h = hjobs[hi]
                    s = ji % 2
                    tk.op("pe", lambda e: e.matmul(bank[:, :], onesm_bf[:, :], ep_sq[p][:, :], start=True, stop=True),
                          reads=[onesm_bf, ep_sq[p]], writes=[bank])
                    tk.op("dve", lambda e: e.tensor_scalar(out=ep_sd[p][:, :], in0=bank[:, :], scalar1=RMS_EPS, scalar2=None,
                                                           op0=ALU.add), reads=[bank], writes=[ep_sd[p]])
                    tk.op("act", lambda e: e.activation(out=ep_sd[p][:, :], in_=ep_sd[p][:, :], func=AF.Sqrt),
                          reads=[ep_sd[p]], writes=[ep_sd[p]])
                    tk.op("dve", lambda e: e.reciprocal(out=ep_sd[p][:, :], in_=ep_sd[p][:, :]), reads=[ep_sd[p]], writes=[ep_sd[p]])
                    tk.op("dve", lambda e: e.tensor_tensor(out=ep_a[p][:, :], in0=ep_a[p][:, :], in1=ep_sd[p][:, :], op=ALU.mult),
                          reads=[ep_a[p], ep_sd[p]], writes=[ep_a[p]])
                    tk.op("dve", lambda e: e.tensor_scalar(out=doT[s][:, h, :], in0=ep_a[p][:, :], scalar1=gcol[:, 1:2], scalar2=None,
                                                           op0=ALU.mult), reads=[ep_a[p], gcol], writes=[doT[s]])
                    if h == 3:
                        _, _, tq = jobs[ji]
                        tk.dma("pool", f"ast{s}", DOd.rearrange("(h p) t -> p h t", p=128)[:, :, tq:tq + BLK], doT[s][:, :, :],
                               reads=[doT[s]])

                a_load_q(0)
                a_load_kv(0)
                for hi, (ji, h) in enumerate(hjobs):
                    t0u, Sk, tq = jobs[ji]
                    s = ji % 2
                    ks = hi % 2
                    if h == 0:
                        if ji + 1 < len(jobs):
                            a_load_q(ji + 1)
                        for hh in range(4):
                            A = PS[4 + 2 * (hh % 2)]
                            B = PS[5 + 2 * (hh % 2)]
                            for kc in range(8):
                                tk.op("pe", lambda e, kc=kc, A=A, hh=hh: e.matmul(A[:, :], Wq[:, kc, hh * 128:(hh + 1) * 128], XTb[s][:, kc, :],
                                                                                start=(kc == 0), stop=(kc == 7)),
                                      reads=[Wq, XTb[s]], writes=[A])
                            for kc in range(8):
                                tk.op("pe", lambda e, kc=kc, B=B, hh=hh: e.matmul(B[:, :], Wq[:, kc, 512 + hh * 128:512 + (hh + 1) * 128], XTb[s][:, kc, :],
                                                                                start=(kc == 0), stop=(kc == 7)),
                                      reads=[Wq, XTb[s]], writes=[B])
                            tk.op("dve", lambda e, A=A: e.tensor_tensor(out=tmpq[0][:, :], in0=A[:, :], in1=csb[s][:, 0, :], op=ALU.mult),
                                  reads=[A, csb[s]], writes=[tmpq[0]])
                            tk.op("dve", lambda e, B=B: e.tensor_tensor(out=tmpq[1][:, :], in0=B[:, :], in1=csb[s][:, 1, :], op=ALU.mult),
                                  reads=[B, csb[s]], writes=[tmpq[1]])
                            for m_ in range(2):
                                tk.op("pool", lambda e, hh=hh, m_=m_: e.tensor_tensor(
                                    out=QT[s][m_ * 64:(m_ + 1) * 64, hh, m_, :], in0=tmpq[0][m_ * 64:(m_ + 1) * 64, :],
                                    in1=tmpq[1][m_ * 64:(m_ + 1) * 64, :], op=ALU.add),
                                    reads=[tmpq[0], tmpq[1]], writes=[QT[s]])
                    if hi + 1 < len(hjobs):
                        a_load_kv(hi + 1)
                    O = [PS[0], PS[1]]
                    Ssum = [PS[2], PS[3]]
                    nkc = Sk // 128
                    steps = [(kc, m) for kc in range(nkc) for m in range(2)]
                    LA = 3

                    def qk(i):
                        kc, m = steps[i]
                        sc = PS[4 + i % 4]
                        tk.op("pe", lambda e: e.matmul(sc[:, :], Kh[ks][:, kc * 128:(kc + 1) * 128],
                                                       QT[s][:, h, m, :], start=True, stop=True),
                              reads=[Kh[ks], QT[s]], writes=[sc])
                        pt = pT[i % NPT]
                        tk.op("act", lambda e: e.activation(out=pt[:, :], in_=sc[:, :], func=AF.Exp, scale=0.125),
                              reads=[sc], writes=[pt])

                    for i in range(min(LA, len(steps))):
                        qk(i)
                    for i, (kc, m) in enumerate(steps):
                        if i + LA < len(steps):
                            qk(i + LA)
                        pt = pT[i % NPT]
                        tk.op("pe", lambda e: e.matmul(O[m][:, :], Vh[ks][:, kc, :], pt[:, :], start=(kc == 0), stop=(kc == nkc - 1)),
                              reads=[Vh[ks], pt], writes=[O[m]])
                        tk.op("pe", lambda e: e.matmul(Ssum[m][:, :], ones_bf[:, :], pt[:, :], start=(kc == 0), stop=(kc == nkc - 1)),
                              reads=[ones_bf, pt], writes=[Ssum[m]])
                        if i == 8 and pending_b:
                            pending_b.pop(0)()
                    while pending_b:
                        pending_b.pop(0)()
                    epi_a(hi, O, Ssum)
                    pending_b.append(lambda hi=hi: epi_b(hi, PS[4 + (hi % 2)]))
                while pending_b:
                    pending_b.pop(0)()
            tk.barrier()

            phase_end()
            with ExitStack() as st3:
                GRPM = 8
                Fg = [sb(st3, [128, GRPM, 256], BF16, "Fg") for _ in range(2)]
                Cg = [sb(st3, [128, GRPM, BLK], BF16, "Cg") for _ in range(2)]
                Sg = [sb(st3, [128, GRPM, BLK], BF16, "Sg") for _ in range(2)]
                cfs = [sb(st3, [128, 4, BLK], BF16, "cfs") for _ in range(2)]
                foT = [sb(st3, [128, 2, BLK], BF16, "foT") for _ in range(2)]
                gj = []
                bjobs = []
                for (t0u, Sk, Sq) in qunits:
                    for qb in range(Sq // BLK):
                        bjobs.append((t0u, Sk, qb))
                for bi_, (t0u, Sk, qb) in enumerate(bjobs):
                    for g in range(Sk // (128 * min(GRPM, Sk // 128))):
                        gj.append((bi_, t0u, Sk, qb, g))

                def f_load(gi):
                    bi_, t0u, Sk, qb, g = gj[gi]
                    GRP = min(GRPM, Sk // 128)
                    s = gi % 2
                    r0 = g * 128 * GRP
                    Cm, Sm = (dft_cs, dft_ss) if Sk == SS and t0u < 2 * SS else (dft_cp, dft_sp)
                    tk.dma("pool", f"flf{s}", Fg[s][:, 0:GRP, :], Fd[t0u + r0:t0u + r0 + 128 * GRP, :].rearrange("(c p) f -> p c f", p=128),
                           writes=[Fg[s]])
                    tk.dma("sp", f"flc{s}", Cg[s][:, 0:GRP, :], Cm[r0:r0 + 128 * GRP, qb * BLK:(qb + 1) * BLK].rearrange("(c p) n -> p c n", p=128),
                           writes=[Cg[s]])
                    tk.dma("act", f"fls{s}", Sg[s][:, 0:GRP, :], Sm[r0:r0 + 128 * GRP, qb * BLK:(qb + 1) * BLK].rearrange("(c p) n -> p c n", p=128),
                           writes=[Sg[s]])

                f_load(0)
                for gi, (bi_, t0u, Sk, qb, g) in enumerate(gj):
                    s = gi % 2
                    if gi + 1 < len(gj):
                        f_load(gi + 1)
                    GRP = min(GRPM, Sk // 128)
                    ng = Sk // (128 * GRP)
                    for c in range(GRP):
                        first = (g == 0 and c == 0)
                        last = (g == ng - 1 and c == GRP - 1)
                        for a in range(4):
                            mat = Cg[s] if a < 2 else Sg[s]
                            tk.op("pe", lambda e, a=a, c=c, mat=mat: e.matmul(PS[a][:, :], Fg[s][:, c, (a % 2) * 128:(a % 2 + 1) * 128], mat[:, c, :],
                                                                          start=first, stop=last),
                                  reads=[Fg[s], mat], writes=[PS[a]])
                    if g == ng - 1:
                        bs = bi_ % 2
                        tq = t0u + qb * BLK
                        for a in range(4):
                            if a % 2 == 0:
                                tk.op("act", lambda e, a=a: e.activation(out=cfs[bs][:, a, :], in_=PS[a][:, :], func=AF.Copy),
                                      reads=[PS[a]], writes=[cfs[bs]])
                            else:
                                tk.op("dve", lambda e, a=a: e.tensor_copy(out=cfs[bs][:, a, :], in_=PS[a][:, :]),
                                      reads=[PS[a]], writes=[cfs[bs]])
                        for cc in range(2):
                            bank = PS[4 + cc + 2 * (bi_ % 2)]
                            tk.op("pe", lambda e, cc=cc, bank=bank: e.matmul(bank[:, :], bdc[:, :], cfs[bs][:, cc, :], start=True, stop=False),
                                  reads=[bdc, cfs[bs]], writes=[bank])
                            tk.op("pe", lambda e, cc=cc, bank=bank: e.matmul(bank[:, :], bdsn[:, :], cfs[bs][:, 2 + cc, :], start=False, stop=True),
                                  reads=[bdsn, cfs[bs]], writes=[bank])
                            if cc == 0:
                                tk.op("act", lambda e, cc=cc, bank=bank: e.activation(out=foT[bs][:, cc, :], in_=bank[:, :], func=AF.Copy),
                                      reads=[bank], writes=[foT[bs]])
                            else:
                                tk.op("dve", lambda e, cc=cc, bank=bank: e.tensor_copy(out=foT[bs][:, cc, :], in_=bank[:, :]),
                                      reads=[bank], writes=[foT[bs]])
                        tk.dma("pool", f"fst{bs}", FOd.rearrange("(c p) t -> p c t", p=128)[:, :, tq:tq + BLK], foT[bs][:, :, :],
                               reads=[foT[bs]])
            tk.barrier()

            phase_end()
            with ExitStack() as st4:
                Wuv = sb(st4, [128, 8, 512], BF16, "Wuv")
                load_w(Wuv, 0, W2, D, C_U, 512, "ldw")
                swT = sb(st4, [128, 4, 128], BF16, "swT")
                for h in range(4):
                    tk.dma("pool", "ldw", swT[:, h, :], sgu_wT[l, h, :, :], writes=[swT])
                btab = sb(st4, [128, 256], F32, "btab")
                tk.dma("sp", "ld_const", btab[:, :], sgu_bt[l, :, :], writes=[btab])
                gvn = sb(st4, [128, 256], F32, "gvn")
                bvn = sb(st4, [128, 256], F32, "bvn")
                bcast_load(gvn, vn_g[l, :])
                bcast_load(bvn, vn_b[l, :])
                XTc = [sb(st4, [128, 8, BLK], BF16, "XTc") for _ in range(2)]
                ub = [sb(st4, [128, 256], F32, "ub") for _ in range(2)]
                vn0 = [sb(st4, [128, 256], F32, "vn0") for _ in range(2)]
                vnb = [sb(st4, [128, 256], BF16, "vnb") for _ in range(2)]
                junk = [sb(st4, [128, 256], F32, "junk") for _ in range(2)]
                so = [sb(st4, [128, 256], F32, "so") for _ in range(2)]
                sst = [sb(st4, [128, 8], F32, "sst") for _ in range(2)]
                soT = [sb(st4, [128, 2, BLK], BF16, "soT") for _ in range(2)]
                cjobs = []
                for (t0u, Sk, Sq) in qunits:
                    for qb in range(Sq // BLK):
                        cjobs.append(t0u + qb * BLK)

                def c_load(bi_):
                    s = bi_ % 2
                    tq = cjobs[bi_]
                    tk.dma("sp", f"cx{s}", XTc[s][:, :, :], XTd.rearrange("(kc p) t -> p kc t", p=128)[:, :, tq:tq + BLK],
                           writes=[XTc[s]])

                def sgu_a(bi_, j):
                    s = bi_ % 2
                    p = j % 2
                    if j == 0 and bi_ + 1 < len(cjobs):
                        c_load(bi_ + 1)
                    bank = PS[p]
                    for kc in range(8):
                        tk.op("pe", lambda e, kc=kc, bank=bank, j=j: e.matmul(bank[:, :], XTc[s][:, kc, j * 128:(j + 1) * 128], Wuv[:, kc, :],
                                                                          start=(kc == 0), stop=(kc == 7)),
                              reads=[XTc[s], Wuv], writes=[bank])
                    tk.op("act", lambda e, bank=bank, p=p: e.activation(out=junk[p][:, :], in_=bank[:, 256:512], func=AF.Copy,
                                                                        accum_out=sst[p][:, 0:1]),
                          reads=[bank], writes=[junk[p], sst[p]])
                    tk.op("act", lambda e, bank=bank, p=p: e.activation(out=junk[p][:, :], in_=bank[:, 256:512], func=AF.Square,
                                                                        accum_out=sst[p][:, 1:2]),
                          reads=[bank], writes=[junk[p], sst[p]])
                    tk.op("act", lambda e, bank=bank, p=p: e.activation(out=ub[p][:, :], in_=bank[:, 0:256], func=AF.Copy),
                          reads=[bank], writes=[ub[p]])
                    tk.op("dve", lambda e, p=p: e.tensor_scalar(out=sst[p][:, 2:4], in0=sst[p][:, 0:2], scalar1=1.0 / 256.0, scalar2=None,
                                                                op0=ALU.mult), reads=[sst[p]], writes=[sst[p]])
                    tk.op("dve", lambda e, p=p: e.tensor_tensor(out=sst[p][:, 4:5], in0=sst[p][:, 2:3], in1=sst[p][:, 2:3], op=ALU.mult),
                          reads=[sst[p]], writes=[sst[p]])
                    tk.op("dve", lambda e, p=p: e.tensor_tensor(out=sst[p][:, 5:6], in0=sst[p][:, 3:4], in1=sst[p][:, 4:5], op=ALU.subtract),
                          reads=[sst[p]], writes=[sst[p]])
                    tk.op("dve", lambda e, p=p: e.tensor_scalar(out=sst[p][:, 6:7], in0=sst[p][:, 5:6], scalar1=LN_EPS, scalar2=None,
                                                                op0=ALU.add), reads=[sst[p]], writes=[sst[p]])
                    tk.op("act", lambda e, p=p: e.activation(out=sst[p][:, 7:8], in_=sst[p][:, 6:7], func=AF.Sqrt),
                          reads=[sst[p]], writes=[sst[p]])
                    tk.op("dve", lambda e, p=p: e.reciprocal(out=sst[p][:, 6:7], in_=sst[p][:, 7:8]), reads=[sst[p]], writes=[sst[p]])
                    tk.op("dve", lambda e, bank=bank, p=p: e.tensor_scalar(out=vn0[p][:, :], in0=bank[:, 256:512], scalar1=sst[p][:, 2:3],
                                                                           scalar2=sst[p][:, 6:7], op0=ALU.subtract, op1=ALU.mult),
                          reads=[bank, sst[p]], writes=[vn0[p]])
                    tk.op("dve", lambda e, p=p: e.tensor_tensor(out=vn0[p][:, :], in0=vn0[p][:, :], in1=gvn[:, :], op=ALU.mult),
                          reads=[vn0[p], gvn], writes=[vn0[p]])
                    tk.op("pool", lambda e, p=p: e.tensor_tensor(out=vnb[p][:, :], in0=vn0[p][:, :], in1=bvn[:, :], op=ALU.add),
                          reads=[vn0[p], bvn], writes=[vnb[p]])

                def sgu_b(bi_, j):
                    s = bi_ % 2
                    p = j % 2
                    tq = cjobs[bi_]
                    bank2 = PS[2 + p]
                    for h in range(4):
                        tk.op("pe", lambda e, h=h, bank2=bank2, p=p: e.matmul(bank2[:, h * 64:(h + 1) * 64], swT[:, h, :], vnb[p][:, h * 64:(h + 1) * 64],
                                                                          start=True, stop=True),
                              reads=[swT, vnb[p]], writes=[bank2])
                    tk.op("dve", lambda e, bank2=bank2, p=p: e.tensor_tensor(out=so[p][:, :], in0=bank2[:, 0:256], in1=btab[:, :], op=ALU.add),
                          reads=[bank2, btab], writes=[so[p]])
                    tk.op("dve", lambda e, p=p: e.tensor_tensor(out=so[p][:, :], in0=so[p][:, :], in1=ub[p][:, :], op=ALU.mult),
                          reads=[so[p], ub[p]], writes=[so[p]])
                    for cc in range(2):
                        bank3 = PS[4 + cc + 2 * (bi_ % 2)]
                        tk.op("pe", lambda e, cc=cc, bank3=bank3, p=p, j=j: e.transpose(bank3[:, j * 128:(j + 1) * 128], so[p][:, cc * 128:(cc + 1) * 128],
                                                                                    ident[:, :]),
                              reads=[so[p], ident], writes=[bank3])
                    if j == 3:
                        for cc in range(2):
                            bank3 = PS[4 + cc + 2 * (bi_ % 2)]
                            if cc == 0:
                                tk.op("act", lambda e, cc=cc, bank3=bank3: e.activation(out=soT[s][:, cc, :], in_=bank3[:, :], func=AF.Copy),
                                      reads=[bank3], writes=[soT[s]])
                            else:
                                tk.op("dve", lambda e, cc=cc, bank3=bank3: e.tensor_copy(out=soT[s][:, cc, :], in_=bank3[:, :]),
                                      reads=[bank3], writes=[soT[s]])
                        tk.dma("pool", f"cst{s}", SOd.rearrange("(c p) t -> p c t", p=128)[:, :, tq:tq + BLK], soT[s][:, :, :],
                               reads=[soT[s]])

                c_load(0)
                ctiles = [(bi_, j) for bi_ in range(len(cjobs)) for j in range(4)]
                for ti, (bi_, j) in enumerate(ctiles):
                    sgu_a(bi_, j)
                    if ti >= 1:
                        sgu_b(*ctiles[ti - 1])
                sgu_b(*ctiles[-1])
            tk.barrier()

            phase_end()
            with ExitStack() as st5:
                Wg = sb(st5, [128, 8, 3072], BF16, "Wg")
                load_w(Wg, 0, W2, D, C_G, 3072, "ldw")
                Wfo = sb(st5, [128, 2, D], BF16, "Wfo")
                load_w(Wfo, 0, w_fourier[l], 256, 0, D, "ldw")
                Wsg = sb(st5, [128, 2, D], BF16, "Wsg")
                load_w(Wsg, 0, w_sgu[l], 256, 0, D, "ldw")
                Wdf = sb(st5, [128, 4, D], BF16, "Wdf")
                load_w(Wdf, 0, w_diff[l], 512, 0, D, "ldw")
                XTe = [sb(st5, [128, 8, BLK], BF16, "XTe") for _ in range(2)]
                dob = [sb(st5, [128, 4, BLK], BF16, "dob") for _ in range(2)]
                sob = [sb(st5, [128, 2, BLK], BF16, "sob") for _ in range(2)]
                fob = [sb(st5, [128, 2, BLK], BF16, "fob") for _ in range(2)]
                sgt = [[sb(st5, [128, BLK], F32, "sgt") for _ in range(3)] for _ in range(2)]
                mt = [sb(st5, [128, BLK], F32, "mt") for _ in range(2)]
                mT = [sb(st5, [128, 8, BLK], BF16, "mT") for _ in range(2)]
                djobs = list(cjobs)

                def d_load(bi_):
                    s = bi_ % 2
                    tq = djobs[bi_]
                    tk.dma("sp", f"dx{s}", XTe[s][:, :, :], XTd.rearrange("(kc p) t -> p kc t", p=128)[:, :, tq:tq + BLK], writes=[XTe[s]])
                    tk.dma("sp", f"dx{s}", dob[s][:, :, :], DOd.rearrange("(h p) t -> p h t", p=128)[:, :, tq:tq + BLK], writes=[dob[s]])
                    tk.dma("sp", f"dx{s}", sob[s][:, :, :], SOd.rearrange("(c p) t -> p c t", p=128)[:, :, tq:tq + BLK], writes=[sob[s]])
                    tk.dma("sp", f"dx{s}", fob[s][:, :, :], FOd.rearrange("(c p) t -> p c t", p=128)[:, :, tq:tq + BLK], writes=[fob[s]])

                d_load(0)
                for bi_, tq in enumerate(djobs):
                    s = bi_ % 2
                    if bi_ + 1 < len(djobs):
                        d_load(bi_ + 1)
                    for n in range(8):
                        p = n % 2
                        for b in range(3):
                            bank = PS[b]
                            for kc in range(8):
                                tk.op("pe", lambda e, kc=kc, b=b, bank=bank, n=n: e.matmul(
                                    bank[:, :], Wg[:, kc, b * D + n * 128:b * D + (n + 1) * 128], XTe[s][:, kc, :],
                                    start=(kc == 0), stop=(kc == 7)), reads=[Wg, XTe[s]], writes=[bank])
                            tk.op("act", lambda e, b=b, bank=bank, p=p: e.activation(out=sgt[p][b][:, :], in_=bank[:, :], func=AF.Sigmoid),
                                  reads=[bank], writes=[sgt[p][b]])
                        brs = [(Wfo, fob[s], 2), (Wsg, sob[s], 2), (Wdf, dob[s], 4)]
                        for b, (Wb, xb_, nkc) in enumerate(brs):
                            bank = PS[3 + b]
                            for kc in range(nkc):
                                tk.op("pe", lambda e, kc=kc, bank=bank, Wb=Wb, xb_=xb_, nkc=nkc, n=n: e.matmul(
                                    bank[:, :], Wb[:, kc, n * 128:(n + 1) * 128], xb_[:, kc, :], start=(kc == 0), stop=(kc == nkc - 1)),
                                    reads=[Wb, xb_], writes=[bank])
                        tk.op("dve", lambda e, p=p: e.tensor_tensor(out=mt[0][:, :], in0=PS[3][:, :], in1=sgt[p][0][:, :], op=ALU.mult),
                              reads=[PS[3], sgt[p][0]], writes=[mt[0]])
                        tk.op("dve", lambda e, p=p: e.tensor_tensor(out=mt[1][:, :], in0=PS[4][:, :], in1=sgt[p][1][:, :], op=ALU.mult),
                              reads=[PS[4], sgt[p][1]], writes=[mt[1]])
                        tk.op("pool", lambda e: e.tensor_tensor(out=mt[0][:, :], in0=mt[0][:, :], in1=mt[1][:, :], op=ALU.add),
                              reads=[mt[0], mt[1]], writes=[mt[0]])
                        tk.op("dve", lambda e, p=p: e.tensor_tensor(out=mt[1][:, :], in0=PS[5][:, :], in1=sgt[p][2][:, :], op=ALU.mult),
                              reads=[PS[5], sgt[p][2]], writes=[mt[1]])
                        tk.op("pool", lambda e, n=n: e.tensor_tensor(out=mT[s][:, n, :], in0=mt[0][:, :], in1=mt[1][:, :], op=ALU.add),
                              reads=[mt[0], mt[1]], writes=[mT[s]])
                    tk.dma("pool", f"dst{s}", MTd.rearrange("(kc p) t -> p kc t", p=128)[:, :, tq:tq + BLK], mT[s][:, :, :], reads=[mT[s]])
            tk.barrier()

            phase_end()
            with ExitStack() as st6:
                Wo = sb(st6, [128, 8, D], BF16, "Wo")
                load_w(Wo, 0, w_out[l], D, 0, D, "ldw")
                g1 = sb(st6, [128, D], F32, "g1")
                b1 = sb(st6, [128, D], F32, "b1")
                bcast_load(g1, ln1_g[l, :])
                bcast_load(b1, ln1_b[l, :])
                moe = (l % 2 == 1)
                if moe:
                    Wr = sb(st6, [128, 8, NE], F32, "Wr")
                    tk.dma("sp", "ld_const", Wr[:, :, :], w_router[l // 2].rearrange("(kc p) e -> p kc e", p=128), writes=[Wr])
                MTb = [sb(st6, [128, 8, BLK], BF16, "MTb") for _ in range(2)]
                xsb = [sb(st6, [128, 4, D], F32, "xsb") for _ in range(2)]
                yb = [sb(st6, [128, D], F32, "yb") for _ in range(2)]
                x1s = [sb(st6, [128, D], F32, "x1s") for _ in range(2)]
                stt = [sb(st6, [128, 12], F32, "stt") for _ in range(2)]
                mvt = [sb(st6, [128, 4], F32, "mvt") for _ in range(2)]
                x1T32 = [sb(st6, [128, 8, 128], F32, "x1T32") for _ in range(2)]
                x1Tb = [sb(st6, [128, 8, BLK], BF16, "x1Tb") for _ in range(2)]
                rt = [sb(st6, [128, 64], F32, "rt") for _ in range(2)]
                ejobs = list(cjobs)

                def e_load(bi_):
                    s = bi_ % 2
                    tq = ejobs[bi_]
                    tk.dma("sp", f"ex{s}", MTb[s][:, :, :], MTd.rearrange("(kc p) t -> p kc t", p=128)[:, :, tq:tq + BLK], writes=[MTb[s]])
                    tk.dma("sp", f"ex{s}", xsb[s][:, :, :], Xsrc[tq:tq + BLK, :].rearrange("(j p) d -> p j d", p=128), writes=[xsb[s]])

                def stage_a(bi_, j):
                    s = bi_ % 2
                    tq = ejobs[bi_]
                    p = j % 2
                    if j == 0 and bi_ + 1 < len(ejobs):
                        e_load(bi_ + 1)
                    for nh in range(2):
                        bank = PS[(2 * j + nh) % 4]
                        for kc in range(8):
                            tk.op("pe", lambda e, kc=kc, bank=bank, nh=nh, j=j: e.matmul(
                                bank[:, :], MTb[s][:, kc, j * 128:(j + 1) * 128], Wo[:, kc, nh * 512:(nh + 1) * 512],
                                start=(kc == 0), stop=(kc == 7)), reads=[MTb[s], Wo], writes=[bank])
                        tk.op("dve", lambda e, bank=bank, nh=nh, j=j, p=p: e.scalar_tensor_tensor(
                            out=yb[p][:, nh * 512:(nh + 1) * 512], in0=xsb[s][:, j, nh * 512:(nh + 1) * 512], scalar=ALPHA,
                            in1=bank[:, :], op0=ALU.mult, op1=ALU.add), reads=[xsb[s], bank], writes=[yb[p]])
                    layer_norm_tile((stt[p], mvt[p]), yb[p], g1, b1, x1s[p])
                    tk.dma("pool", f"est{p}", X1d[tq + j * 128:tq + (j + 1) * 128, :], x1s[p][:, :], reads=[x1s[p]])

                def stage_b(bi_, j):
                    s = bi_ % 2
                    tq = ejobs[bi_]
                    p = j % 2
                    for g in range(2):
                        bank = PS[4 + g]
                        for k4 in range(4):
                            kc = g * 4 + k4
                            tk.op("pe", lambda e, bank=bank, k4=k4, kc=kc, p=p: e.transpose(
                                bank[:, k4 * 128:(k4 + 1) * 128], x1s[p][:, kc * 128:(kc + 1) * 128], ident[:, :]),
                                reads=[x1s[p], ident], writes=[bank])
                        if not moe:
                            tk.op("act", lambda e, bank=bank, g=g, j=j: e.activation(
                                out=x1Tb[s][:, g * 4:(g + 1) * 4, j * 128:(j + 1) * 128],
                                in_=bank[:, :].rearrange("p (k t) -> p k t", k=4), func=AF.Copy),
                                reads=[bank], writes=[x1Tb[s]])
                        else:
                            tk.op("act", lambda e, bank=bank, g=g, p=p: e.activation(
                                out=x1T32[p][:, g * 4:(g + 1) * 4, :],
                                in_=bank[:, :].rearrange("p (k t) -> p k t", k=4), func=AF.Copy),
                                reads=[bank], writes=[x1T32[p]])
                            tk.op("pool", lambda e, g=g, j=j, p=p: e.tensor_copy(
                                out=x1Tb[s][:, g * 4:(g + 1) * 4, j * 128:(j + 1) * 128],
                                in_=x1T32[p][:, g * 4:(g + 1) * 4, :]),
                                reads=[x1T32[p]], writes=[x1Tb[s]])
                    if j == 3:
                        tk.dma("pool", f"est2{s}", X1Td.rearrange("(kc p) t -> p kc t", p=128)[:, :, tq:tq + BLK], x1Tb[s][:, :, :], reads=[x1Tb[s]])

                def stage_c(bi_, j):
                    s = bi_ % 2
                    tq = ejobs[bi_]
                    p = j % 2
                    if moe:
                        bank = PS[6 + p]
                        for kc in range(8):
                            tk.op("pe", lambda e, kc=kc, bank=bank, p=p: e.matmul(bank[:, 0:NE], x1T32[p][:, kc, :], Wr[:, kc, :],
                                                                              start=(kc == 0), stop=(kc == 7)),
                                  reads=[x1T32[p], Wr], writes=[bank])
                        r = rt[p]
                        tk.op("dve", lambda e, bank=bank, r=r: e.tensor_copy(out=r[:, 0:8], in_=bank[:, 0:NE]), reads=[bank], writes=[r])
                        tk.op("dve", lambda e, r=r: e.reduce_max(out=r[:, 32:33], in_=r[:, 0:8], axis=AX.X), reads=[r], writes=[r])
                        tk.op("dve", lambda e, r=r: e.tensor_scalar(out=r[:, 8:16], in0=r[:, 0:8], scalar1=r[:, 32:33], scalar2=None,
                                                                    op0=ALU.is_equal), reads=[r], writes=[r])
                        tk.op("dve", lambda e, r=r: e.scalar_tensor_tensor(out=r[:, 16:24], in0=r[:, 8:16], scalar=-1e30, in1=r[:, 0:8],
                                                                           op0=ALU.mult, op1=ALU.add), reads=[r], writes=[r])
                        tk.op("dve", lambda e, r=r: e.reduce_max(out=r[:, 33:34], in_=r[:, 16:24], axis=AX.X), reads=[r], writes=[r])
                        tk.op("dve", lambda e, r=r: e.tensor_scalar(out=r[:, 24:32], in0=r[:, 16:24], scalar1=r[:, 33:34], scalar2=None,
                                                                    op0=ALU.is_equal), reads=[r], writes=[r])
                        tk.op("dve", lambda e, r=r: e.tensor_tensor(out=r[:, 34:35], in0=r[:, 33:34], in1=r[:, 32:33], op=ALU.subtract),
                              reads=[r], writes=[r])
                        tk.op("act", lambda e, r=r: e.activation(out=r[:, 35:36], in_=r[:, 34:35], func=AF.Sigmoid), reads=[r], writes=[r])
                        tk.op("dve", lambda e, r=r: e.tensor_scalar(out=r[:, 36:37], in0=r[:, 35:36], scalar1=-1.0, scalar2=1.0,
                                                                    op0=ALU.mult, op1=ALU.add), reads=[r], writes=[r])
                        tk.op("dve", lambda e, r=r: e.tensor_scalar(out=r[:, 48:56], in0=r[:, 8:16], scalar1=r[:, 36:37], scalar2=None,
                                                                    op0=ALU.mult), reads=[r], writes=[r])
                        tk.op("dve", lambda e, r=r: e.scalar_tensor_tensor(out=r[:, 40:48], in0=r[:, 24:32], scalar=r[:, 35:36], in1=r[:, 48:56],
                                                                           op0=ALU.mult, op1=ALU.add), reads=[r], writes=[r])
                        tk.dma("pool", f"est{p}", CMBd[tq + j * 128:tq + (j + 1) * 128, :], r[:, 40:48], reads=[r])

                e_load(0)
                tiles = [(bi_, j) for bi_ in range(len(ejobs)) for j in range(4)]
                for ti, (bi_, j) in enumerate(tiles):
                    stage_a(bi_, j)
                    if ti >= 1:
                        stage_b(*tiles[ti - 1])
                    if ti >= 2:
                        stage_c(*tiles[ti - 2])
                stage_b(*tiles[-1])
                if len(tiles) >= 2:
                    stage_c(*tiles[-2])
                stage_c(*tiles[-1])
            tk.barrier()

            phase_end()
            moe = (l % 2 == 1)
            if moe:
                passes = [(moe_w_gate[l // 2, e_], moe_w_up[l // 2, e_], moe_w_down[l // 2, e_], e_) for e_ in range(NE)]
            else:
                passes = [(ffn_w_gate[l // 2], ffn_w_up[l // 2], ffn_w_down[l // 2], None)]
            HC = NFC // 2
            HCOLS = HC * 128
            hpasses = []
            for (wg_ap, wu_ap, wd_ap, ex) in passes:
                for half in range(2):
                    c0col = half * HCOLS
                    ncols = min(HCOLS, DFF - c0col)
                    hpasses.append((wg_ap, wu_ap, wd_ap, ex, c0col, ncols, (ex is None or ex == 0) and half == 0))
            with ExitStack() as st7:
                Wg2 = [sb(st7, [128, 8, HCOLS], BF16, "Wg2") for _ in range(2)]
                Wu2 = [sb(st7, [128, 8, HCOLS], BF16, "Wu2") for _ in range(2)]
                Wd2 = [sb(st7, [128, HC, D], BF16, "Wd2") for _ in range(2)]
                XTf = [sb(st7, [128, 8, BLK], BF16, "XTf") for _ in range(2)]
                cmbb = [sb(st7, [128, 4, NE], F32, "cmbb") for _ in range(2)]
                hT = sb(st7, [128, HC, BLK], BF16, "hT")
                sgf = [sb(st7, [128, BLK], F32, "sgf") for _ in range(2)]
                ysb = [sb(st7, [128, D], F32, "ysb") for _ in range(2)]
                fjobs = list(cjobs)

                def w_load_ops(pi):
                    wg_ap, wu_ap, wd_ap, ex, c0col, ncols, first = hpasses[pi]
                    ws = pi % 2
                    return (load_w_ops(Wg2[ws], 0, wg_ap, D, c0col, ncols, f"ldw{ws}")
                            + load_w_ops(Wu2[ws], 0, wu_ap, D, c0col, ncols, f"ldw{ws}")
                            + load_w_ops(Wd2[ws], 0, wd_ap[c0col:c0col + ncols, :], ncols, 0, D, f"ldw{ws}"))

                pend_w = []
                nblk_f = len(fjobs)

                gjobs = [(pi, bi_) for pi in range(len(hpasses)) for bi_ in range(len(fjobs))]

                def g_load(gi):
                    pi, bi_ = gjobs[gi]
                    ex = hpasses[pi][3]
                    s = gi % 2
                    tq = fjobs[bi_]
                    tk.dma("sp", f"gx{s}", XTf[s][:, :, :], X1Td.rearrange("(kc p) t -> p kc t", p=128)[:, :, tq:tq + BLK], writes=[XTf[s]])
                    if ex is not None:
                        tk.dma("sp", f"gx{s}", cmbb[s][:, :, :], CMBd[tq:tq + BLK, :].rearrange("(j p) e -> p j e", p=128), writes=[cmbb[s]])

                for op_ in w_load_ops(0):
                    op_()
                g_load(0)
                for gi, (pi, bi_) in enumerate(gjobs):
                    wg_ap, wu_ap, wd_ap, ex, c0col, ncols, first = hpasses[pi]
                    ws = pi % 2
                    Wgs, Wus, Wds = Wg2[ws], Wu2[ws], Wd2[ws]
                    nfc = (ncols + 127) // 128
                    s = gi % 2
                    tq = fjobs[bi_]
                    if gi + 1 < len(gjobs):
                        g_load(gi + 1)
                    for c in range(nfc):
                        rows = min(128, ncols - c * 128)
                        bg = PS[(2 * c) % 4]
                        bu = PS[(2 * c + 1) % 4]
                        for kc in range(8):
                            tk.op("pe", lambda e, kc=kc, bg=bg, c=c, rows=rows: e.matmul(bg[0:rows, :], Wgs[:, kc, c * 128:c * 128 + rows], XTf[s][:, kc, :],
                                                                                     start=(kc == 0), stop=(kc == 7)),
                                  reads=[Wgs, XTf[s]], writes=[bg])
                        for kc in range(8):
                            tk.op("pe", lambda e, kc=kc, bu=bu, c=c, rows=rows: e.matmul(bu[0:rows, :], Wus[:, kc, c * 128:c * 128 + rows], XTf[s][:, kc, :],
                                                                                     start=(kc == 0), stop=(kc == 7)),
                                  reads=[Wus, XTf[s]], writes=[bu])
                        sg_ = sgf[c % 2]
                        tk.op("act", lambda e, bg=bg, sg_=sg_, rows=rows: e.activation(out=sg_[0:rows, :], in_=bg[0:rows, :], func=AF.Silu),
                              reads=[bg], writes=[sg_])
                        tk.op("dve", lambda e, bu=bu, sg_=sg_, rows=rows, c=c: e.tensor_tensor(out=hT[0:rows, c, :], in0=bu[0:rows, :], in1=sg_[0:rows, :],
                                                                                           op=ALU.mult),
                              reads=[bu, sg_], writes=[hT])
                    if bi_ == 0 and pi + 1 < len(hpasses):
                        pend_w = w_load_ops(pi + 1)
                    if pend_w:
                        left = max(1, nblk_f - 2 - bi_)
                        for _ in range((len(pend_w) + left - 1) // left):
                            pend_w.pop(0)()
                    for j in range(4):
                        p = j % 2
                        for nh in range(2):
                            bank = PS[4 + (2 * j + nh) % 4]
                            for c in range(nfc):
                                rows = min(128, ncols - c * 128)
                                tk.op("pe", lambda e, c=c, bank=bank, rows=rows, nh=nh, j=j: e.matmul(
                                    bank[:, :], hT[0:rows, c, j * 128:(j + 1) * 128], Wds[0:rows, c, nh * 512:(nh + 1) * 512],
                                    start=(c == 0), stop=(c == nfc - 1)), reads=[hT, Wds], writes=[bank])
                            if ex is None:
                                tk.op("act", lambda e, bank=bank, nh=nh, p=p: e.activation(out=ysb[p][:, nh * 512:(nh + 1) * 512], in_=bank[:, :],
                                                                                       func=AF.Copy), reads=[bank], writes=[ysb[p]])
                            else:
                                tk.op("act", lambda e, bank=bank, nh=nh, p=p, j=j: e.activation(
                                    out=ysb[p][:, nh * 512:(nh + 1) * 512], in_=bank[:, :], func=AF.Copy, scale=cmbb[s][:, j, ex:ex + 1]),
                                    reads=[bank, cmbb[s]], writes=[ysb[p]])
                        ydst = Yd[tq + j * 128:tq + (j + 1) * 128, :]
                        if first:
                            tk.dma("pool", f"gst{p}", ydst, ysb[p][:, :], reads=[ysb[p]])
                        else:
                            tk.dma("pool", f"gst{p}", ydst, ysb[p][:, :], reads=[ysb[p]], accum_op=ALU.add)
            tk.barrier()

            phase_end()
            with ExitStack() as st8:
                g2 = sb(st8, [128, D], F32, "g2")
                b2 = sb(st8, [128, D], F32, "b2")
                bcast_load(g2, ln2_g[l, :])
                bcast_load(b2, ln2_b[l, :])
                xa = [sb(st8, [128, 4, D], F32, "xa") for _ in range(2)]
                ya = [sb(st8, [128, 4, D], F32, "ya") for _ in range(2)]
                tb_ = [sb(st8, [128, D], F32, "tb") for _ in range(2)]
                ob = [sb(st8, [128, D], F32, "ob") for _ in range(2)]
                stt2 = [sb(st8, [128, 12], F32, "stt2") for _ in range(2)]
                mvt2 = [sb(st8, [128, 4], F32, "mvt2") for _ in range(2)]
                hjobs4 = list(cjobs)
                dest = Xmid if l + 1 < NLAYER else out

                def h_load(bi_):
                    s = bi_ % 2
                    tq = hjobs4[bi_]
                    tk.dma("sp", f"hx{s}", xa[s][:, :, :], X1d[tq:tq + BLK, :].rearrange("(j p) d -> p j d", p=128), writes=[xa[s]])
                    tk.dma("sp", f"hx{s}", ya[s][:, :, :], Yd[tq:tq + BLK, :].rearrange("(j p) d -> p j d", p=128), writes=[ya[s]])

                h_load(0)
                for bi_, tq in enumerate(hjobs4):
                    s = bi_ % 2
                    if bi_ + 1 < len(hjobs4):
                        h_load(bi_ + 1)
                    for j in range(4):
                        p = j % 2
                        tk.op("dve", lambda e, j=j, p=p: e.scalar_tensor_tensor(out=tb_[p][:, :], in0=xa[s][:, j, :], scalar=ALPHA, in1=ya[s][:, j, :],
                                                                            op0=ALU.mult, op1=ALU.add), reads=[xa[s], ya[s]], writes=[tb_[p]])
                        layer_norm_tile((stt2[p], mvt2[p]), tb_[p], g2, b2, ob[p])
                        tk.dma("pool", f"hst{p}", dest[tq + j * 128:tq + (j + 1) * 128, :], ob[p][:, :], reads=[ob[p]])
            tk.barrier()
            phase_end()
      except _StopBuild:
        pass
    return nc


def _rope_tables(pos):
    inv = ROPE_THETA ** (-np.arange(0, 64, 2, dtype=np.float64) / 64.0)
    ang = pos.astype(np.float64)[None, :] * inv[:, None]
    c = np.cos(ang)
    s = np.sin(ang)
    d = np.arange(128) % 64
    j = d % 32
    sign = np.where(d < 32, -1.0, 1.0)
    return c[j].astype(np.float32), (s[j] * sign[:, None]).astype(np.float32)


def _dft_mats(pos_rows, pos_cols, S):
    k = (pos_rows.astype(np.int64)[:, None] * pos_cols.astype(np.int64)[None, :]) % S
    ang = (2.0 * np.pi / S) * k.astype(np.float64)
    sc = 1.0 / math.sqrt(S * 64.0)
    return (np.cos(ang) * sc).astype(ml_dtypes.bfloat16), (np.sin(ang) * sc).astype(ml_dtypes.bfloat16)


_PROG_CACHE = {}


def run_module(inputs, SS, SP, NLAYER=2, debug=None, stop_phase=None, run_cores=8):
    f32 = np.float32
    xp = np.asarray(inputs["x_prompt"], f32)
    xs = np.asarray(inputs["x_sample"], f32)
    ncores = 8
    HP = SP // 2
    assert xp.shape[0] * 2 == ncores and xs.shape[0] == 2 * ncores
    w_in = np.asarray(inputs["w_in"], f32)
    perm = np.arange(512).reshape(8, 2, 32)[:, ::-1, :].reshape(512)
    q = w_in[:, :, 768:1280]
    k = w_in[:, :, 1280:1792]
    w_in_e = np.ascontiguousarray(np.concatenate([w_in, q[:, :, perm], k[:, :, perm]], axis=2))
    sgu_w = np.asarray(inputs["sgu_w"], f32)
    sgu_b = np.asarray(inputs["sgu_b"], f32)
    sgu_wT = np.ascontiguousarray(np.transpose(sgu_w, (0, 1, 3, 2)))
    sgu_bt = np.ascontiguousarray(np.repeat(np.transpose(sgu_b, (0, 2, 1))[:, :, :, None], 64, axis=3).reshape(sgu_b.shape[0], 128, 256))
    ident = np.eye(128, dtype=f32)
    kk = np.arange(64)
    ang = 2.0 * np.pi * ((kk[:, None] * kk[None, :]) % 64) / 64.0
    bdc = np.zeros((128, 128), f32)
    bdsn = np.zeros((128, 128), f32)
    for g in range(2):
        bdc[g * 64:(g + 1) * 64, g * 64:(g + 1) * 64] = np.cos(ang)
        bdsn[g * 64:(g + 1) * 64, g * 64:(g + 1) * 64] = -np.sin(ang)
    pos_s = np.arange(SS)
    dcs, dss = _dft_mats(pos_s, pos_s, SS)
    shared = {
        "w_in_e": w_in_e, "sgu_wT": sgu_wT, "sgu_bt": sgu_bt, "ident": ident, "bdc": bdc, "bdsn": bdsn,
        "dft_cs": dcs, "dft_ss": dss,
    }
    for nm in ["w_fourier", "w_sgu", "w_diff", "w_out", "vn_g", "vn_b", "lam_q1", "lam_k1", "lam_q2", "lam_k2", "subln_g",
               "ln1_g", "ln1_b", "ln2_g", "ln2_b", "ffn_w_gate", "ffn_w_up", "ffn_w_down", "w_router",
               "moe_w_gate", "moe_w_up", "moe_w_down"]:
        shared[nm] = np.ascontiguousarray(np.asarray(inputs[nm], f32))
    par = {}
    for parity in range(2):
        pos_p = np.concatenate([np.arange(parity * HP, (parity + 1) * HP), np.arange((1 - parity) * HP, (2 - parity) * HP)])
        dcp, dsp = _dft_mats(pos_p, pos_p, SP)
        pos_all = np.concatenate([pos_s, pos_s, pos_p])
        rc, rs = _rope_tables(pos_all)
        par[parity] = dict(pos_p=pos_p, dft_cp=dcp, dft_sp=dsp, ropec=rc, ropes=rs)
    in_maps = []
    for c in range(ncores):
        parity = c % 2
        P = par[parity]
        xin = np.concatenate([xs[2 * c], xs[2 * c + 1], xp[c // 2][P["pos_p"]]], axis=0)
        m = dict(shared)
        m.update(xin=np.ascontiguousarray(xin), dft_cp=P["dft_cp"], dft_sp=P["dft_sp"], ropec=P["ropec"], ropes=P["ropes"])
        in_maps.append(m)
    key = (SS, SP, NLAYER, tuple(sorted(debug)) if debug else None, stop_phase)
    if key not in _PROG_CACHE:
        _PROG_CACHE[key] = build_program(SS, SP, NLAYER, debug, stop_phase)
    nc = _PROG_CACHE[key]
    res = run_bass_kernel_spmd(nc, in_maps[:run_cores], core_ids=list(range(run_cores)))
    outs = [r["out"] for r in res.results]
    outs = outs + [outs[0]] * (ncores - run_cores)
    y_sample = np.stack([outs[c][i * SS:(i + 1) * SS] for c in range(ncores) for i in range(2)], axis=0)
    y_prompt = np.stack([np.concatenate([outs[2 * b][2 * SS:2 * SS + HP], outs[2 * b + 1][2 * SS:2 * SS + HP]], axis=0)
                         for b in range(ncores // 2)], axis=0)
    if debug:
        return (y_prompt.astype(f32), y_sample.astype(f32)), res.results
    return (y_prompt.astype(f32), y_sample.astype(f32))


def kernel(**inputs):
    return run_module(inputs, SS=4096, SP=8192)
```
